# Optimizing a Trainium2 kernel written in Bass

```python
import math
import jax, jax.numpy as jnp
from jax import lax
import numpy as np

D_MODEL = 1024
BATCH = 8
SEQ = 2048
DEPTH = 1

MLA_HEADS = 8
MLA_NOPE_DIM = 64
MLA_ROPE_DIM = 32
MLA_QK_DIM = MLA_NOPE_DIM + MLA_ROPE_DIM
MLA_V_DIM = 64
MLA_WIDTH = MLA_HEADS * MLA_V_DIM
MLA_Q_RANK = 384
MLA_KV_RANK = 256
ROPE_THETA = 10000.0
Q_BLOCK = 128
GLA_HEADS = 4
GLA_DV = D_MODEL // 2
GLA_DK = GLA_DV // 2
GLA_HEAD_K = GLA_DK // GLA_HEADS
GLA_HEAD_V = GLA_DV // GLA_HEADS
GLA_GATE_RANK = 16
GLA_GATE_TAU = 16.0
GLA_CHUNK = 64
D_FF = -(-8 * D_MODEL // (3 * 256)) * 256
NORM_EPS = 1e-6
IN_SIZES = (MLA_Q_RANK, MLA_KV_RANK, MLA_ROPE_DIM,
            GLA_DK, GLA_DK, GLA_DV, GLA_GATE_RANK, GLA_DV,
            D_MODEL, D_MODEL)
D_IN = sum(IN_SIZES)

kernel_name = "hybrid_mla_gla_gated_block"


def rmsnorm(x, w):
    xf = x.astype(jnp.float32)
    y = xf * lax.rsqrt(jnp.mean(xf * xf, axis=-1, keepdims=True) + NORM_EPS)
    return (y * w.astype(jnp.float32)).astype(x.dtype)


def rope_tables(positions):
    half = MLA_ROPE_DIM // 2
    inv = 1.0 / (ROPE_THETA ** (jnp.arange(half, dtype=jnp.float32) / half))
    ang = positions.astype(jnp.float32)[..., None] * inv
    return jnp.cos(ang), jnp.sin(ang)


def apply_rope(x, cos, sin):
    x1, x2 = jnp.split(x.astype(jnp.float32), 2, axis=-1)
    return jnp.concatenate([x1 * cos - x2 * sin, x2 * cos + x1 * sin], axis=-1).astype(x.dtype)


def mla(c_q, c_kv, k_rope, cos, sin, norm_cq, w_uq, norm_ckv, w_ukv):
    B, S, _ = c_q.shape
    H = MLA_HEADS
    q = (rmsnorm(c_q, norm_cq) @ w_uq).reshape(B, S, H, MLA_QK_DIM)
    q_nope, q_rope = q[..., :MLA_NOPE_DIM], q[..., MLA_NOPE_DIM:]
    q_rope = apply_rope(q_rope, cos[:, :, None], sin[:, :, None])
    kv = (rmsnorm(c_kv, norm_ckv) @ w_ukv).reshape(B, S, H, MLA_NOPE_DIM + MLA_V_DIM)
    k_nope, v = kv[..., :MLA_NOPE_DIM], kv[..., MLA_NOPE_DIM:]
    k_rope = apply_rope(k_rope, cos, sin)
    k = jnp.concatenate([k_nope, jnp.broadcast_to(k_rope[:, :, None], (B, S, H, MLA_ROPE_DIM)).astype(k_nope.dtype)], axis=-1)
    q = jnp.concatenate([q_nope, q_rope], axis=-1) * (MLA_QK_DIM ** -0.5)
    nb = S // Q_BLOCK
    q_blocks = q.reshape(B, nb, Q_BLOCK, H, MLA_QK_DIM).transpose(1, 0, 2, 3, 4)
    key_idx = jnp.arange(S)

    def attend(args):
        q_blk, start = args
        s = jnp.einsum('bqhd,bkhd->bhqk', q_blk, k).astype(jnp.float32)
        q_idx = start + jnp.arange(Q_BLOCK)
        mask = key_idx[None, :] <= q_idx[:, None]
        s = jnp.where(mask, s, -jnp.inf)
        p = jax.nn.softmax(s, axis=-1).astype(v.dtype)
        return jnp.einsum('bhqk,bkhd->bqhd', p, v)

    o = lax.map(attend, (q_blocks, jnp.arange(nb) * Q_BLOCK))
    return o.transpose(1, 0, 2, 3, 4).reshape(B, S, MLA_WIDTH)


def gla(q, k, v, g_lr, og, w_gate2, b_gate, norm_o):
    B, S, _ = q.shape
    H, C = GLA_HEADS, GLA_CHUNK
    n = S // C
    log_a = jax.nn.log_sigmoid((g_lr @ w_gate2 + b_gate).astype(jnp.float32)) / GLA_GATE_TAU

    def heads(t, d):
        return t.reshape(B, S, H, d).transpose(0, 2, 1, 3).astype(jnp.float32)

    def chunks(t):
        return t.reshape(B, H, n, C, t.shape[-1]).transpose(2, 0, 1, 3, 4)

    qh = heads(q, GLA_HEAD_K) * (GLA_HEAD_K ** -0.5)
    kh = heads(k, GLA_HEAD_K)
    vh = heads(v, GLA_HEAD_V)
    gh = heads(log_a, GLA_HEAD_K)
    bcum = jnp.cumsum(chunks(gh), axis=3)
    causal = jnp.tril(jnp.ones((C, C), dtype=bool))

    def step(state, xs):
        qc, kc, vc, bc = xs
        o_inter = jnp.einsum('bhcd,bhde->bhce', qc * jnp.exp(bc), state)
        diff = bc[:, :, :, None, :] - bc[:, :, None, :, :]
        decay = jnp.exp(jnp.where(causal[:, :, None], diff, -jnp.inf))
        attn = jnp.einsum('bhid,bhjd,bhijd->bhij', qc, kc, decay)
        o_intra = jnp.einsum('bhij,bhje->bhie', attn, vc)
        b_last = bc[:, :, -1:, :]
        k_dec = kc * jnp.exp(b_last - bc)
        state = jnp.exp(b_last[:, :, 0, :])[..., None] * state + jnp.einsum('bhcd,bhce->bhde', k_dec, vc)
        return state, o_inter + o_intra

    state0 = jnp.zeros((B, H, GLA_HEAD_K, GLA_HEAD_V), jnp.float32)
    _, o = lax.scan(step, state0, (chunks(qh), chunks(kh), chunks(vh), bcum))
    o = o.transpose(1, 0, 3, 2, 4).reshape(B, S, H, GLA_HEAD_V)
    o = rmsnorm(o, norm_o).reshape(B, S, GLA_DV)
    o = o * jax.nn.silu(og.astype(jnp.float32))
    return o.astype(q.dtype)


def setup_inputs(seed: int = 0) -> dict:
    key = jax.random.key(seed)
    ks = jax.random.split(key, 24)
    L, D = DEPTH, D_MODEL

    def w(k, shape, fan_in):
        return jax.random.normal(k, shape, jnp.float32) * (fan_in ** -0.5)

    def gain(k, shape):
        return 1.0 + 0.02 * jax.random.normal(k, shape, jnp.float32)

    return {
        "x": jax.random.normal(ks[0], (BATCH, SEQ, D), jnp.float32),
        "positions": jnp.broadcast_to(jnp.arange(SEQ, dtype=jnp.int32), (BATCH, SEQ)),
        "ln_mix": gain(ks[1], (L, D)),
        "w_in": w(ks[2], (L, D, D_IN), D),
        "mla_norm_cq": gain(ks[3], (L, MLA_Q_RANK)),
        "mla_w_uq": w(ks[4], (L, MLA_Q_RANK, MLA_HEADS * MLA_QK_DIM), MLA_Q_RANK),
        "mla_norm_ckv": gain(ks[5], (L, MLA_KV_RANK)),
        "mla_w_ukv": w(ks[6], (L, MLA_KV_RANK, MLA_HEADS * (MLA_NOPE_DIM + MLA_V_DIM)), MLA_KV_RANK),
        "mla_w_o": w(ks[7], (L, MLA_WIDTH, D), MLA_WIDTH),
        "gla_w_gate2": w(ks[8], (L, GLA_GATE_RANK, GLA_DK), GLA_GATE_RANK),
        "gla_b_gate": 0.1 * jax.random.normal(ks[9], (L, GLA_DK), jnp.float32),
        "gla_norm": gain(ks[10], (L, GLA_HEAD_V)),
        "gla_w_o": w(ks[11], (L, GLA_DV, D), GLA_DV),
        "w_out": w(ks[12], (L, D, D), D),
        "ln_ffn": gain(ks[13], (L, D)),
        "ffn_w_gate": w(ks[14], (L, D, D_FF), D),
        "ffn_w_up": w(ks[15], (L, D, D_FF), D),
        "ffn_w_down": w(ks[16], (L, D_FF, D), D_FF),
        "final_norm": gain(ks[17], (D,)),
    }


def reference(x, positions, ln_mix, w_in, mla_norm_cq, mla_w_uq, mla_norm_ckv, mla_w_ukv, mla_w_o,
              gla_w_gate2, gla_b_gate, gla_norm, gla_w_o, w_out, ln_ffn, ffn_w_gate, ffn_w_up,
              ffn_w_down, final_norm):
    offsets = np.cumsum(IN_SIZES)[:-1].tolist()
    cos, sin = rope_tables(positions)
    h = x
    for l in range(DEPTH):
        u = rmsnorm(h, ln_mix[l])
        z = u @ w_in[l]
        c_q, c_kv, k_rope, g_q, g_k, g_v, g_lr, g_og, gate_a, gate_b = jnp.split(z, offsets, axis=-1)
        y_a = mla(c_q, c_kv, k_rope, cos, sin, mla_norm_cq[l], mla_w_uq[l], mla_norm_ckv[l], mla_w_ukv[l]) @ mla_w_o[l]
        y_b = gla(g_q, g_k, g_v, g_lr, g_og, gla_w_gate2[l], gla_b_gate[l], gla_norm[l]) @ gla_w_o[l]
        mix = jax.nn.sigmoid(gate_a) * y_a + jax.nn.sigmoid(gate_b) * y_b
        h = h + mix @ w_out[l]
        u = rmsnorm(h, ln_ffn[l])
        h = h + (jax.nn.silu(u @ ffn_w_gate[l]) * (u @ ffn_w_up[l])) @ ffn_w_down[l]
    return rmsnorm(h, final_norm)
```

```python
import contextlib
import numpy as np
import ml_dtypes
import concourse.bass as bass
import concourse.mybir as mybir
from concourse.bass_utils import run_bass_kernel_spmd

F32 = mybir.dt.float32
BF16 = mybir.dt.bfloat16
I32 = mybir.dt.int32
U8 = mybir.dt.uint8
AF = mybir.ActivationFunctionType
ALU = mybir.AluOpType

T = 2048
D = 1024
NT = 16
NSB = 4
DFF = 2816
NF = 22
DIN = 4272
O_CQ, O_CKV, O_KR, O_GQ, O_GK, O_GV, O_GLR, O_GOG, O_GA, O_GB = 0, 384, 640, 672, 928, 1184, 1696, 1712, 2224, 3248
EPS = 1e-6
PI = float(np.pi)
NSLOT = 8
B1_PER_STAGE = 2
ENGS = ("pe", "act", "dve", "pool", "sp")


_DISJOINT = ("mixT", "qz", "ktT", "vg", "sog", "ktok", "Vt", "QT", "KT", "OT", "OGT", "cqT", "ckvT", "krtok", "glrT", "Sall",
             "ebl", "uT", "vT", "ssg", "ssk", "aT")


def _disjoint(k):
    return k.rstrip("0123456789_") in _DISJOINT


class Prog:
    def __init__(self):
        self.ops = []
        self.lw = {}
        self.rd = {}
        self.last = {}
        self.dmas_since_barrier = []

    def add(self, eng, fn, r=(), w=(), dma=False):
        idx = len(self.ops)
        deps = set()
        for k in r:
            p = self.lw.get(k)
            if p is not None:
                deps.add(p)
        for k in w:
            p = self.lw.get(k)
            if p is not None and not (_disjoint(k) and not dma and not self.ops[p]["dma"] and self.ops[p]["eng"] == eng):
                deps.add(p)
            for q in self.rd.get(k, {}).values():
                if isinstance(q, list):
                    deps.update(q)
                else:
                    deps.add(q)
        for k in r:
            d = self.rd.setdefault(k, {})
            if dma:
                d.setdefault("dma", []).append(idx)
            else:
                d[eng] = idx
        for k in w:
            self.lw[k] = idx
            self.rd[k] = {}
        deps.discard(idx)
        self.ops.append(dict(eng=eng, fn=fn, deps=deps, dma=dma, sig=False))
        if dma:
            self.dmas_since_barrier.append(idx)
        else:
            self.last[eng] = idx
        return idx

    def pe(self, fn, r=(), w=()):
        return self.add("pe", fn, r, w)

    def act(self, fn, r=(), w=()):
        return self.add("act", fn, r, w)

    def dve(self, fn, r=(), w=()):
        return self.add("dve", fn, r, w)

    def pool(self, fn, r=(), w=()):
        return self.add("pool", fn, r, w)

    def dma(self, q, out, in_, r=(), w=()):
        return self.add(q, lambda e: e.dma_start(out=out, in_=in_), r, w, dma=True)

    def barrier(self, keep=()):
        kept = {self.lw[k] for k in keep if k in self.lw and self.ops[self.lw[k]]["dma"]}
        deps = set(self.last.values()) | (set(self.dmas_since_barrier) - kept)
        for eng in ENGS:
            self.ops.append(dict(eng=eng, fn=None, deps=set(deps), dma=False, sig=False))
        self.last = {}
        self.lw = {k: v for k, v in self.lw.items() if v in kept}
        self.rd = {}
        self.dmas_since_barrier = sorted(kept)

    def emit(self, nc, st):
        ops = self.ops
        for op in ops:
            for d in op["deps"]:
                p = ops[d]
                if p["dma"]:
                    continue
                if p["eng"] == "pe" and op["eng"] == "pe" and not op["dma"]:
                    continue
                p["sig"] = True
        cnt = {e: 0 for e in ENGS}
        dcnt = {"pool": 0, "sp": 0}
        for op in ops:
            if op["dma"]:
                k = dcnt[op["eng"]]
                dcnt[op["eng"]] += 1
                op["slot"] = k % NSLOT
                op["val"] = 16 * (k // NSLOT + 1)
            elif op["sig"]:
                cnt[op["eng"]] += 1
                op["cnt"] = cnt[op["eng"]]
        sem = {e: st.enter_context(nc.semaphore("s_" + e)) for e in ENGS}
        dsem = {q: [st.enter_context(nc.semaphore(f"d_{q}{i}")) for i in range(NSLOT)] for q in ("pool", "sp")}
        per = {e: [] for e in ENGS}
        for i, op in enumerate(ops):
            per[op["eng"]].append(i)

        def run(engname, e):
            seen = {}
            for i in per[engname]:
                op = ops[i]
                waits = {}
                for d in op["deps"]:
                    p = ops[d]
                    if p["dma"]:
                        key = ("d", p["eng"], p["slot"])
                        v = p["val"]
                    else:
                        if p["eng"] == "pe" and engname == "pe" and not op["dma"]:
                            continue
                        key = ("c", p["eng"])
                        v = p["cnt"]
                    if v > waits.get(key, 0):
                        waits[key] = v
                if op["dma"] and op["val"] > 16:
                    key = ("d", engname, op["slot"])
                    waits[key] = max(waits.get(key, 0), op["val"] - 16)
                for key, v in waits.items():
                    if seen.get(key, 0) >= v:
                        continue
                    seen[key] = v
                    s = sem[key[1]] if key[0] == "c" else dsem[key[1]][key[2]]
                    e.wait_ge(s, v)
                if op["fn"] is None:
                    continue
                ins = op["fn"](e)
                if op["dma"]:
                    ins.then_inc(dsem[engname][op["slot"]], 16)
                elif op["sig"]:
                    ins.then_inc(sem[engname], 1)

        block = st.enter_context(nc.Block())

        @block.tensor
        def _(e):
            run("pe", e)

        @block.scalar
        def _(e):
            run("act", e)

        @block.vector
        def _(e):
            run("dve", e)

        @block.gpsimd
        def _(e):
            run("pool", e)

        @block.sync
        def _(e):
            run("sp", e)


class Arena:
    def __init__(self, nc, nbytes):
        self.t = nc.alloc_sbuf_tensor("arena", [128, nbytes], U8)
        self.n = nbytes
        self.top = 0
        self.peak = 0

    def alloc(self, shape, dt):
        esz = 4 if dt in (F32, I32) else 2
        nb = int(np.prod(shape[1:])) * esz
        off = (self.top + 63) // 64 * 64
        assert off + nb <= self.n, f"SBUF arena overflow {off + nb} > {self.n}"
        self.top = off + nb
        self.peak = max(self.peak, self.top)
        a = self.t[0:shape[0], off:off + nb].bitcast(dt)
        if len(shape) == 3:
            a = a.rearrange("p (a b) -> p a b", a=shape[1])
        elif len(shape) == 4:
            a = a.rearrange("p (a b c) -> p a b c", a=shape[1], b=shape[2])
        return a


def build(debug=(), stop_after=None):
    nc = bass.Bass("TRN2", target_bir_lowering=False)

    def din(name, shape, dt=F32):
        return nc.dram_tensor(name, list(shape), dt, kind="ExternalInput").ap()

    x = din("x", [T, D])
    pos = din("pos", [128, NT], I32)
    w_in = din("w_in", [D, DIN])
    w_uq = din("w_uq", [384, 768])
    w_ukv = din("w_ukv", [256, 1024])
    w_oa = din("w_oa", [512, 1024])
    w_g2 = din("w_g2", [16, 256])
    w_ob = din("w_ob", [512, 1024])
    w_out = din("w_out", [D, D])
    w_fg = din("w_fg", [D, DFF])
    w_fu = din("w_fu", [D, DFF])
    w_fd = din("w_fd", [DFF, D])
    v_lnmix = din("v_lnmix", [128, 8])
    v_lnffn = din("v_lnffn", [128, 8])
    v_ncq = din("v_ncq", [128, 3])
    v_nckv = din("v_nckv", [128, 2])
    v_bg = din("v_bg", [128, 2])
    v_gn = din("v_gn", [128, 128])
    v_gnc = din("v_gnc", [128, 1])
    v_fn = din("v_fn", [128, D])
    c_inv = din("c_inv", [128, 32])
    c_ident = din("c_ident", [128, 128], BF16)
    c_tri = din("c_tri", [128, 128], BF16)
    out = nc.dram_tensor("out", [T, D], F32, kind="ExternalOutput").ap()
    dbg_out = {}

    st = contextlib.ExitStack()
    A = Arena(nc, 212800)
    P = Prog()
    ps = [nc.alloc_psum_tensor(f"ps{i}", [128, 512], F32)[:] for i in range(8)]
    psb = [p.bitcast(BF16) for p in ps]

    def sb_(s):
        return slice(s * 512, (s + 1) * 512)

    def tl(i):
        return slice(i * 128, (i + 1) * 128)

    def mmg(out_ap, pairs, r, w):
        def fn(e):
            n = len(pairs)
            ins = None
            for k, (l, rh) in enumerate(pairs):
                ins = e.matmul(out_ap, l, rh, start=(k == 0), stop=(k == n - 1))
            return ins
        P.pe(fn, r, w)

    def dump(name, ap, shape, dt=F32):
        if name not in debug:
            return
        d = nc.dram_tensor("dbg_" + name, list(shape), dt, kind="ExternalOutput").ap()
        dbg_out[name] = d
        P.barrier()
        P.dma("sp", d, ap, r=[], w=["dbg_" + name])
        P.barrier()

    lnmix = A.alloc([128, 8], F32)
    lnffn = A.alloc([128, 8], F32)
    ncq = A.alloc([128, 3], F32)
    nckv = A.alloc([128, 2], F32)
    bg = A.alloc([128, 2], F32)
    nbg = A.alloc([128, 2], F32)
    gnb = A.alloc([128, 128], F32)
    gnc = A.alloc([128, 1], F32)
    invb = A.alloc([128, 32], F32)
    ident = A.alloc([128, 128], BF16)
    tri = A.alloc([128, 128], BF16)
    ones_bf = A.alloc([128, 64], BF16)
    posi = A.alloc([128, NT], I32)
    ki = A.alloc([128, NT, 16], I32)
    for dst, src, k in ((lnmix, v_lnmix, "lnmix"), (lnffn, v_lnffn, "lnffn"), (ncq, v_ncq, "ncq"),
                        (nckv, v_nckv, "nckv"), (bg, v_bg, "bg"), (gnb, v_gn, "gnb"), (gnc, v_gnc, "gnc"), (invb, c_inv, "invb"),
                        (ident, c_ident, "ident"), (tri, c_tri, "tri")):
        P.dma("pool", dst, src, r=[], w=[k])
    P.dve(lambda e: e.tensor_scalar(out=nbg, in0=bg, scalar1=-1.0, scalar2=None, op0=ALU.mult), r=["bg"], w=["nbg"])
    P.dve(lambda e: e.memset(ones_bf, 1.0), r=[], w=["ones"])

    uT = A.alloc([128, 8, T], BF16)
    off_OGT = (A.top + 63) // 64 * 64
    OGT = A.alloc([128, 4, T], BF16)
    base_mark = A.top

    def make_nt(dstT, gain, tagp, ntiles, banks=(0, 1)):
        xn = [A.alloc([128, D], BF16) for _ in range(2)]
        sqj = A.alloc([128, D], BF16)
        ssq = A.alloc([128, ntiles], F32)
        lnv = A.alloc([128, ntiles], F32)
        rs = A.alloc([128, ntiles], F32)

        def part1(i, src, skeys):
            b = i % 2
            P.act(lambda e, src=src, i=i: e.activation(out=sqj, in_=src, func=AF.Square, accum_out=ssq[:, i:i + 1]),
                  r=skeys, w=[tagp + "sqj", f"{tagp}ss{i}"])
            P.act(lambda e, i=i: e.activation(out=lnv[:, i:i + 1], in_=ssq[:, i:i + 1], func=AF.Ln, scale=1.0 / D, bias=epsb[:, 0:1]),
                  r=[f"{tagp}ss{i}", "epsb"], w=[f"{tagp}ln{i}"])
            P.act(lambda e, i=i: e.activation(out=rs[:, i:i + 1], in_=lnv[:, i:i + 1], func=AF.Exp, scale=-0.5),
                  r=[f"{tagp}ln{i}"], w=[f"{tagp}rs{i}"])
            P.dve(lambda e, src=src, b=b, i=i: e.tensor_scalar(out=xn[b], in0=src, scalar1=rs[:, i:i + 1], scalar2=None, op0=ALU.mult),
                  r=skeys + [f"{tagp}rs{i}"], w=[f"{tagp}xn{b}"])

        def part2(i):
            b = i % 2
            bk = banks[b]
            pb = psb[bk]

            def tr(e, b=b, pb=pb):
                ins = None
                for c in range(8):
                    ins = e.transpose(pb[:, c * 128:(c + 1) * 128], xn[b][:, c * 128:(c + 1) * 128], ident)
                return ins
            P.pe(tr, r=[f"{tagp}xn{b}", "ident"], w=[f"ps{bk}"])
            P.dve(lambda e, pb=pb, i=i: e.tensor_tensor(out=dstT[:, :, tl(i)], in0=pb.rearrange("p (c t) -> p c t", c=8),
                                                        in1=gain[:, 0:8].unsqueeze(2).to_broadcast([128, 8, 128]), op=ALU.mult),
                  r=[f"ps{bk}", "lnmix", "lnffn"], w=[f"{tagp}T{i // 4}"])

        def step(i, src, skeys):
            part1(i, src, skeys)
            part2(i)
        step.part1 = part1
        step.part2 = part2
        return step

    epsb = A.alloc([128, 1], F32)
    P.dve(lambda e: e.memset(epsb, EPS), r=[], w=["epsb"])
    one_b = A.alloc([128, 1], F32)
    P.dve(lambda e: e.memset(one_b, 1.0), r=[], w=["one_b"])
    base_mark = A.top

    A1_BASE = 193000
    A.top = A1_BASE
    xt = [A.alloc([128, D], F32) for _ in range(3)]

    nt_u = make_nt(uT, lnmix, "u", NT)
    for i in range(NT):
        b = i % 3
        P.dma("sp", xt[b], x[tl(i), :], r=[], w=[f"xt{b}"])
        nt_u(i, xt[b], [f"xt{b}"])
    dump("uT", uT, [128, 8, T], BF16)
    A.top = base_mark
    if stop_after == "a1":
        return finish(nc, st, P, out, dbg_out, A)

    m0 = A.top
    qz = A.alloc([128, 4, T], BF16)
    ktT = A.alloc([128, 2, T], BF16)
    ktok = A.alloc([128, NT, 256], BF16)
    vg = A.alloc([128, NT, 512], BF16)
    sog = A.alloc([128, NT, 512], BF16)
    ebl = A.alloc([128, 2, NT], F32)
    Sall = A.alloc([128, NT - 1, 2, 256], BF16)
    Z = A.alloc([128, 2, 256], F32)
    m1 = A.top
    glrT = A.alloc([16, T], BF16)
    wlr = A.alloc([128, 8, 16], BF16)
    wg2 = A.alloc([16, 256], BF16)
    rmask = A.alloc([128, T], BF16)
    bufA = A.alloc([128, T], F32)
    bufB = A.alloc([128, T], F32)
    bufC = A.alloc([128, T], F32)
    tmpE = [A.alloc([128, 512], F32) for _ in range(2)]
    slab = [A.alloc([128, 8, 512], BF16) for _ in range(2)]

    P.dma("pool", wlr, w_in[:, O_GLR:O_GLR + 16].rearrange("(c p) n -> p c n", p=128), r=[], w=["wlr"])
    P.dma("pool", wg2, w_g2, r=[], w=["wg2"])
    P.dma("pool", slab[0], w_in[:, O_GQ:O_GQ + 512].rearrange("(c p) n -> p c n", p=128), r=[], w=["slab0"])
    P.dma("pool", slab[1], w_in[:, O_GV:O_GV + 512].rearrange("(c p) n -> p c n", p=128), r=[], w=["slab1"])
    sv_top = A.top
    A.top = A1_BASE
    slabM = A.alloc([128, 8, 672], BF16)
    wuq = A.alloc([128, 3, 768], BF16)
    wukv = A.alloc([128, 2, 1024], BF16)
    A.top = sv_top
    a1_keys = ["xt0", "xt1", "xt2", "uxn0", "uxn1", "usqj"] + [f"u{t}{i}" for t in ("ss", "ln", "rs") for i in range(NT)]
    P.dma("pool", slabM, w_in[:, 0:672].rearrange("(c p) n -> p c n", p=128), r=[], w=["slabM"] + a1_keys)
    P.dma("pool", wuq, w_uq.rearrange("(c p) n -> p c n", p=128), r=[], w=["wuq"] + a1_keys)
    P.dma("pool", wukv, w_ukv.rearrange("(c p) n -> p c n", p=128), r=[], w=["wukv"] + a1_keys)
    P.dve(lambda e: e.memset(qz, 0.0), r=[], w=["qz"])
    P.dve(lambda e: e.memset(rmask, 1.0), r=[], w=["rmask"])
    P.dve(lambda e: e.memset(rmask.rearrange("p (c t) -> p c t", t=128)[:, :, 0:1], 0.0), r=[], w=["rmask"])
    for s in range(NSB):
        b = s % 2
        mmg(ps[b][0:16, :], [(wlr[:, c, :], uT[:, c, sb_(s)]) for c in range(8)], r=["wlr", f"uT{s}"], w=[f"ps{b}"])
        P.act(lambda e, b=b, s=s: e.activation(out=glrT[:, sb_(s)], in_=ps[b][0:16, :], func=AF.Copy), r=[f"ps{b}"], w=["glrT"])
    def v_tile(i):
        b = 4 + i % 2
        mmg(ps[b], [(uT[:, c, tl(i)], slab[1][:, c, :]) for c in range(8)], r=["slab1", f"uT{i // 4}"], w=[f"ps{b}"])
        P.act(lambda e, b=b, i=i: e.activation(out=vg[:, i, :], in_=ps[b], func=AF.Copy), r=[f"ps{b}"], w=["vg"])

    for th in range(2):
        for s in range(NSB):
            b = s % 2
            mmg(ps[b], [(wg2[:, th * 128:(th + 1) * 128], glrT[:, sb_(s)])], r=["wg2", "glrT"], w=[f"ps{b}"])
            P.act(lambda e, b=b, th=th: e.activation(out=tmpE[b], in_=ps[b], func=AF.Exp, scale=-1.0, bias=nbg[:, th:th + 1]),
                  r=[f"ps{b}", "nbg"], w=[f"tmpE{b}"])
            P.act(lambda e, b=b, s=s: e.activation(out=bufA[:, sb_(s)], in_=tmpE[b], func=AF.Ln, bias=one_b[:, 0:1]),
                  r=[f"tmpE{b}", "one_b"], w=["bufA"])
        for i in range(th * 8, th * 8 + 8):
            v_tile(i)
        P.dve(lambda e: e.tensor_tensor_scan(out=bufB, data0=rmask, data1=bufA, initial=0.0, op0=ALU.mult, op1=ALU.add),
              r=["bufA", "rmask"], w=["bufB"])
        P.act(lambda e: e.activation(out=bufA, in_=bufB, func=AF.Exp, scale=-1.0 / 16.0), r=["bufB"], w=["bufA"])
        P.act(lambda e: e.activation(out=bufC, in_=bufB, func=AF.Exp, scale=1.0 / 16.0), r=["bufB"], w=["bufC"])
        P.dve(lambda e, th=th: e.tensor_copy(out=ebl[:, th, :], in_=bufA.rearrange("p (c t) -> p c t", t=128)[:, :, 127]),
              r=["bufA"], w=["ebl"])
        for s in range(NSB):
            b = s % 2
            mmg(ps[b], [(slab[0][:, c, th * 128:(th + 1) * 128], uT[:, c, sb_(s)]) for c in range(8)], r=["slab0", f"uT{s}"], w=[f"ps{b}"])
            for hh in range(2):
                pr = slice(hh * 64, hh * 64 + 64)
                P.dve(lambda e, b=b, s=s, th=th, hh=hh, pr=pr: e.scalar_tensor_tensor(out=qz[pr, 2 * th + hh, sb_(s)], in0=ps[b][pr, :], scalar=0.125,
                                                                                     in1=bufA[pr, sb_(s)], op0=ALU.mult, op1=ALU.mult),
                      r=[f"ps{b}", "bufA"], w=["qz"])
            b2 = 2 + s % 2
            mmg(ps[b2], [(slab[0][:, c, 256 + th * 128:256 + (th + 1) * 128], uT[:, c, sb_(s)]) for c in range(8)], r=["slab0", f"uT{s}"], w=[f"ps{b2}"])
            P.dve(lambda e, b2=b2, s=s, th=th: e.tensor_tensor(out=ktT[:, th, sb_(s)], in0=ps[b2], in1=bufC[:, sb_(s)], op=ALU.mult),
                  r=[f"ps{b2}", "bufC"], w=["ktT"])
    for g in range(4):
        b = 4 + g % 2
        pb = psb[b]

        def trk(e, g=g, pb=pb):
            ins = None
            for ii in range(4):
                for th in range(2):
                    k = ii * 2 + th
                    ins = e.transpose(pb[:, k * 128:(k + 1) * 128], ktT[:, th, tl(g * 4 + ii)], ident)
            return ins
        P.pe(trk, r=["ktT", "ident"], w=[f"ps{b}"])
        P.act(lambda e, g=g, pb=pb: e.activation(out=ktok[:, g * 4:(g + 1) * 4, :], in_=pb.rearrange("p (a b) -> p a b", a=4), func=AF.Copy),
              r=[f"ps{b}"], w=["ktok"])
    def d0_step(c):
        pk = 6 + c % 2

        def kv_mm(e, c=c, pk=pk):
            ins = None
            for th in range(2):
                ins = e.matmul(ps[pk][:, th * 256:(th + 1) * 256], ktok[:, c, th * 128:(th + 1) * 128],
                               vg[:, c, th * 256:(th + 1) * 256], start=True, stop=True)
            return ins
        P.pe(kv_mm, r=["ktok", "vg"], w=[f"ps{pk}"])
        for th in range(2):
            if c == 0:
                P.dve(lambda e, th=th, pk=pk: e.tensor_copy(out=Z[:, th, :], in_=ps[pk][:, th * 256:(th + 1) * 256]), r=[f"ps{pk}"], w=[f"Z{th}"])
            else:
                P.dve(lambda e, th=th, c=c, pk=pk: e.scalar_tensor_tensor(out=Z[:, th, :], in0=Z[:, th, :], scalar=ebl[:, th, c - 1:c],
                                                                          in1=ps[pk][:, th * 256:(th + 1) * 256], op0=ALU.mult, op1=ALU.add),
                      r=[f"ps{pk}", "ebl", f"Z{th}"], w=[f"Z{th}"])
            P.dve(lambda e, th=th, c=c: e.tensor_scalar(out=Sall[:, c, th, :], in0=Z[:, th, :], scalar1=ebl[:, th, c:c + 1], scalar2=None, op0=ALU.mult),
                  r=[f"Z{th}", "ebl"], w=["Sall"])

    P.dma("pool", slab[0], w_in[:, O_GOG:O_GOG + 512].rearrange("(c p) n -> p c n", p=128), r=[], w=["slab0"])
    assert A.top <= A1_BASE, (A.top, A1_BASE)
    for i in range(NT):
        b = 2 + i % 2
        mmg(ps[b], [(uT[:, c, tl(i)], slab[0][:, c, :]) for c in range(8)], r=["slab0", f"uT{i // 4}"], w=[f"ps{b}"])
        P.act(lambda e, b=b: e.activation(out=tmpE[b - 2], in_=ps[b], func=AF.Exp, scale=-1.0), r=[f"ps{b}"], w=[f"tmpE{b - 2}"])
        P.act(lambda e, b=b: e.activation(out=tmpE[b - 2], in_=tmpE[b - 2], func=AF.Ln, bias=one_b[:, 0:1]), r=[f"tmpE{b - 2}", "one_b"], w=[f"tmpE{b - 2}"])
        P.act(lambda e, b=b: e.activation(out=tmpE[b - 2], in_=tmpE[b - 2], func=AF.Exp, scale=-1.0), r=[f"tmpE{b - 2}"], w=[f"tmpE{b - 2}"])
        P.dve(lambda e, b=b, i=i: e.tensor_tensor(out=sog[:, i, :], in0=ps[b], in1=tmpE[b - 2], op=ALU.mult),
              r=[f"ps{b}", f"tmpE{b - 2}"], w=["sog"])
        if i < NT - 1:
            d0_step(i)
    dump("qz", qz, [128, 4, T], BF16)
    dump("ktT", ktT, [128, 2, T], BF16)
    dump("ktok", ktok, [128, NT, 256], BF16)
    dump("vg", vg, [128, NT, 512], BF16)
    dump("sog", sog, [128, NT, 512], BF16)
    dump("ebl", ebl, [128, 2, NT], F32)
    P.barrier()
    A.top = m1
    if stop_after == "a2":
        return finish(nc, st, P, out, dbg_out, A)

    attn_sb = [A.alloc([128, 4, 128], BF16) for _ in range(2)]
    ssg = A.alloc([128, 4 * NT], F32)
    lng = A.alloc([128, 4 * NT], F32)
    rsg = A.alloc([128, 4 * NT], F32)
    sqj2 = A.alloc([128, 4, 128], BF16)
    ogt = [A.alloc([128, 512], BF16) for _ in range(2)]
    d_end = A.top
    A.top = m0
    QT = A.alloc([96, 8, T], BF16)
    KT = A.alloc([96, 8, T], BF16)
    Vt4 = A.alloc([128, NT, 8, 128], BF16)
    mB = A.top
    cqT = A.alloc([128, 3, T], BF16)
    ckvT = A.alloc([128, 2, T], BF16)
    rq = A.alloc([128, NT], F32)
    rkv = A.alloc([128, NT], F32)
    krtok = A.alloc([128, NT, 32], F32)
    cs2 = A.alloc([128, NT, 32], F32)
    sn2 = A.alloc([128, NT, 32], F32)
    mB2 = A.top
    ssq = A.alloc([128, NT], F32)
    ssk = A.alloc([128, 2 * NT], F32)
    sskk = A.alloc([128, NT], F32)
    lq = A.alloc([128, NT], F32)
    lk = A.alloc([128, NT], F32)
    sqj3 = A.alloc([128, 3, 384], BF16)
    posf = A.alloc([128, NT], F32)
    ang = A.alloc([128, NT, 16], F32)
    angl = A.alloc([128, NT, 16], F32)
    ra = A.alloc([128, NT, 16], F32)
    rb = A.alloc([128, NT, 16], F32)
    rbs = A.alloc([128, NT, 16], F32)
    rbc = A.alloc([128, NT, 16], F32)
    kf = A.alloc([128, NT, 16], F32)

    C1 = 6.28125
    C2 = float(2.0 * np.pi - 6.28125)

    def b1_rope_setup():
        P.dma("sp", posi, pos, r=[], w=["posi"])
        P.dve(lambda e: e.tensor_copy(out=posf, in_=posi), r=["posi"], w=["posf"])
        P.dve(lambda e: e.tensor_tensor(out=ang, in0=posf.unsqueeze(2).to_broadcast([128, NT, 16]),
                                        in1=invb[:, 0:16].unsqueeze(1).to_broadcast([128, NT, 16]), op=ALU.mult), r=["posf", "invb"], w=["ang"])
        P.dve(lambda e: e.tensor_tensor(out=angl, in0=posf.unsqueeze(2).to_broadcast([128, NT, 16]),
                                        in1=invb[:, 16:32].unsqueeze(1).to_broadcast([128, NT, 16]), op=ALU.mult), r=["posf", "invb"], w=["angl"])

    def b1_rope_reduce(shift, rout, nm):
        P.dve(lambda e: e.tensor_scalar(out=ra, in0=ang, scalar1=shift, scalar2=None, op0=ALU.add), r=["ang"], w=["ra"])
        P.dve(lambda e: e.tensor_scalar(out=kf, in0=ra, scalar1=1.0 / (2.0 * PI), scalar2=None, op0=ALU.mult), r=["ra"], w=["kf"])
        P.dve(lambda e: e.tensor_copy(out=ki, in_=kf), r=["kf"], w=["ki"])
        P.dve(lambda e: e.tensor_copy(out=kf, in_=ki), r=["ki"], w=["kf"])
        P.dve(lambda e: e.scalar_tensor_tensor(out=rb, in0=kf, scalar=-C1, in1=ra, op0=ALU.mult, op1=ALU.add), r=["kf", "ra"], w=["rb"])
        P.dve(lambda e: e.scalar_tensor_tensor(out=ra, in0=kf, scalar=-C2, in1=rb, op0=ALU.mult, op1=ALU.add), r=["kf", "rb"], w=["ra"])
        P.dve(lambda e: e.tensor_tensor(out=ra, in0=ra, in1=angl, op=ALU.add), r=["ra", "angl"], w=["ra"])
        P.dve(lambda e: e.tensor_scalar(out=kf, in0=ra, scalar1=PI, scalar2=None, op0=ALU.is_gt), r=["ra"], w=["kf"])
        P.dve(lambda e: e.scalar_tensor_tensor(out=rb, in0=kf, scalar=-2.0 * PI, in1=ra, op0=ALU.mult, op1=ALU.add), r=["kf", "ra"], w=["rb"])
        P.dve(lambda e: e.tensor_scalar(out=kf, in0=rb, scalar1=-PI, scalar2=None, op0=ALU.is_lt), r=["rb"], w=["kf"])
        P.dve(lambda e: e.scalar_tensor_tensor(out=ra, in0=kf, scalar=2.0 * PI, in1=rb, op0=ALU.mult, op1=ALU.add), r=["kf", "rb"], w=["ra"])
        P.dve(lambda e: e.tensor_scalar(out=rout, in0=ra, scalar1=3.1415925, scalar2=-3.1415925, op0=ALU.min, op1=ALU.max), r=["ra"], w=[nm])

    def b1_rope_sin():
        P.act(lambda e: e.activation(out=sn2[:, :, 16:32], in_=rbs, func=AF.Sin), r=["rbs"], w=["sin"])
        P.act(lambda e: e.activation(out=cs2[:, :, 0:16], in_=rbc, func=AF.Sin), r=["rbc"], w=["cos"])
        P.act(lambda e: e.activation(out=cs2[:, :, 16:32], in_=rbc, func=AF.Sin), r=["rbc"], w=["cos"])
        P.dve(lambda e: e.tensor_scalar(out=sn2[:, :, 0:16], in0=sn2[:, :, 16:32], scalar1=-1.0, scalar2=None, op0=ALU.mult), r=["sin"], w=["sin"])

    def b1_stats(i):
        b0, b1 = 6, 7
        mmg(ps[b0], [(uT[:, c, tl(i)], slabM[:, c, 0:512]) for c in range(8)], r=["slabM", f"uT{i // 4}"], w=[f"ps{b0}"])
        mmg(ps[b1][:, 0:160], [(uT[:, c, tl(i)], slabM[:, c, 512:672]) for c in range(8)], r=["slabM", f"uT{i // 4}"], w=[f"ps{b1}"])
        P.act(lambda e, i=i, b0=b0: e.activation(out=sqj3[:, 0, :], in_=ps[b0][:, 0:384], func=AF.Square, accum_out=ssq[:, i:i + 1]),
              r=[f"ps{b0}"], w=["sqj3_0", f"ssq{i}"])
        P.act(lambda e, i=i, b0=b0: e.activation(out=sqj3[:, 1, 0:128], in_=ps[b0][:, 384:512], func=AF.Square, accum_out=ssk[:, 2 * i:2 * i + 1]),
              r=[f"ps{b0}"], w=["sqj3_1", f"ssk{i}"])
        P.act(lambda e, i=i, b1=b1: e.activation(out=sqj3[:, 2, 0:128], in_=ps[b1][:, 0:128], func=AF.Square, accum_out=ssk[:, 2 * i + 1:2 * i + 2]),
              r=[f"ps{b1}"], w=["sqj3_2", f"ssk{i}"])
        P.act(lambda e, i=i, b1=b1: e.activation(out=krtok[:, i, :], in_=ps[b1][:, 128:160], func=AF.Copy), r=[f"ps{b1}"], w=["krtok"])

    def b1_fin():
        allss = [f"ssq{i}" for i in range(NT)] + [f"ssk{i}" for i in range(NT)]
        P.dve(lambda e: e.tensor_tensor(out=sskk, in0=ssk.rearrange("p (i two) -> p i two", two=2)[:, :, 0],
                                        in1=ssk.rearrange("p (i two) -> p i two", two=2)[:, :, 1], op=ALU.add), r=allss, w=["sskk"])
        P.act(lambda e: e.activation(out=lq, in_=ssq, func=AF.Ln, scale=1.0 / 384.0, bias=epsb[:, 0:1]), r=allss + ["epsb"], w=["lq"])
        P.act(lambda e: e.activation(out=lk, in_=sskk, func=AF.Ln, scale=1.0 / 256.0, bias=epsb[:, 0:1]), r=["sskk", "epsb"], w=["lk"])
        P.act(lambda e: e.activation(out=rq, in_=lq, func=AF.Exp, scale=-0.5), r=["lq"], w=["rq0"])
        P.dve(lambda e: e.tensor_scalar(out=rq, in0=rq, scalar1=float(96.0 ** -0.5), scalar2=None, op0=ALU.mult), r=["rq0"], w=["rq"])
        P.act(lambda e: e.activation(out=rkv, in_=lk, func=AF.Exp, scale=-0.5), r=["lk"], w=["rkv"])

    def b1_feat(ct, s):
        b = 5
        mmg(ps[b], [(slabM[:, c, ct * 128:(ct + 1) * 128], uT[:, c, sb_(s)]) for c in range(8)], r=["slabM", f"uT{s}"], w=[f"ps{b}"])
        if ct < 3:
            P.dve(lambda e, b=b, ct=ct, s=s: e.tensor_scalar(out=cqT[:, ct, sb_(s)], in0=ps[b], scalar1=ncq[:, ct:ct + 1], scalar2=None, op0=ALU.mult),
                  r=[f"ps{b}", "ncq"], w=["cqT"])
        else:
            P.dve(lambda e, b=b, ct=ct, s=s: e.tensor_scalar(out=ckvT[:, ct - 3, sb_(s)], in0=ps[b], scalar1=nckv[:, ct - 3:ct - 2], scalar2=None, op0=ALU.mult),
                  r=[f"ps{b}", "nckv"], w=["ckvT"])

    b1_items = []
    feats = [(ct, s) for ct in range(5) for s in range(NSB)]
    extra = {1: b1_rope_setup, 3: lambda: b1_rope_reduce(0.0, rbs, "rbs"), 6: lambda: b1_rope_reduce(PI / 2.0, rbc, "rbc"), 9: b1_rope_sin}
    for k in range(NT):
        b1_items.append(lambda k=k: (b1_stats(k), extra[k]() if k in extra else None))
        b1_items.append(lambda k=k: b1_feat(*feats[k]))
    for k in range(NT, len(feats)):
        b1_items.append(lambda k=k: b1_feat(*feats[k]))
    b1_items.append(b1_fin)
    assert d_end <= mB, (d_end, mB)

    def stage_a(c):
        pa = c % 2

        def attn_mm(e, c=c, pa=pa):
            ins = None
            for h in range(4):
                ins = e.matmul(ps[pa][:, h * 128:(h + 1) * 128], ktT[:, h // 2, tl(c)], qz[:, h, tl(c)], start=True, stop=True)
            return ins
        P.pe(attn_mm, r=["qz", "ktT"], w=[f"ps{pa}"])
        P.dve(lambda e, pa=pa: e.tensor_tensor(out=attn_sb[pa], in0=ps[pa].rearrange("p (h t) -> p h t", h=4),
                                               in1=tri.unsqueeze(1).to_broadcast([128, 4, 128]), op=ALU.mult),
              r=[f"ps{pa}", "tri"], w=[f"attn{pa}"])

    def stage_b(c):
        po = 2 + c % 2
        ab = c % 2

        def o_mm(e, c=c, po=po, ab=ab):
            ins = None
            for h in range(4):
                if c > 0:
                    e.matmul(ps[po][:, h * 128:(h + 1) * 128], qz[:, h, tl(c)], Sall[:, c - 1, h // 2, (h % 2) * 128:(h % 2 + 1) * 128],
                             start=True, stop=False)
                ins = e.matmul(ps[po][:, h * 128:(h + 1) * 128], attn_sb[ab][:, h, :], vg[:, c, h * 128:(h + 1) * 128],
                               start=(c == 0), stop=True)
            return ins
        P.pe(o_mm, r=[f"attn{ab}", "vg", "qz", "Sall"], w=[f"ps{po}"])
        for h in range(4):
            P.act(lambda e, h=h, c=c, po=po: e.activation(out=sqj2[:, h, :], in_=ps[po][:, h * 128:(h + 1) * 128], func=AF.Square,
                                                          accum_out=ssg[:, c * 4 + h:c * 4 + h + 1]),
                  r=[f"ps{po}"], w=[f"sqj2_{h}", f"ssg{c}"])
        P.act(lambda e, c=c: e.activation(out=lng[:, c * 4:c * 4 + 4], in_=ssg[:, c * 4:c * 4 + 4], func=AF.Ln, scale=1.0 / 128.0, bias=epsb[:, 0:1]),
              r=[f"ssg{c}", "epsb"], w=[f"lng{c}"])
        P.act(lambda e, c=c: e.activation(out=rsg[:, c * 4:c * 4 + 4], in_=lng[:, c * 4:c * 4 + 4], func=AF.Exp, scale=-0.5),
              r=[f"lng{c}"], w=[f"rsg{c}"])
        for h in range(4):
            P.dve(lambda e, h=h, c=c, po=po, ab=ab: e.scalar_tensor_tensor(out=ogt[ab][:, h * 128:(h + 1) * 128], in0=ps[po][:, h * 128:(h + 1) * 128],
                                                                           scalar=rsg[:, c * 4 + h:c * 4 + h + 1], in1=sog[:, c, h * 128:(h + 1) * 128],
                                                                           op0=ALU.mult, op1=ALU.mult),
                  r=[f"ps{po}", f"rsg{c}", "sog"], w=[f"ogt{ab}"])

    def stage_c(c):
        ab = c % 2
        pt = 4
        pbt = psb[pt]

        def tro(e, ab=ab, pbt=pbt):
            ins = None
            for h in range(4):
                ins = e.transpose(pbt[:, h * 128:(h + 1) * 128], ogt[ab][:, h * 128:(h + 1) * 128], ident)
            return ins
        P.pe(tro, r=[f"ogt{ab}", "ident"], w=[f"ps{pt}"])
        P.act(lambda e, c=c, pbt=pbt: e.activation(out=OGT[:, :, tl(c)], in_=pbt[:, 0:512].rearrange("p (h t) -> p h t", h=4), func=AF.Copy,
                                                   scale=gnc[:, 0:1]),
              r=[f"ps{pt}", "gnc"], w=["OGT"])

    for c in range(NT + 2):
        if c < NT:
            stage_a(c)
        if 1 <= c <= NT:
            stage_b(c - 1)
        if c >= 2:
            stage_c(c - 2)
        for _ in range(B1_PER_STAGE):
            if b1_items:
                b1_items.pop(0)()
    while b1_items:
        b1_items.pop(0)()
    dump("OGT", OGT, [128, 4, T], BF16)

    if stop_after == "gla":
        return finish(nc, st, P, out, dbg_out, A)

    P.barrier()
    A.top = mB2
    qs9 = [A.alloc([128, 9, 32], F32) for _ in range(2)]
    ra9 = [A.alloc([128, 9, 32], F32) for _ in range(2)]
    rb9 = [A.alloc([128, 9, 32], F32) for _ in range(2)]
    qrot = [A.alloc([128, 8, 96], BF16) for _ in range(2)]
    krot = [A.alloc([128, 8, 96], BF16) for _ in range(2)]
    kro = A.alloc([128, 32], BF16)
    P.dve(lambda e: e.memset(Vt4[:, :, 0:8:2, 64:128], 1.0), r=[], w=["Vones"])
    P.dve(lambda e: e.memset(Vt4[:, :, 1:8:2, 0:64], 1.0), r=[], w=["Vones"])
    assert A.top <= A1_BASE, (A.top, A1_BASE)

    def b_proj(i):
        b = i % 2
        mmg(ps[0][:, 0:384], [(cqT[:, c, tl(i)], wuq[:, c, 0:384]) for c in range(3)], r=["cqT", "wuq"], w=["ps0"])
        mmg(ps[1][:, 0:384], [(cqT[:, c, tl(i)], wuq[:, c, 384:768]) for c in range(3)], r=["cqT", "wuq"], w=["ps1"])
        mmg(ps[2], [(ckvT[:, c, tl(i)], wukv[:, c, 0:512]) for c in range(2)], r=["ckvT", "wukv"], w=["ps2"])
        mmg(ps[3], [(ckvT[:, c, tl(i)], wukv[:, c, 512:1024]) for c in range(2)], r=["ckvT", "wukv"], w=["ps3"])
        for half in range(2):
            pq_ = ps[half][:, 0:384].rearrange("p (h d) -> p h d", h=4)
            hs = slice(half * 4, half * 4 + 4)
            P.act(lambda e, i=i, b=b, pq_=pq_, hs=hs: e.activation(out=qrot[b][:, hs, 0:64], in_=pq_[:, :, 0:64], func=AF.Copy, scale=rq[:, i:i + 1]),
                  r=[f"ps{half}", "rq"], w=[f"qrot{b}"])
            P.act(lambda e, i=i, b=b, pq_=pq_, hs=hs: e.activation(out=qs9[b][:, hs, :], in_=pq_[:, :, 64:96], func=AF.Copy, scale=rq[:, i:i + 1]),
                  r=[f"ps{half}", "rq"], w=[f"qs9{b}"])
        for half in range(2):
            pv = ps[2 + half].rearrange("p (h d) -> p h d", h=4)
            hs = slice(half * 4, half * 4 + 4)
            P.act(lambda e, i=i, b=b, pv=pv, hs=hs: e.activation(out=krot[b][:, hs, 0:64], in_=pv[:, :, 0:64], func=AF.Copy, scale=rkv[:, i:i + 1]),
                  r=[f"ps{2 + half}", "rkv"], w=[f"krot{b}"])
            P.act(lambda e, i=i, half=half, pv=pv: e.activation(out=Vt4[:, i, half * 4:half * 4 + 4:2, 0:64],
                                                                in_=pv[:, 0:4:2, 64:128], func=AF.Copy, scale=rkv[:, i:i + 1]),
                  r=[f"ps{2 + half}", "rkv"], w=["Vt"])
            P.act(lambda e, i=i, half=half, pv=pv: e.activation(out=Vt4[:, i, half * 4 + 1:half * 4 + 4:2, 64:128],
                                                                in_=pv[:, 1:4:2, 64:128], func=AF.Copy, scale=rkv[:, i:i + 1]),
                  r=[f"ps{2 + half}", "rkv"], w=["Vt"])
        P.dve(lambda e, i=i, b=b: e.tensor_copy(out=qs9[b][:, 8, :], in_=krtok[:, i, :]), r=["krtok"], w=[f"qs9{b}"])
        P.dve(lambda e, i=i, b=b: e.tensor_tensor(out=ra9[b], in0=qs9[b], in1=cs2[:, i, :].unsqueeze(1).to_broadcast([128, 9, 32]), op=ALU.mult),
              r=[f"qs9{b}", "cos"], w=[f"ra9{b}"])
        P.dve(lambda e, i=i, b=b: e.tensor_tensor(out=rb9[b][:, :, 0:16], in0=qs9[b][:, :, 16:32],
                                                  in1=sn2[:, i, 0:16].unsqueeze(1).to_broadcast([128, 9, 16]), op=ALU.mult),
              r=[f"qs9{b}", "sin"], w=[f"rb9{b}"])
        P.dve(lambda e, i=i, b=b: e.tensor_tensor(out=rb9[b][:, :, 16:32], in0=qs9[b][:, :, 0:16],
                                                  in1=sn2[:, i, 16:32].unsqueeze(1).to_broadcast([128, 9, 16]), op=ALU.mult),
              r=[f"qs9{b}", "sin"], w=[f"rb9{b}"])
        P.dve(lambda e, b=b: e.tensor_tensor(out=qrot[b][:, :, 64:96], in0=ra9[b][:, 0:8, :], in1=rb9[b][:, 0:8, :], op=ALU.add),
              r=[f"ra9{b}", f"rb9{b}"], w=[f"qrot{b}"])
        P.dve(lambda e, b=b: e.tensor_tensor(out=kro, in0=ra9[b][:, 8, :], in1=rb9[b][:, 8, :], op=ALU.add), r=[f"ra9{b}", f"rb9{b}"], w=["kro"])
        P.dve(lambda e, b=b: e.tensor_copy(out=krot[b][:, :, 64:96], in_=kro.unsqueeze(1).to_broadcast([128, 8, 32])), r=["kro"], w=[f"krot{b}"])

    def b_trans(i):
        b = i % 2
        pq = psb[4 + b]

        def trq(e, b=b, pq=pq):
            ins = None
            for h in range(8):
                ins = e.transpose(pq[0:96, h * 128:(h + 1) * 128], qrot[b][:, h, :], ident)
            return ins
        P.pe(trq, r=[f"qrot{b}", "ident"], w=[f"ps{4 + b}"])
        P.dve(lambda e, i=i, pq=pq: e.tensor_copy(out=QT[:, :, tl(i)], in_=pq[0:96, :].rearrange("p (h t) -> p h t", h=8)),
              r=[f"ps{4 + b}"], w=["QT"])
        pk = psb[6 + b]

        def trk2(e, b=b, pk=pk):
            ins = None
            for h in range(8):
                ins = e.transpose(pk[0:96, h * 128:(h + 1) * 128], krot[b][:, h, :], ident)
            return ins
        P.pe(trk2, r=[f"krot{b}", "ident"], w=[f"ps{6 + b}"])
        P.dve(lambda e, i=i, pk=pk: e.tensor_copy(out=KT[:, :, tl(i)], in_=pk[0:96, :].rearrange("p (h t) -> p h t", h=8)),
              r=[f"ps{6 + b}"], w=["KT"])

    for i in range(NT + 1):
        if i < NT:
            b_proj(i)
        if i >= 1:
            b_trans(i - 1)
    dump("QT", QT, [96, 8, T], BF16)
    dump("KT", KT, [96, 8, T], BF16)
    dump("Vt", Vt4, [128, NT, 8, 128], BF16)
    P.barrier()
    A.top = mB

    OT = A.alloc([128, 4, T], BF16)
    NSC = 6
    PT = [A.alloc([128, 512], BF16) for _ in range(NSC)]
    rinv = A.alloc([128, 512], F32)
    lnl = A.alloc([128, 512], F32)
    woa = A.alloc([128, 4, 1024], BF16)
    wob = A.alloc([128, 4, 1024], BF16)
    gsl = [A.alloc([128, 8, 256], BF16) for _ in range(2)]
    off_e1w = A.top
    P.dma("pool", woa, w_oa.rearrange("(c p) n -> p c n", p=128), r=[], w=["woa"])
    P.dma("pool", wob, w_ob.rearrange("(c p) n -> p c n", p=128), r=[], w=["wob"])

    def load_gsl(ft):
        g = ft % 2
        P.dma("pool", gsl[g][:, :, 0:128], w_in[:, O_GA + ft * 128:O_GA + (ft + 1) * 128].rearrange("(c p) n -> p c n", p=128), r=[], w=[f"gslA{g}"])
        P.dma("pool", gsl[g][:, :, 128:256], w_in[:, O_GB + ft * 128:O_GB + (ft + 1) * 128].rearrange("(c p) n -> p c n", p=128), r=[], w=[f"gslB{g}"])
    load_gsl(0)
    load_gsl(1)
    for p in range(4):
        for Q in range(NSB):
            nj = 4 * Q + 4
            items = [(hh, j) for hh in range(2) for j in range(nj)]
            obank = (6, 7)

            def geom(j, Q=Q):
                m = j - 4 * Q if j >= 4 * Q else 0
                c0 = m * 128
                return c0, 512 - c0

            def emit_qk(k, p=p, Q=Q, items=items):
                hh, j = items[k]
                h = 2 * p + hh
                c0, N = geom(j)
                sbk = k % NSC
                mmg(ps[sbk][:, 0:N], [(KT[:, h, tl(j)], QT[:, h, Q * 512 + c0:(Q + 1) * 512])], r=["KT", "QT"], w=[f"ps{sbk}"])
                P.act(lambda e, sbk=sbk, N=N: e.activation(out=PT[sbk][:, 0:N], in_=ps[sbk][:, 0:N], func=AF.Exp), r=[f"ps{sbk}"], w=[f"PT{sbk}"])
                if j >= 4 * Q:
                    P.dve(lambda e, sbk=sbk: e.tensor_tensor(out=PT[sbk][:, 0:128], in0=PT[sbk][:, 0:128], in1=tri, op=ALU.mult),
                          r=[f"PT{sbk}", "tri"], w=[f"PT{sbk}"])

            def emit_pv(k, p=p, Q=Q, items=items, nj=nj, obank=obank):
                hh, j = items[k]
                h = 2 * p + hh
                c0, N = geom(j)
                sbk = k % NSC
                bk = obank[hh]
                P.pe(lambda e, bk=bk, c0=c0, N=N, h=h, sbk=sbk, j=j: e.matmul(ps[bk][:, c0:512], Vt4[:, j, h, :], PT[sbk][:, 0:N],
                                                                            start=(j == 0), stop=(j == nj - 1)),
                     r=[f"PT{sbk}", "Vt", "Vones"], w=[f"ps{bk}"])
            n = len(items)
            LA = NSC - 1
            for k in range(n + LA):
                if k < n:
                    emit_qk(k)
                if k >= LA:
                    emit_pv(k - LA)
            for hh in range(2):
                bk = obank[hh]
                lr = slice(64, 128) if hh == 0 else slice(0, 64)
                orow = slice(0, 64) if hh == 0 else slice(64, 128)
                P.act(lambda e, bk=bk, lr=lr: e.activation(out=lnl[lr, :], in_=ps[bk][lr, :], func=AF.Ln), r=[f"ps{bk}"], w=[f"lnl{hh}"])
                P.act(lambda e, lr=lr: e.activation(out=lnl[lr, :], in_=lnl[lr, :], func=AF.Exp, scale=-1.0), r=[f"lnl{hh}"], w=[f"lnl{hh}"])
                P.dve(lambda e, lr=lr, orow=orow: e.tensor_copy(out=rinv[orow, :], in_=lnl[lr, :]), r=[f"lnl{hh}"], w=[f"rinv{hh}"])
                P.dve(lambda e, p=p, Q=Q, bk=bk, orow=orow: e.tensor_tensor(out=OT[orow, p, sb_(Q)], in0=ps[bk][orow, :], in1=rinv[orow, :], op=ALU.mult),
                      r=[f"ps{bk}", f"rinv{hh}"], w=["OT"])
    dump("OT", OT, [128, 4, T], BF16)
    if stop_after == "attn":
        return finish(nc, st, P, out, dbg_out, A)
    P.barrier(keep=("woa", "wob", "gslA0", "gslB0", "gslA1", "gslB1"))

    A.top = m0
    mixT = A.alloc([128, 8, T], BF16)
    off_after_mix = A.top
    assert A.top <= mB, (A.top, mB)
    off_e1t = A.top
    ea = [A.alloc([128, 512], F32) for _ in range(2)]
    eb2 = [A.alloc([128, 512], F32) for _ in range(2)]
    tA = A.alloc([128, 512], F32)
    tB = A.alloc([128, 512], F32)
    off_wout = (A.top + 63) // 64 * 64
    wout = A.alloc([128, 8, 1024], BF16)
    P.dma("pool", wout, w_out.rearrange("(c p) n -> p c n", p=128), r=[], w=["wout"])
    for ft in range(8):
        g = ft % 2
        if ft >= 2:
            load_gsl(ft)
        for s in range(NSB):
            q = (ft * NSB + s) % 2
            ba, bb, bya, byb = (0, 1, 2, 3) if q == 0 else (4, 5, 6, 7)
            mmg(ps[ba], [(gsl[g][:, c, 0:128], uT[:, c, sb_(s)]) for c in range(8)], r=[f"gslA{g}", f"uT{s}"], w=[f"ps{ba}"])
            mmg(ps[bb], [(gsl[g][:, c, 128:256], uT[:, c, sb_(s)]) for c in range(8)], r=[f"gslB{g}", f"uT{s}"], w=[f"ps{bb}"])
            mmg(ps[bya], [(woa[:, pp, tl(ft)], OT[:, pp, sb_(s)]) for pp in range(4)], r=["woa", "OT"], w=[f"ps{bya}"])
            mmg(ps[byb], [(wob[:, pp, tl(ft)], OGT[:, pp, sb_(s)]) for pp in range(4)], r=["wob", "OGT"], w=[f"ps{byb}"])
            for (bk, et, nm) in ((ba, ea[q], f"ea{q}"), (bb, eb2[q], f"eb{q}")):
                P.act(lambda e, bk=bk, et=et: e.activation(out=et, in_=ps[bk], func=AF.Exp, scale=-1.0), r=[f"ps{bk}"], w=[nm])
                P.act(lambda e, et=et: e.activation(out=et, in_=et, func=AF.Ln, bias=one_b[:, 0:1]), r=[nm, "one_b"], w=[nm])
                P.act(lambda e, et=et: e.activation(out=et, in_=et, func=AF.Exp, scale=-1.0), r=[nm], w=[nm])
            P.dve(lambda e, q=q, bya=bya: e.tensor_tensor(out=tA, in0=ps[bya], in1=ea[q], op=ALU.mult), r=[f"ps{bya}", f"ea{q}"], w=["tA"])
            P.dve(lambda e, q=q, byb=byb: e.tensor_tensor(out=tB, in0=ps[byb], in1=eb2[q], op=ALU.mult), r=[f"ps{byb}", f"eb{q}"], w=["tB"])
            P.dve(lambda e, ft=ft, s=s: e.tensor_tensor(out=mixT[:, ft, sb_(s)], in0=tA, in1=tB, op=ALU.add), r=["tA", "tB"], w=["mixT"])
    dump("mixT", mixT, [128, 8, T], BF16)
    if stop_after == "mix":
        return finish(nc, st, P, out, dbg_out, A)
    P.barrier(keep=("wout",))

    h1 = A.alloc([128, NT, D], F32)
    off_after_h1 = A.top
    fsl = [A.alloc([128, 8, 256], BF16) for _ in range(2)]
    fnb = A.alloc([128, D], F32)
    ot = [A.alloc([128, D], F32) for _ in range(2)]
    ssf = A.alloc([128, NT], F32)
    lnf = A.alloc([128, NT], F32)
    rsf = A.alloc([128, NT], F32)
    sqf = A.alloc([128, D], BF16)
    off_e2t = A.top
    A.top = off_e1t
    xt2 = [A.alloc([128, D], F32) for _ in range(2)]
    assert A.top <= off_wout

    def load_fsl(f):
        g = f % 2
        P.dma("pool", fsl[g][:, :, 0:128], w_fg[:, f * 128:(f + 1) * 128].rearrange("(c p) n -> p c n", p=128), r=[], w=[f"fslG{g}"])
        P.dma("pool", fsl[g][:, :, 128:256], w_fu[:, f * 128:(f + 1) * 128].rearrange("(c p) n -> p c n", p=128), r=[], w=[f"fslU{g}"])
    load_fsl(0)
    load_fsl(1)
    P.dma("pool", fnb, v_fn, r=[], w=["fnb"])
    u2T = uT
    sv_top = A.top
    A.top = off_e2t
    nt_v = make_nt(u2T, lnffn, "v", NT, banks=(6, 7))
    off_e2t = A.top
    A.top = sv_top
    for i in range(NT):
        b = i % 2
        P.dma("sp", xt2[b], x[tl(i), :], r=[], w=[f"xt2{b}"])
        for half in range(2):
            bk = 2 * b + half
            mmg(ps[bk], [(mixT[:, ft, tl(i)], wout[:, ft, half * 512:(half + 1) * 512]) for ft in range(8)], r=["mixT", "wout"], w=[f"ps{bk}"])
            P.dve(lambda e, i=i, b=b, half=half, bk=bk: e.tensor_tensor(out=h1[:, i, half * 512:(half + 1) * 512], in0=ps[bk],
                                                                       in1=xt2[b][:, half * 512:(half + 1) * 512], op=ALU.add),
                  r=[f"ps{bk}", f"xt2{b}"], w=[f"h1_{i}"])
            if half == 0:
                if i >= 1:
                    nt_v.part1(i - 1, h1[:, i - 1, :], [f"h1_{i - 1}"])
                if i >= 2:
                    nt_v.part2(i - 2)
    nt_v.part2(NT - 2)
    nt_v(NT - 1, h1[:, NT - 1, :], [f"h1_{NT - 1}"])
    dump("h1", h1, [128, NT, D], F32)

    A.top = off_OGT
    wd = A.alloc([128, 6, 1024], BF16)
    A.top = m0
    aT = A.alloc([128, 6, T], BF16)
    tg = [A.alloc([128, 512], F32) for _ in range(2)]
    tt = [A.alloc([128, 512], F32) for _ in range(2)]
    assert A.top <= off_after_mix
    A.top = off_e2t

    def final_tile(i):
        b = i % 2
        P.act(lambda e, i=i: e.activation(out=sqf, in_=h1[:, i, :], func=AF.Square, accum_out=ssf[:, i:i + 1]), r=[f"h1_{i}"], w=["sqf", f"ssf{i}"])
        P.act(lambda e, i=i: e.activation(out=lnf[:, i:i + 1], in_=ssf[:, i:i + 1], func=AF.Ln, scale=1.0 / D, bias=epsb[:, 0:1]),
              r=[f"ssf{i}", "epsb"], w=[f"lnf{i}"])
        P.act(lambda e, i=i: e.activation(out=rsf[:, i:i + 1], in_=lnf[:, i:i + 1], func=AF.Exp, scale=-0.5), r=[f"lnf{i}"], w=[f"rsf{i}"])
        P.dve(lambda e, i=i, b=b: e.scalar_tensor_tensor(out=ot[b], in0=h1[:, i, :], scalar=rsf[:, i:i + 1], in1=fnb, op0=ALU.mult, op1=ALU.mult),
              r=[f"h1_{i}", f"rsf{i}", "fnb"], w=[f"ot{b}"])
        P.dma("sp", out[tl(i), :], ot[b], r=[f"ot{b}"], w=[f"out{i}"])

    groups = [(0, 5), (5, 10), (10, 16), (16, 22)]
    for (f0, f1) in groups:
        nf = f1 - f0
        for f in range(f0, f1):
            g = f % 2
            if f >= 2:
                load_fsl(f)
            if f == f0 + 1:
                P.dma("pool", wd[:, 0:nf, :], w_fd[f0 * 128:f1 * 128, :].rearrange("(f p) n -> p f n", p=128), r=[], w=["wd"])
            for s in range(NSB):
                q = s % 2
                bg_, bu_ = (0, 1) if q == 0 else (2, 3)
                alias0 = ["mixT"] if (f == 0 and s == 0) else []
                mmg(ps[bg_], [(fsl[g][:, c, 0:128], u2T[:, c, sb_(s)]) for c in range(8)], r=[f"fslG{g}", f"vT{s}"], w=[f"ps{bg_}"])
                mmg(ps[bu_], [(fsl[g][:, c, 128:256], u2T[:, c, sb_(s)]) for c in range(8)], r=[f"fslU{g}", f"vT{s}"], w=[f"ps{bu_}"])
                P.act(lambda e, q=q, bg_=bg_: e.activation(out=tg[q], in_=ps[bg_], func=AF.Exp, scale=-1.0), r=[f"ps{bg_}"], w=[f"tg{q}"] + alias0)
                P.act(lambda e, q=q: e.activation(out=tg[q], in_=tg[q], func=AF.Ln, bias=one_b[:, 0:1]), r=[f"tg{q}", "one_b"], w=[f"tg{q}"])
                P.act(lambda e, q=q: e.activation(out=tg[q], in_=tg[q], func=AF.Exp, scale=-1.0), r=[f"tg{q}"], w=[f"tg{q}"])
                P.dve(lambda e, q=q, bg_=bg_: e.tensor_tensor(out=tt[q], in0=ps[bg_], in1=tg[q], op=ALU.mult), r=[f"ps{bg_}", f"tg{q}"], w=[f"tt{q}"] + alias0)
                P.dve(lambda e, q=q, bu_=bu_, f=f, f0=f0, s=s: e.tensor_tensor(out=aT[:, f - f0, sb_(s)], in0=ps[bu_], in1=tt[q], op=ALU.mult),
                      r=[f"ps{bu_}", f"tt{q}"], w=[f"aT{s}"])
        for i in range(NT):
            for half in range(2):
                bk = 4 + (2 * i + half) % 4
                mmg(ps[bk], [(aT[:, fl, tl(i)], wd[:, fl, half * 512:(half + 1) * 512]) for fl in range(nf)], r=[f"aT{i // 4}", "wd"], w=[f"ps{bk}"])
                P.dve(lambda e, i=i, half=half, bk=bk: e.tensor_tensor(out=h1[:, i, half * 512:(half + 1) * 512], in0=ps[bk],
                                                                      in1=h1[:, i, half * 512:(half + 1) * 512], op=ALU.add),
                      r=[f"ps{bk}", f"h1_{i}"], w=[f"h1_{i}"])
            if f1 == NF and i >= 1:
                final_tile(i - 1)
    final_tile(NT - 1)
    return finish(nc, st, P, out, dbg_out, A)


def finish(nc, st, P, out, dbg_out, A):
    P.barrier()
    P.emit(nc, st)
    st.close()
    return nc, dbg_out


def host_consts():
    half = 16
    inv64 = 1.0 / (10000.0 ** (np.arange(half, dtype=np.float64) / half))
    hi = inv64.astype(np.float32)
    lo = (inv64 - hi.astype(np.float64)).astype(np.float32)
    c = {}
    c["c_inv"] = np.ascontiguousarray(np.broadcast_to(np.concatenate([hi, lo])[None, :], (128, 32))).astype(np.float32)
    c["c_ident"] = np.eye(128, dtype=np.float32).astype(ml_dtypes.bfloat16)
    j = np.arange(128)[:, None]
    i = np.arange(128)[None, :]
    c["c_tri"] = (i >= j).astype(np.float32).astype(ml_dtypes.bfloat16)
    return c


def pcol(v, n):
    return np.ascontiguousarray(np.asarray(v, dtype=np.float32).reshape(n, 128).T)


def shared_map(inp):
    f = lambda a: np.ascontiguousarray(np.asarray(a, dtype=np.float32))
    m = dict(
        w_in=f(inp["w_in"][0]), w_uq=f(inp["mla_w_uq"][0]), w_ukv=f(inp["mla_w_ukv"][0]), w_oa=f(inp["mla_w_o"][0]),
        w_g2=f(inp["gla_w_gate2"][0]), w_ob=f(inp["gla_w_o"][0]), w_out=f(inp["w_out"][0]),
        w_fg=f(inp["ffn_w_gate"][0]), w_fu=f(inp["ffn_w_up"][0]), w_fd=f(inp["ffn_w_down"][0]),
        v_lnmix=pcol(inp["ln_mix"][0], 8), v_lnffn=pcol(inp["ln_ffn"][0], 8), v_ncq=pcol(inp["mla_norm_cq"][0], 3),
        v_nckv=pcol(inp["mla_norm_ckv"][0], 2), v_bg=pcol(inp["gla_b_gate"][0], 2),
        v_gn=np.ascontiguousarray(np.broadcast_to(np.asarray(inp["gla_norm"][0], dtype=np.float32)[None, :], (128, 128))),
        v_gnc=pcol(inp["gla_norm"][0], 1),
        v_fn=np.ascontiguousarray(np.broadcast_to(np.asarray(inp["final_norm"], dtype=np.float32)[None, :], (128, D))),
    )
    m.update(host_consts())
    return m


def core_map(inp, b, shared):
    m = dict(shared)
    m["x"] = np.ascontiguousarray(np.asarray(inp["x"][b], dtype=np.float32))
    m["pos"] = np.ascontiguousarray(np.asarray(inp["positions"][b], dtype=np.int32).reshape(NT, 128).T)
    return m


_CACHE = {}


def kernel(**inputs):
    if "nc" not in _CACHE:
        _CACHE["nc"] = build()[0]
    nc = _CACHE["nc"]
    shared = shared_map(inputs)
    B = np.asarray(inputs["x"]).shape[0]
    in_maps = [core_map(inputs, b, shared) for b in range(B)]
    res = run_bass_kernel_spmd(nc, in_maps, core_ids=list(range(B)))
    return np.stack([np.asarray(r["out"], dtype=np.float32) for r in res.results], axis=0)
```

```python
import contextlib
import numpy as np
import ml_dtypes
import concourse.bass as bass
import concourse.mybir as mybir
from concourse.bass_utils import run_bass_kernel_spmd

F32 = mybir.dt.float32
BF16 = mybir.dt.bfloat16
I32 = mybir.dt.int32
U8 = mybir.dt.uint8
AF = mybir.ActivationFunctionType
ALU = mybir.AluOpType

T = 2048
D = 1024
NT = 16
NSB = 4
DFF = 2816
NF = 22
DIN = 4272
O_CQ, O_CKV, O_KR, O_GQ, O_GK, O_GV, O_GLR, O_GOG, O_GA, O_GB = 0, 384, 640, 672, 928, 1184, 1696, 1712, 2224, 3248
EPS = 1e-6
PI = float(np.pi)
NSLOT = 8
B1_PER_STAGE = 2
ENGS = ("pe", "act", "dve", "pool", "sp")


_DISJOINT = ("mixT", "qz", "ktT", "vg", "sog", "ktok", "Vt", "QT", "KT", "OT", "OGT", "cqT", "ckvT", "krtok", "glrT", "Sall",
             "ebl", "uT", "vT", "ssg", "ssk", "aT")


def _disjoint(k):
    return k.rstrip("0123456789_") in _DISJOINT


class Prog:
    def __init__(self):
        self.ops = []
        self.lw = {}
        self.rd = {}
        self.last = {}
        self.dmas_since_barrier = []

    def add(self, eng, fn, r=(), w=(), dma=False):
        idx = len(self.ops)
        deps = set()
        for k in r:
            p = self.lw.get(k)
            if p is not None:
                deps.add(p)
        for k in w:
            p = self.lw.get(k)
            if p is not None and not (_disjoint(k) and not dma and not self.ops[p]["dma"] and self.ops[p]["eng"] == eng):
                deps.add(p)
            for q in self.rd.get(k, {}).values():
                if isinstance(q, list):
                    deps.update(q)
                else:
                    deps.add(q)
        for k in r:
            d = self.rd.setdefault(k, {})
            if dma:
                d.setdefault("dma", []).append(idx)
            else:
                d[eng] = idx
        for k in w:
            self.lw[k] = idx
            self.rd[k] = {}
        deps.discard(idx)
        self.ops.append(dict(eng=eng, fn=fn, deps=deps, dma=dma, sig=False))
        if dma:
            self.dmas_since_barrier.append(idx)
        else:
            self.last[eng] = idx
        return idx

    def pe(self, fn, r=(), w=()):
        return self.add("pe", fn, r, w)

    def act(self, fn, r=(), w=()):
        return self.add("act", fn, r, w)

    def dve(self, fn, r=(), w=()):
        return self.add("dve", fn, r, w)

    def pool(self, fn, r=(), w=()):
        return self.add("pool", fn, r, w)

    def dma(self, q, out, in_, r=(), w=()):
        return self.add(q, lambda e: e.dma_start(out=out, in_=in_), r, w, dma=True)

    def barrier(self, keep=()):
        kept = {self.lw[k] for k in keep if k in self.lw and self.ops[self.lw[k]]["dma"]}
        deps = set(self.last.values()) | (set(self.dmas_since_barrier) - kept)
        for eng in ENGS:
            self.ops.append(dict(eng=eng, fn=None, deps=set(deps), dma=False, sig=False))
        self.last = {}
        self.lw = {k: v for k, v in self.lw.items() if v in kept}
        self.rd = {}
        self.dmas_since_barrier = sorted(kept)

    def emit(self, nc, st):
        ops = self.ops
        for op in ops:
            for d in op["deps"]:
                p = ops[d]
                if p["dma"]:
                    continue
                if p["eng"] == "pe" and op["eng"] == "pe" and not op["dma"]:
                    continue
                p["sig"] = True
        cnt = {e: 0 for e in ENGS}
        dcnt = {"pool": 0, "sp": 0}
        for op in ops:
            if op["dma"]:
                k = dcnt[op["eng"]]
                dcnt[op["eng"]] += 1
                op["slot"] = k % NSLOT
                op["val"] = 16 * (k // NSLOT + 1)
            elif op["sig"]:
                cnt[op["eng"]] += 1
                op["cnt"] = cnt[op["eng"]]
        sem = {e: st.enter_context(nc.semaphore("s_" + e)) for e in ENGS}
        dsem = {q: [st.enter_context(nc.semaphore(f"d_{q}{i}")) for i in range(NSLOT)] for q in ("pool", "sp")}
        per = {e: [] for e in ENGS}
        for i, op in enumerate(ops):
            per[op["eng"]].append(i)

        def run(engname, e):
            seen = {}
            for i in per[engname]:
                op = ops[i]
                waits = {}
                for d in op["deps"]:
                    p = ops[d]
                    if p["dma"]:
                        key = ("d", p["eng"], p["slot"])
                        v = p["val"]
                    else:
                        if p["eng"] == "pe" and engname == "pe" and not op["dma"]:
                            continue
                        key = ("c", p["eng"])
                        v = p["cnt"]
                    if v > waits.get(key, 0):
                        waits[key] = v
                if op["dma"] and op["val"] > 16:
                    key = ("d", engname, op["slot"])
                    waits[key] = max(waits.get(key, 0), op["val"] - 16)
                for key, v in waits.items():
                    if seen.get(key, 0) >= v:
                        continue
                    seen[key] = v
                    s = sem[key[1]] if key[0] == "c" else dsem[key[1]][key[2]]
                    e.wait_ge(s, v)
                if op["fn"] is None:
                    continue
                ins = op["fn"](e)
                if op["dma"]:
                    ins.then_inc(dsem[engname][op["slot"]], 16)
                elif op["sig"]:
                    ins.then_inc(sem[engname], 1)

        block = st.enter_context(nc.Block())

        @block.tensor
        def _(e):
            run("pe", e)

        @block.scalar
        def _(e):
            run("act", e)

        @block.vector
        def _(e):
            run("dve", e)

        @block.gpsimd
        def _(e):
            run("pool", e)

        @block.sync
        def _(e):
            run("sp", e)


class Arena:
    def __init__(self, nc, nbytes):
        self.t = nc.alloc_sbuf_tensor("arena", [128, nbytes], U8)
        self.n = nbytes
        self.top = 0
        self.peak = 0

    def alloc(self, shape, dt):
        esz = 4 if dt in (F32, I32) else 2
        nb = int(np.prod(shape[1:])) * esz
        off = (self.top + 63) // 64 * 64
        assert off + nb <= self.n, f"SBUF arena overflow {off + nb} > {self.n}"
        self.top = off + nb
        self.peak = max(self.peak, self.top)
        a = self.t[0:shape[0], off:off + nb].bitcast(dt)
        if len(shape) == 3:
            a = a.rearrange("p (a b) -> p a b", a=shape[1])
        elif len(shape) == 4:
            a = a.rearrange("p (a b c) -> p a b c", a=shape[1], b=shape[2])
        return a


def build(debug=(), stop_after=None):
    nc = bass.Bass("TRN2", target_bir_lowering=False)

    def din(name, shape, dt=F32):
        return nc.dram_tensor(name, list(shape), dt, kind="ExternalInput").ap()

    x = din("x", [T, D])
    pos = din("pos", [128, NT], I32)
    w_in = din("w_in", [D, DIN])
    w_uq = din("w_uq", [384, 768])
    w_ukv = din("w_ukv", [256, 1024])
    w_oa = din("w_oa", [512, 1024])
    w_g2 = din("w_g2", [16, 256])
    w_ob = din("w_ob", [512, 1024])
    w_out = din("w_out", [D, D])
    w_fg = din("w_fg", [D, DFF])
    w_fu = din("w_fu", [D, DFF])
    w_fd = din("w_fd", [DFF, D])
    v_lnmix = din("v_lnmix", [128, 8])
    v_lnffn = din("v_lnffn", [128, 8])
    v_ncq = din("v_ncq", [128, 3])
    v_nckv = din("v_nckv", [128, 2])
    v_bg = din("v_bg", [128, 2])
    v_gn = din("v_gn", [128, 128])
    v_gnc = din("v_gnc", [128, 1])
    v_fn = din("v_fn", [128, D])
    c_inv = din("c_inv", [128, 32])
    c_ident = din("c_ident", [128, 128], BF16)
    c_tri = din("c_tri", [128, 128], BF16)
    out = nc.dram_tensor("out", [T, D], F32, kind="ExternalOutput").ap()
    dbg_out = {}

    st = contextlib.ExitStack()
    A = Arena(nc, 212800)
    P = Prog()
    ps = [nc.alloc_psum_tensor(f"ps{i}", [128, 512], F32)[:] for i in range(8)]
    psb = [p.bitcast(BF16) for p in ps]

    def sb_(s):
        return slice(s * 512, (s + 1) * 512)

    def tl(i):
        return slice(i * 128, (i + 1) * 128)

    def mmg(out_ap, pairs, r, w):
        def fn(e):
            n = len(pairs)
            ins = None
            for k, (l, rh) in enumerate(pairs):
                ins = e.matmul(out_ap, l, rh, start=(k == 0), stop=(k == n - 1))
            return ins
        P.pe(fn, r, w)

    def dump(name, ap, shape, dt=F32):
        if name not in debug:
            return
        d = nc.dram_tensor("dbg_" + name, list(shape), dt, kind="ExternalOutput").ap()
        dbg_out[name] = d
        P.barrier()
        P.dma("sp", d, ap, r=[], w=["dbg_" + name])
        P.barrier()

    lnmix = A.alloc([128, 8], F32)
    lnffn = A.alloc([128, 8], F32)
    ncq = A.alloc([128, 3], F32)
    nckv = A.alloc([128, 2], F32)
    bg = A.alloc([128, 2], F32)
    nbg = A.alloc([128, 2], F32)
    gnb = A.alloc([128, 128], F32)
    gnc = A.alloc([128, 1], F32)
    invb = A.alloc([128, 32], F32)
    ident = A.alloc([128, 128], BF16)
    tri = A.alloc([128, 128], BF16)
    ones_bf = A.alloc([128, 64], BF16)
    posi = A.alloc([128, NT], I32)
    ki = A.alloc([128, NT, 16], I32)
    for dst, src, k in ((lnmix, v_lnmix, "lnmix"), (lnffn, v_lnffn, "lnffn"), (ncq, v_ncq, "ncq"),
                        (nckv, v_nckv, "nckv"), (bg, v_bg, "bg"), (gnb, v_gn, "gnb"), (gnc, v_gnc, "gnc"), (invb, c_inv, "invb"),
                        (ident, c_ident, "ident"), (tri, c_tri, "tri")):
        P.dma("pool", dst, src, r=[], w=[k])
    P.dve(lambda e: e.tensor_scalar(out=nbg, in0=bg, scalar1=-1.0, scalar2=None, op0=ALU.mult), r=["bg"], w=["nbg"])
    P.dve(lambda e: e.memset(ones_bf, 1.0), r=[], w=["ones"])

    uT = A.alloc([128, 8, T], BF16)
    off_OGT = (A.top + 63) // 64 * 64
    OGT = A.alloc([128, 4, T], BF16)
    base_mark = A.top

    def make_nt(dstT, gain, tagp, ntiles, banks=(0, 1)):
        xn = [A.alloc([128, D], BF16) for _ in range(2)]
        sqj = A.alloc([128, D], BF16)
        ssq = A.alloc([128, ntiles], F32)
        lnv = A.alloc([128, ntiles], F32)
        rs = A.alloc([128, ntiles], F32)

        def part1(i, src, skeys):
            b = i % 2
            P.act(lambda e, src=src, i=i: e.activation(out=sqj, in_=src, func=AF.Square, accum_out=ssq[:, i:i + 1]),
                  r=skeys, w=[tagp + "sqj", f"{tagp}ss{i}"])
            P.act(lambda e, i=i: e.activation(out=lnv[:, i:i + 1], in_=ssq[:, i:i + 1], func=AF.Ln, scale=1.0 / D, bias=epsb[:, 0:1]),
                  r=[f"{tagp}ss{i}", "epsb"], w=[f"{tagp}ln{i}"])
            P.act(lambda e, i=i: e.activation(out=rs[:, i:i + 1], in_=lnv[:, i:i + 1], func=AF.Exp, scale=-0.5),
                  r=[f"{tagp}ln{i}"], w=[f"{tagp}rs{i}"])
            P.dve(lambda e, src=src, b=b, i=i: e.tensor_scalar(out=xn[b], in0=src, scalar1=rs[:, i:i + 1], scalar2=None, op0=ALU.mult),
                  r=skeys + [f"{tagp}rs{i}"], w=[f"{tagp}xn{b}"])

        def part2(i):
            b = i % 2
            bk = banks[b]
            pb = psb[bk]

            def tr(e, b=b, pb=pb):
                ins = None
                for c in range(8):
                    ins = e.transpose(pb[:, c * 128:(c + 1) * 128], xn[b][:, c * 128:(c + 1) * 128], ident)
                return ins
            P.pe(tr, r=[f"{tagp}xn{b}", "ident"], w=[f"ps{bk}"])
            P.dve(lambda e, pb=pb, i=i: e.tensor_tensor(out=dstT[:, :, tl(i)], in0=pb.rearrange("p (c t) -> p c t", c=8),
                                                        in1=gain[:, 0:8].unsqueeze(2).to_broadcast([128, 8, 128]), op=ALU.mult),
                  r=[f"ps{bk}", "lnmix", "lnffn"], w=[f"{tagp}T{i // 4}"])

        def step(i, src, skeys):
            part1(i, src, skeys)
            part2(i)
        step.part1 = part1
        step.part2 = part2
        return step

    epsb = A.alloc([128, 1], F32)
    P.dve(lambda e: e.memset(epsb, EPS), r=[], w=["epsb"])
    one_b = A.alloc([128, 1], F32)
    P.dve(lambda e: e.memset(one_b, 1.0), r=[], w=["one_b"])
    base_mark = A.top

    A1_BASE = 193000
    A.top = A1_BASE
    xt = [A.alloc([128, D], F32) for _ in range(3)]

    nt_u = make_nt(uT, lnmix, "u", NT)
    for i in range(NT):
        b = i % 3
        P.dma("sp", xt[b], x[tl(i), :], r=[], w=[f"xt{b}"])
        nt_u(i, xt[b], [f"xt{b}"])
    dump("uT", uT, [128, 8, T], BF16)
    A.top = base_mark
    if stop_after == "a1":
        return finish(nc, st, P, out, dbg_out, A)

    m0 = A.top
    qz = A.alloc([128, 4, T], BF16)
    ktT = A.alloc([128, 2, T], BF16)
    ktok = A.alloc([128, NT, 256], BF16)
    vg = A.alloc([128, NT, 512], BF16)
    sog = A.alloc([128, NT, 512], BF16)
    ebl = A.alloc([128, 2, NT], F32)
    Sall = A.alloc([128, NT - 1, 2, 256], BF16)
    Z = A.alloc([128, 2, 256], F32)
    m1 = A.top
    glrT = A.alloc([16, T], BF16)
    wlr = A.alloc([128, 8, 16], BF16)
    wg2 = A.alloc([16, 256], BF16)
    rmask = A.alloc([128, T], BF16)
    bufA = A.alloc([128, T], F32)
    bufB = A.alloc([128, T], F32)
    bufC = A.alloc([128, T], F32)
    tmpE = [A.alloc([128, 512], F32) for _ in range(2)]
    slab = [A.alloc([128, 8, 512], BF16) for _ in range(2)]

    P.dma("pool", wlr, w_in[:, O_GLR:O_GLR + 16].rearrange("(c p) n -> p c n", p=128), r=[], w=["wlr"])
    P.dma("pool", wg2, w_g2, r=[], w=["wg2"])
    P.dma("pool", slab[0], w_in[:, O_GQ:O_GQ + 512].rearrange("(c p) n -> p c n", p=128), r=[], w=["slab0"])
    P.dma("pool", slab[1], w_in[:, O_GV:O_GV + 512].rearrange("(c p) n -> p c n", p=128), r=[], w=["slab1"])
    sv_top = A.top
    A.top = A1_BASE
    slabM = A.alloc([128, 8, 672], BF16)
    wuq = A.alloc([128, 3, 768], BF16)
    wukv = A.alloc([128, 2, 1024], BF16)
    A.top = sv_top
    a1_keys = ["xt0", "xt1", "xt2", "uxn0", "uxn1", "usqj"] + [f"u{t}{i}" for t in ("ss", "ln", "rs") for i in range(NT)]
    P.dma("pool", slabM, w_in[:, 0:672].rearrange("(c p) n -> p c n", p=128), r=[], w=["slabM"] + a1_keys)
    P.dma("pool", wuq, w_uq.rearrange("(c p) n -> p c n", p=128), r=[], w=["wuq"] + a1_keys)
    P.dma("pool", wukv, w_ukv.rearrange("(c p) n -> p c n", p=128), r=[], w=["wukv"] + a1_keys)
    P.dve(lambda e: e.memset(qz, 0.0), r=[], w=["qz"])
    P.dve(lambda e: e.memset(rmask, 1.0), r=[], w=["rmask"])
    P.dve(lambda e: e.memset(rmask.rearrange("p (c t) -> p c t", t=128)[:, :, 0:1], 0.0), r=[], w=["rmask"])
    for s in range(NSB):
        b = s % 2
        mmg(ps[b][0:16, :], [(wlr[:, c, :], uT[:, c, sb_(s)]) for c in range(8)], r=["wlr", f"uT{s}"], w=[f"ps{b}"])
        P.act(lambda e, b=b, s=s: e.activation(out=glrT[:, sb_(s)], in_=ps[b][0:16, :], func=AF.Copy), r=[f"ps{b}"], w=["glrT"])
    def v_tile(i):
        b = 4 + i % 2
        mmg(ps[b], [(uT[:, c, tl(i)], slab[1][:, c, :]) for c in range(8)], r=["slab1", f"uT{i // 4}"], w=[f"ps{b}"])
        P.act(lambda e, b=b, i=i: e.activation(out=vg[:, i, :], in_=ps[b], func=AF.Copy), r=[f"ps{b}"], w=["vg"])

    for th in range(2):
        for s in range(NSB):
            b = s % 2
            mmg(ps[b], [(wg2[:, th * 128:(th + 1) * 128], glrT[:, sb_(s)])], r=["wg2", "glrT"], w=[f"ps{b}"])
            P.act(lambda e, b=b, th=th: e.activation(out=tmpE[b], in_=ps[b], func=AF.Exp, scale=-1.0, bias=nbg[:, th:th + 1]),
                  r=[f"ps{b}", "nbg"], w=[f"tmpE{b}"])
            P.act(lambda e, b=b, s=s: e.activation(out=bufA[:, sb_(s)], in_=tmpE[b], func=AF.Ln, bias=one_b[:, 0:1]),
                  r=[f"tmpE{b}", "one_b"], w=["bufA"])
        for i in range(th * 8, th * 8 + 8):
            v_tile(i)
        P.dve(lambda e: e.tensor_tensor_scan(out=bufB, data0=rmask, data1=bufA, initial=0.0, op0=ALU.mult, op1=ALU.add),
              r=["bufA", "rmask"], w=["bufB"])
        P.act(lambda e: e.activation(out=bufA, in_=bufB, func=AF.Exp, scale=-1.0 / 16.0), r=["bufB"], w=["bufA"])
        P.act(lambda e: e.activation(out=bufC, in_=bufB, func=AF.Exp, scale=1.0 / 16.0), r=["bufB"], w=["bufC"])
        P.dve(lambda e, th=th: e.tensor_copy(out=ebl[:, th, :], in_=bufA.rearrange("p (c t) -> p c t", t=128)[:, :, 127]),
              r=["bufA"], w=["ebl"])
        for s in range(NSB):
            b = s % 2
            mmg(ps[b], [(slab[0][:, c, th * 128:(th + 1) * 128], uT[:, c, sb_(s)]) for c in range(8)], r=["slab0", f"uT{s}"], w=[f"ps{b}"])
            for hh in range(2):
                pr = slice(hh * 64, hh * 64 + 64)
                P.dve(lambda e, b=b, s=s, th=th, hh=hh, pr=pr: e.scalar_tensor_tensor(out=qz[pr, 2 * th + hh, sb_(s)], in0=ps[b][pr, :], scalar=0.125,
                                                                                     in1=bufA[pr, sb_(s)], op0=ALU.mult, op1=ALU.mult),
                      r=[f"ps{b}", "bufA"], w=["qz"])
            b2 = 2 + s % 2
            mmg(ps[b2], [(slab[0][:, c, 256 + th * 128:256 + (th + 1) * 128], uT[:, c, sb_(s)]) for c in range(8)], r=["slab0", f"uT{s}"], w=[f"ps{b2}"])
            P.dve(lambda e, b2=b2, s=s, th=th: e.tensor_tensor(out=ktT[:, th, sb_(s)], in0=ps[b2], in1=bufC[:, sb_(s)], op=ALU.mult),
                  r=[f"ps{b2}", "bufC"], w=["ktT"])
    for g in range(4):
        b = 4 + g % 2
        pb = psb[b]

        def trk(e, g=g, pb=pb):
            ins = None
            for ii in range(4):
                for th in range(2):
                    k = ii * 2 + th
                    ins = e.transpose(pb[:, k * 128:(k + 1) * 128], ktT[:, th, tl(g * 4 + ii)], ident)
            return ins
        P.pe(trk, r=["ktT", "ident"], w=[f"ps{b}"])
        P.act(lambda e, g=g, pb=pb: e.activation(out=ktok[:, g * 4:(g + 1) * 4, :], in_=pb.rearrange("p (a b) -> p a b", a=4), func=AF.Copy),
              r=[f"ps{b}"], w=["ktok"])
    def d0_step(c):
        pk = 6 + c % 2

        def kv_mm(e, c=c, pk=pk):
            ins = None
            for th in range(2):
                ins = e.matmul(ps[pk][:, th * 256:(th + 1) * 256], ktok[:, c, th * 128:(th + 1) * 128],
                               vg[:, c, th * 256:(th + 1) * 256], start=True, stop=True)
            return ins
        P.pe(kv_mm, r=["ktok", "vg"], w=[f"ps{pk}"])
        for th in range(2):
            if c == 0:
                P.dve(lambda e, th=th, pk=pk: e.tensor_copy(out=Z[:, th, :], in_=ps[pk][:, th * 256:(th + 1) * 256]), r=[f"ps{pk}"], w=[f"Z{th}"])
            else:
                P.dve(lambda e, th=th, c=c, pk=pk: e.scalar_tensor_tensor(out=Z[:, th, :], in0=Z[:, th, :], scalar=ebl[:, th, c - 1:c],
                                                                          in1=ps[pk][:, th * 256:(th + 1) * 256], op0=ALU.mult, op1=ALU.add),
                      r=[f"ps{pk}", "ebl", f"Z{th}"], w=[f"Z{th}"])
            P.dve(lambda e, th=th, c=c: e.tensor_scalar(out=Sall[:, c, th, :], in0=Z[:, th, :], scalar1=ebl[:, th, c:c + 1], scalar2=None, op0=ALU.mult),
                  r=[f"Z{th}", "ebl"], w=["Sall"])

    P.dma("pool", slab[0], w_in[:, O_GOG:O_GOG + 512].rearrange("(c p) n -> p c n", p=128), r=[], w=["slab0"])
    assert A.top <= A1_BASE, (A.top, A1_BASE)
    for i in range(NT):
        b = 2 + i % 2
        mmg(ps[b], [(uT[:, c, tl(i)], slab[0][:, c, :]) for c in range(8)], r=["slab0", f"uT{i // 4}"], w=[f"ps{b}"])
        P.act(lambda e, b=b: e.activation(out=tmpE[b - 2], in_=ps[b], func=AF.Exp, scale=-1.0), r=[f"ps{b}"], w=[f"tmpE{b - 2}"])
        P.act(lambda e, b=b: e.activation(out=tmpE[b - 2], in_=tmpE[b - 2], func=AF.Ln, bias=one_b[:, 0:1]), r=[f"tmpE{b - 2}", "one_b"], w=[f"tmpE{b - 2}"])
        P.act(lambda e, b=b: e.activation(out=tmpE[b - 2], in_=tmpE[b - 2], func=AF.Exp, scale=-1.0), r=[f"tmpE{b - 2}"], w=[f"tmpE{b - 2}"])
        P.dve(lambda e, b=b, i=i: e.tensor_tensor(out=sog[:, i, :], in0=ps[b], in1=tmpE[b - 2], op=ALU.mult),
              r=[f"ps{b}", f"tmpE{b - 2}"], w=["sog"])
        if i < NT - 1:
            d0_step(i)
    dump("qz", qz, [128, 4, T], BF16)
    dump("ktT", ktT, [128, 2, T], BF16)
    dump("ktok", ktok, [128, NT, 256], BF16)
    dump("vg", vg, [128, NT, 512], BF16)
    dump("sog", sog, [128, NT, 512], BF16)
    dump("ebl", ebl, [128, 2, NT], F32)
    P.barrier()
    A.top = m1
    if stop_after == "a2":
        return finish(nc, st, P, out, dbg_out, A)

    attn_sb = [A.alloc([128, 4, 128], BF16) for _ in range(2)]
    ssg = A.alloc([128, 4 * NT], F32)
    lng = A.alloc([128, 4 * NT], F32)
    rsg = A.alloc([128, 4 * NT], F32)
    sqj2 = A.alloc([128, 4, 128], BF16)
    ogt = [A.alloc([128, 512], BF16) for _ in range(2)]
    d_end = A.top
    A.top = m0
    QT = A.alloc([96, 8, T], BF16)
    KT = A.alloc([96, 8, T], BF16)
    Vt4 = A.alloc([128, NT, 8, 128], BF16)
    mB = A.top
    cqT = A.alloc([128, 3, T], BF16)
    ckvT = A.alloc([128, 2, T], BF16)
    rq = A.alloc([128, NT], F32)
    rkv = A.alloc([128, NT], F32)
    krtok = A.alloc([128, NT, 32], F32)
    cs2 = A.alloc([128, NT, 32], F32)
    sn2 = A.alloc([128, NT, 32], F32)
    mB2 = A.top
    ssq = A.alloc([128, NT], F32)
    ssk = A.alloc([128, 2 * NT], F32)
    sskk = A.alloc([128, NT], F32)
    lq = A.alloc([128, NT], F32)
    lk = A.alloc([128, NT], F32)
    sqj3 = A.alloc([128, 3, 384], BF16)
    posf = A.alloc([128, NT], F32)
    ang = A.alloc([128, NT, 16], F32)
    angl = A.alloc([128, NT, 16], F32)
    ra = A.alloc([128, NT, 16], F32)
    rb = A.alloc([128, NT, 16], F32)
    rbs = A.alloc([128, NT, 16], F32)
    rbc = A.alloc([128, NT, 16], F32)
    kf = A.alloc([128, NT, 16], F32)

    C1 = 6.28125
    C2 = float(2.0 * np.pi - 6.28125)

    def b1_rope_setup():
        P.dma("sp", posi, pos, r=[], w=["posi"])
        P.dve(lambda e: e.tensor_copy(out=posf, in_=posi), r=["posi"], w=["posf"])
        P.dve(lambda e: e.tensor_tensor(out=ang, in0=posf.unsqueeze(2).to_broadcast([128, NT, 16]),
                                        in1=invb[:, 0:16].unsqueeze(1).to_broadcast([128, NT, 16]), op=ALU.mult), r=["posf", "invb"], w=["ang"])
        P.dve(lambda e: e.tensor_tensor(out=angl, in0=posf.unsqueeze(2).to_broadcast([128, NT, 16]),
                                        in1=invb[:, 16:32].unsqueeze(1).to_broadcast([128, NT, 16]), op=ALU.mult), r=["posf", "invb"], w=["angl"])

    def b1_rope_reduce(shift, rout, nm):
        P.dve(lambda e: e.tensor_scalar(out=ra, in0=ang, scalar1=shift, scalar2=None, op0=ALU.add), r=["ang"], w=["ra"])
        P.dve(lambda e: e.tensor_scalar(out=kf, in0=ra, scalar1=1.0 / (2.0 * PI), scalar2=None, op0=ALU.mult), r=["ra"], w=["kf"])
        P.dve(lambda e: e.tensor_copy(out=ki, in_=kf), r=["kf"], w=["ki"])
        P.dve(lambda e: e.tensor_copy(out=kf, in_=ki), r=["ki"], w=["kf"])
        P.dve(lambda e: e.scalar_tensor_tensor(out=rb, in0=kf, scalar=-C1, in1=ra, op0=ALU.mult, op1=ALU.add), r=["kf", "ra"], w=["rb"])
        P.dve(lambda e: e.scalar_tensor_tensor(out=ra, in0=kf, scalar=-C2, in1=rb, op0=ALU.mult, op1=ALU.add), r=["kf", "rb"], w=["ra"])
        P.dve(lambda e: e.tensor_tensor(out=ra, in0=ra, in1=angl, op=ALU.add), r=["ra", "angl"], w=["ra"])
        P.dve(lambda e: e.tensor_scalar(out=kf, in0=ra, scalar1=PI, scalar2=None, op0=ALU.is_gt), r=["ra"], w=["kf"])
        P.dve(lambda e: e.scalar_tensor_tensor(out=rb, in0=kf, scalar=-2.0 * PI, in1=ra, op0=ALU.mult, op1=ALU.add), r=["kf", "ra"], w=["rb"])
        P.dve(lambda e: e.tensor_scalar(out=kf, in0=rb, scalar1=-PI, scalar2=None, op0=ALU.is_lt), r=["rb"], w=["kf"])
        P.dve(lambda e: e.scalar_tensor_tensor(out=ra, in0=kf, scalar=2.0 * PI, in1=rb, op0=ALU.mult, op1=ALU.add), r=["kf", "rb"], w=["ra"])
        P.dve(lambda e: e.tensor_scalar(out=rout, in0=ra, scalar1=3.1415925, scalar2=-3.1415925, op0=ALU.min, op1=ALU.max), r=["ra"], w=[nm])

    def b1_rope_sin():
        P.act(lambda e: e.activation(out=sn2[:, :, 16:32], in_=rbs, func=AF.Sin), r=["rbs"], w=["sin"])
        P.act(lambda e: e.activation(out=cs2[:, :, 0:16], in_=rbc, func=AF.Sin), r=["rbc"], w=["cos"])
        P.act(lambda e: e.activation(out=cs2[:, :, 16:32], in_=rbc, func=AF.Sin), r=["rbc"], w=["cos"])
        P.dve(lambda e: e.tensor_scalar(out=sn2[:, :, 0:16], in0=sn2[:, :, 16:32], scalar1=-1.0, scalar2=None, op0=ALU.mult), r=["sin"], w=["sin"])

    def b1_stats(i):
        b0, b1 = 6, 7
        mmg(ps[b0], [(uT[:, c, tl(i)], slabM[:, c, 0:512]) for c in range(8)], r=["slabM", f"uT{i // 4}"], w=[f"ps{b0}"])
        mmg(ps[b1][:, 0:160], [(uT[:, c, tl(i)], slabM[:, c, 512:672]) for c in range(8)], r=["slabM", f"uT{i // 4}"], w=[f"ps{b1}"])
        P.act(lambda e, i=i, b0=b0: e.activation(out=sqj3[:, 0, :], in_=ps[b0][:, 0:384], func=AF.Square, accum_out=ssq[:, i:i + 1]),
              r=[f"ps{b0}"], w=["sqj3_0", f"ssq{i}"])
        P.act(lambda e, i=i, b0=b0: e.activation(out=sqj3[:, 1, 0:128], in_=ps[b0][:, 384:512], func=AF.Square, accum_out=ssk[:, 2 * i:2 * i + 1]),
              r=[f"ps{b0}"], w=["sqj3_1", f"ssk{i}"])
        P.act(lambda e, i=i, b1=b1: e.activation(out=sqj3[:, 2, 0:128], in_=ps[b1][:, 0:128], func=AF.Square, accum_out=ssk[:, 2 * i + 1:2 * i + 2]),
              r=[f"ps{b1}"], w=["sqj3_2", f"ssk{i}"])
        P.act(lambda e, i=i, b1=b1: e.activation(out=krtok[:, i, :], in_=ps[b1][:, 128:160], func=AF.Copy), r=[f"ps{b1}"], w=["krtok"])

    def b1_fin():
        allss = [f"ssq{i}" for i in range(NT)] + [f"ssk{i}" for i in range(NT)]
        P.dve(lambda e: e.tensor_tensor(out=sskk, in0=ssk.rearrange("p (i two) -> p i two", two=2)[:, :, 0],
                                        in1=ssk.rearrange("p (i two) -> p i two", two=2)[:, :, 1], op=ALU.add), r=allss, w=["sskk"])
        P.act(lambda e: e.activation(out=lq, in_=ssq, func=AF.Ln, scale=1.0 / 384.0, bias=epsb[:, 0:1]), r=allss + ["epsb"], w=["lq"])
        P.act(lambda e: e.activation(out=lk, in_=sskk, func=AF.Ln, scale=1.0 / 256.0, bias=epsb[:, 0:1]), r=["sskk", "epsb"], w=["lk"])
        P.act(lambda e: e.activation(out=rq, in_=lq, func=AF.Exp, scale=-0.5), r=["lq"], w=["rq0"])
        P.dve(lambda e: e.tensor_scalar(out=rq, in0=rq, scalar1=float(96.0 ** -0.5), scalar2=None, op0=ALU.mult), r=["rq0"], w=["rq"])
        P.act(lambda e: e.activation(out=rkv, in_=lk, func=AF.Exp, scale=-0.5), r=["lk"], w=["rkv"])

    def b1_feat(ct, s):
        b = 5
        mmg(ps[b], [(slabM[:, c, ct * 128:(ct + 1) * 128], uT[:, c, sb_(s)]) for c in range(8)], r=["slabM", f"uT{s}"], w=[f"ps{b}"])
        if ct < 3:
            P.dve(lambda e, b=b, ct=ct, s=s: e.tensor_scalar(out=cqT[:, ct, sb_(s)], in0=ps[b], scalar1=ncq[:, ct:ct + 1], scalar2=None, op0=ALU.mult),
                  r=[f"ps{b}", "ncq"], w=["cqT"])
        else:
            P.dve(lambda e, b=b, ct=ct, s=s: e.tensor_scalar(out=ckvT[:, ct - 3, sb_(s)], in0=ps[b], scalar1=nckv[:, ct - 3:ct - 2], scalar2=None, op0=ALU.mult),
                  r=[f"ps{b}", "nckv"], w=["ckvT"])

    b1_items = []
    feats = [(ct, s) for ct in range(5) for s in range(NSB)]
    extra = {1: b1_rope_setup, 3: lambda: b1_rope_reduce(0.0, rbs, "rbs"), 6: lambda: b1_rope_reduce(PI / 2.0, rbc, "rbc"), 9: b1_rope_sin}
    for k in range(NT):
        b1_items.append(lambda k=k: (b1_stats(k), extra[k]() if k in extra else None))
        b1_items.append(lambda k=k: b1_feat(*feats[k]))
    for k in range(NT, len(feats)):
        b1_items.append(lambda k=k: b1_feat(*feats[k]))
    b1_items.append(b1_fin)
    assert d_end <= mB, (d_end, mB)

    def stage_a(c):
        pa = c % 2

        def attn_mm(e, c=c, pa=pa):
            ins = None
            for h in range(4):
                ins = e.matmul(ps[pa][:, h * 128:(h + 1) * 128], ktT[:, h // 2, tl(c)], qz[:, h, tl(c)], start=True, stop=True)
            return ins
        P.pe(attn_mm, r=["qz", "ktT"], w=[f"ps{pa}"])
        P.dve(lambda e, pa=pa: e.tensor_tensor(out=attn_sb[pa], in0=ps[pa].rearrange("p (h t) -> p h t", h=4),
                                               in1=tri.unsqueeze(1).to_broadcast([128, 4, 128]), op=ALU.mult),
              r=[f"ps{pa}", "tri"], w=[f"attn{pa}"])

    def stage_b(c):
        po = 2 + c % 2
        ab = c % 2

        def o_mm(e, c=c, po=po, ab=ab):
            ins = None
            for h in range(4):
                if c > 0:
                    e.matmul(ps[po][:, h * 128:(h + 1) * 128], qz[:, h, tl(c)], Sall[:, c - 1, h // 2, (h % 2) * 128:(h % 2 + 1) * 128],
                             start=True, stop=False)
                ins = e.matmul(ps[po][:, h * 128:(h + 1) * 128], attn_sb[ab][:, h, :], vg[:, c, h * 128:(h + 1) * 128],
                               start=(c == 0), stop=True)
            return ins
        P.pe(o_mm, r=[f"attn{ab}", "vg", "qz", "Sall"], w=[f"ps{po}"])
        for h in range(4):
            P.act(lambda e, h=h, c=c, po=po: e.activation(out=sqj2[:, h, :], in_=ps[po][:, h * 128:(h + 1) * 128], func=AF.Square,
                                                          accum_out=ssg[:, c * 4 + h:c * 4 + h + 1]),
                  r=[f"ps{po}"], w=[f"sqj2_{h}", f"ssg{c}"])
        P.act(lambda e, c=c: e.activation(out=lng[:, c * 4:c * 4 + 4], in_=ssg[:, c * 4:c * 4 + 4], func=AF.Ln, scale=1.0 / 128.0, bias=epsb[:, 0:1]),
              r=[f"ssg{c}", "epsb"], w=[f"lng{c}"])
        P.act(lambda e, c=c: e.activation(out=rsg[:, c * 4:c * 4 + 4], in_=lng[:, c * 4:c * 4 + 4], func=AF.Exp, scale=-0.5),
              r=[f"lng{c}"], w=[f"rsg{c}"])
        for h in range(4):
            P.dve(lambda e, h=h, c=c, po=po, ab=ab: e.scalar_tensor_tensor(out=ogt[ab][:, h * 128:(h + 1) * 128], in0=ps[po][:, h * 128:(h + 1) * 128],
                                                                           scalar=rsg[:, c * 4 + h:c * 4 + h + 1], in1=sog[:, c, h * 128:(h + 1) * 128],
                                                                           op0=ALU.mult, op1=ALU.mult),
                  r=[f"ps{po}", f"rsg{c}", "sog"], w=[f"ogt{ab}"])

    def stage_c(c):
        ab = c % 2
        pt = 4
        pbt = psb[pt]

        def tro(e, ab=ab, pbt=pbt):
            ins = None
            for h in range(4):
                ins = e.transpose(pbt[:, h * 128:(h + 1) * 128], ogt[ab][:, h * 128:(h + 1) * 128], ident)
            return ins
        P.pe(tro, r=[f"ogt{ab}", "ident"], w=[f"ps{pt}"])
        P.act(lambda e, c=c, pbt=pbt: e.activation(out=OGT[:, :, tl(c)], in_=pbt[:, 0:512].rearrange("p (h t) -> p h t", h=4), func=AF.Copy,
                                                   scale=gnc[:, 0:1]),
              r=[f"ps{pt}", "gnc"], w=["OGT"])

    for c in range(NT + 2):
        if c < NT:
            stage_a(c)
        if 1 <= c <= NT:
            stage_b(c - 1)
        if c >= 2:
            stage_c(c - 2)
        for _ in range(B1_PER_STAGE):
            if b1_items:
                b1_items.pop(0)()
    while b1_items:
        b1_items.pop(0)()
    dump("OGT", OGT, [128, 4, T], BF16)

    if stop_after == "gla":
        return finish(nc, st, P, out, dbg_out, A)

    P.barrier()
    A.top = mB2
    qs9 = [A.alloc([128, 9, 32], F32) for _ in range(2)]
    ra9 = [A.alloc([128, 9, 32], F32) for _ in range(2)]
    rb9 = [A.alloc([128, 9, 32], F32) for _ in range(2)]
    qrot = [A.alloc([128, 8, 96], BF16) for _ in range(2)]
    krot = [A.alloc([128, 8, 96], BF16) for _ in range(2)]
    kro = A.alloc([128, 32], BF16)
    P.dve(lambda e: e.memset(Vt4[:, :, 0:8:2, 64:128], 1.0), r=[], w=["Vones"])
    P.dve(lambda e: e.memset(Vt4[:, :, 1:8:2, 0:64], 1.0), r=[], w=["Vones"])
    assert A.top <= A1_BASE, (A.top, A1_BASE)

    def b_proj(i):
        b = i % 2
        mmg(ps[0][:, 0:384], [(cqT[:, c, tl(i)], wuq[:, c, 0:384]) for c in range(3)], r=["cqT", "wuq"], w=["ps0"])
        mmg(ps[1][:, 0:384], [(cqT[:, c, tl(i)], wuq[:, c, 384:768]) for c in range(3)], r=["cqT", "wuq"], w=["ps1"])
        mmg(ps[2], [(ckvT[:, c, tl(i)], wukv[:, c, 0:512]) for c in range(2)], r=["ckvT", "wukv"], w=["ps2"])
        mmg(ps[3], [(ckvT[:, c, tl(i)], wukv[:, c, 512:1024]) for c in range(2)], r=["ckvT", "wukv"], w=["ps3"])
        for half in range(2):
            pq_ = ps[half][:, 0:384].rearrange("p (h d) -> p h d", h=4)
            hs = slice(half * 4, half * 4 + 4)
            P.act(lambda e, i=i, b=b, pq_=pq_, hs=hs: e.activation(out=qrot[b][:, hs, 0:64], in_=pq_[:, :, 0:64], func=AF.Copy, scale=rq[:, i:i + 1]),
                  r=[f"ps{half}", "rq"], w=[f"qrot{b}"])
            P.act(lambda e, i=i, b=b, pq_=pq_, hs=hs: e.activation(out=qs9[b][:, hs, :], in_=pq_[:, :, 64:96], func=AF.Copy, scale=rq[:, i:i + 1]),
                  r=[f"ps{half}", "rq"], w=[f"qs9{b}"])
        for half in range(2):
            pv = ps[2 + half].rearrange("p (h d) -> p h d", h=4)
            hs = slice(half * 4, half * 4 + 4)
            P.act(lambda e, i=i, b=b, pv=pv, hs=hs: e.activation(out=krot[b][:, hs, 0:64], in_=pv[:, :, 0:64], func=AF.Copy, scale=rkv[:, i:i + 1]),
                  r=[f"ps{2 + half}", "rkv"], w=[f"krot{b}"])
            P.act(lambda e, i=i, half=half, pv=pv: e.activation(out=Vt4[:, i, half * 4:half * 4 + 4:2, 0:64],
                                                                in_=pv[:, 0:4:2, 64:128], func=AF.Copy, scale=rkv[:, i:i + 1]),
                  r=[f"ps{2 + half}", "rkv"], w=["Vt"])
            P.act(lambda e, i=i, half=half, pv=pv: e.activation(out=Vt4[:, i, half * 4 + 1:half * 4 + 4:2, 64:128],
                                                                in_=pv[:, 1:4:2, 64:128], func=AF.Copy, scale=rkv[:, i:i + 1]),
                  r=[f"ps{2 + half}", "rkv"], w=["Vt"])
        P.dve(lambda e, i=i, b=b: e.tensor_copy(out=qs9[b][:, 8, :], in_=krtok[:, i, :]), r=["krtok"], w=[f"qs9{b}"])
        P.dve(lambda e, i=i, b=b: e.tensor_tensor(out=ra9[b], in0=qs9[b], in1=cs2[:, i, :].unsqueeze(1).to_broadcast([128, 9, 32]), op=ALU.mult),
              r=[f"qs9{b}", "cos"], w=[f"ra9{b}"])
        P.dve(lambda e, i=i, b=b: e.tensor_tensor(out=rb9[b][:, :, 0:16], in0=qs9[b][:, :, 16:32],
                                                  in1=sn2[:, i, 0:16].unsqueeze(1).to_broadcast([128, 9, 16]), op=ALU.mult),
              r=[f"qs9{b}", "sin"], w=[f"rb9{b}"])
        P.dve(lambda e, i=i, b=b: e.tensor_tensor(out=rb9[b][:, :, 16:32], in0=qs9[b][:, :, 0:16],
                                                  in1=sn2[:, i, 16:32].unsqueeze(1).to_broadcast([128, 9, 16]), op=ALU.mult),
              r=[f"qs9{b}", "sin"], w=[f"rb9{b}"])
        P.dve(lambda e, b=b: e.tensor_tensor(out=qrot[b][:, :, 64:96], in0=ra9[b][:, 0:8, :], in1=rb9[b][:, 0:8, :], op=ALU.add),
              r=[f"ra9{b}", f"rb9{b}"], w=[f"qrot{b}"])
        P.dve(lambda e, b=b: e.tensor_tensor(out=kro, in0=ra9[b][:, 8, :], in1=rb9[b][:, 8, :], op=ALU.add), r=[f"ra9{b}", f"rb9{b}"], w=["kro"])
        P.dve(lambda e, b=b: e.tensor_copy(out=krot[b][:, :, 64:96], in_=kro.unsqueeze(1).to_broadcast([128, 8, 32])), r=["kro"], w=[f"krot{b}"])

    def b_trans(i):
        b = i % 2
        pq = psb[4 + b]

        def trq(e, b=b, pq=pq):
            ins = None
            for h in range(8):
                ins = e.transpose(pq[0:96, h * 128:(h + 1) * 128], qrot[b][:, h, :], ident)
            return ins
        P.pe(trq, r=[f"qrot{b}", "ident"], w=[f"ps{4 + b}"])
        P.dve(lambda e, i=i, pq=pq: e.tensor_copy(out=QT[:, :, tl(i)], in_=pq[0:96, :].rearrange("p (h t) -> p h t", h=8)),
              r=[f"ps{4 + b}"], w=["QT"])
        pk = psb[6 + b]

        def trk2(e, b=b, pk=pk):
            ins = None
            for h in range(8):
                ins = e.transpose(pk[0:96, h * 128:(h + 1) * 128], krot[b][:, h, :], ident)
            return ins
        P.pe(trk2, r=[f"krot{b}", "ident"], w=[f"ps{6 + b}"])
        P.dve(lambda e, i=i, pk=pk: e.tensor_copy(out=KT[:, :, tl(i)], in_=pk[0:96, :].rearrange("p (h t) -> p h t", h=8)),
              r=[f"ps{6 + b}"], w=["KT"])

    for i in range(NT + 1):
        if i < NT:
            b_proj(i)
        if i >= 1:
            b_trans(i - 1)
    dump("QT", QT, [96, 8, T], BF16)
    dump("KT", KT, [96, 8, T], BF16)
    dump("Vt", Vt4, [128, NT, 8, 128], BF16)
    P.barrier()
    A.top = mB

    OT = A.alloc([128, 4, T], BF16)
    NSC = 6
    PT = [A.alloc([128, 512], BF16) for _ in range(NSC)]
    rinv = A.alloc([128, 512], F32)
    lnl = A.alloc([128, 512], F32)
    woa = A.alloc([128, 4, 1024], BF16)
    wob = A.alloc([128, 4, 1024], BF16)
    gsl = [A.alloc([128, 8, 256], BF16) for _ in range(2)]
    off_e1w = A.top
    P.dma("pool", woa, w_oa.rearrange("(c p) n -> p c n", p=128), r=[], w=["woa"])
    P.dma("pool", wob, w_ob.rearrange("(c p) n -> p c n", p=128), r=[], w=["wob"])

    def load_gsl(ft):
        g = ft % 2
        P.dma("pool", gsl[g][:, :, 0:128], w_in[:, O_GA + ft * 128:O_GA + (ft + 1) * 128].rearrange("(c p) n -> p c n", p=128), r=[], w=[f"gslA{g}"])
        P.dma("pool", gsl[g][:, :, 128:256], w_in[:, O_GB + ft * 128:O_GB + (ft + 1) * 128].rearrange("(c p) n -> p c n", p=128), r=[], w=[f"gslB{g}"])
    load_gsl(0)
    load_gsl(1)
    for p in range(4):
        for Q in range(NSB):
            nj = 4 * Q + 4
            items = [(hh, j) for hh in range(2) for j in range(nj)]
            obank = (6, 7)

            def geom(j, Q=Q):
                m = j - 4 * Q if j >= 4 * Q else 0
                c0 = m * 128
                return c0, 512 - c0

            def emit_qk(k, p=p, Q=Q, items=items):
                hh, j = items[k]
                h = 2 * p + hh
                c0, N = geom(j)
                sbk = k % NSC
                mmg(ps[sbk][:, 0:N], [(KT[:, h, tl(j)], QT[:, h, Q * 512 + c0:(Q + 1) * 512])], r=["KT", "QT"], w=[f"ps{sbk}"])
                P.act(lambda e, sbk=sbk, N=N: e.activation(out=PT[sbk][:, 0:N], in_=ps[sbk][:, 0:N], func=AF.Exp), r=[f"ps{sbk}"], w=[f"PT{sbk}"])
                if j >= 4 * Q:
                    P.dve(lambda e, sbk=sbk: e.tensor_tensor(out=PT[sbk][:, 0:128], in0=PT[sbk][:, 0:128], in1=tri, op=ALU.mult),
                          r=[f"PT{sbk}", "tri"], w=[f"PT{sbk}"])

            def emit_pv(k, p=p, Q=Q, items=items, nj=nj, obank=obank):
                hh, j = items[k]
                h = 2 * p + hh
                c0, N = geom(j)
                sbk = k % NSC
                bk = obank[hh]
                P.pe(lambda e, bk=bk, c0=c0, N=N, h=h, sbk=sbk, j=j: e.matmul(ps[bk][:, c0:512], Vt4[:, j, h, :], PT[sbk][:, 0:N],
                                                                            start=(j == 0), stop=(j == nj - 1)),
                     r=[f"PT{sbk}", "Vt", "Vones"], w=[f"ps{bk}"])
            n = len(items)
            LA = NSC - 1
            for k in range(n + LA):
                if k < n:
                    emit_qk(k)
                if k >= LA:
                    emit_pv(k - LA)
            for hh in range(2):
                bk = obank[hh]
                lr = slice(64, 128) if hh == 0 else slice(0, 64)
                orow = slice(0, 64) if hh == 0 else slice(64, 128)
                P.act(lambda e, bk=bk, lr=lr: e.activation(out=lnl[lr, :], in_=ps[bk][lr, :], func=AF.Ln), r=[f"ps{bk}"], w=[f"lnl{hh}"])
                P.act(lambda e, lr=lr: e.activation(out=lnl[lr, :], in_=lnl[lr, :], func=AF.Exp, scale=-1.0), r=[f"lnl{hh}"], w=[f"lnl{hh}"])
                P.dve(lambda e, lr=lr, orow=orow: e.tensor_copy(out=rinv[orow, :], in_=lnl[lr, :]), r=[f"lnl{hh}"], w=[f"rinv{hh}"])
                P.dve(lambda e, p=p, Q=Q, bk=bk, orow=orow: e.tensor_tensor(out=OT[orow, p, sb_(Q)], in0=ps[bk][orow, :], in1=rinv[orow, :], op=ALU.mult),
                      r=[f"ps{bk}", f"rinv{hh}"], w=["OT"])
    dump("OT", OT, [128, 4, T], BF16)
    if stop_after == "attn":
        return finish(nc, st, P, out, dbg_out, A)
    P.barrier(keep=("woa", "wob", "gslA0", "gslB0", "gslA1", "gslB1"))

    A.top = m0
    mixT = A.alloc([128, 8, T], BF16)
    off_after_mix = A.top
    assert A.top <= mB, (A.top, mB)
    off_e1t = A.top
    ea = [A.alloc([128, 512], F32) for _ in range(2)]
    eb2 = [A.alloc([128, 512], F32) for _ in range(2)]
    tA = A.alloc([128, 512], F32)
    tB = A.alloc([128, 512], F32)
    off_wout = (A.top + 63) // 64 * 64
    wout = A.alloc([128, 8, 1024], BF16)
    P.dma("pool", wout, w_out.rearrange("(c p) n -> p c n", p=128), r=[], w=["wout"])
    for ft in range(8):
        g = ft % 2
        if ft >= 2:
            load_gsl(ft)
        for s in range(NSB):
            q = (ft * NSB + s) % 2
            ba, bb, bya, byb = (0, 1, 2, 3) if q == 0 else (4, 5, 6, 7)
            mmg(ps[ba], [(gsl[g][:, c, 0:128], uT[:, c, sb_(s)]) for c in range(8)], r=[f"gslA{g}", f"uT{s}"], w=[f"ps{ba}"])
            mmg(ps[bb], [(gsl[g][:, c, 128:256], uT[:, c, sb_(s)]) for c in range(8)], r=[f"gslB{g}", f"uT{s}"], w=[f"ps{bb}"])
            mmg(ps[bya], [(woa[:, pp, tl(ft)], OT[:, pp, sb_(s)]) for pp in range(4)], r=["woa", "OT"], w=[f"ps{bya}"])
            mmg(ps[byb], [(wob[:, pp, tl(ft)], OGT[:, pp, sb_(s)]) for pp in range(4)], r=["wob", "OGT"], w=[f"ps{byb}"])
            for (bk, et, nm) in ((ba, ea[q], f"ea{q}"), (bb, eb2[q], f"eb{q}")):
                P.act(lambda e, bk=bk, et=et: e.activation(out=et, in_=ps[bk], func=AF.Exp, scale=-1.0), r=[f"ps{bk}"], w=[nm])
                P.act(lambda e, et=et: e.activation(out=et, in_=et, func=AF.Ln, bias=one_b[:, 0:1]), r=[nm, "one_b"], w=[nm])
                P.act(lambda e, et=et: e.activation(out=et, in_=et, func=AF.Exp, scale=-1.0), r=[nm], w=[nm])
            P.dve(lambda e, q=q, bya=bya: e.tensor_tensor(out=tA, in0=ps[bya], in1=ea[q], op=ALU.mult), r=[f"ps{bya}", f"ea{q}"], w=["tA"])
            P.dve(lambda e, q=q, byb=byb: e.tensor_tensor(out=tB, in0=ps[byb], in1=eb2[q], op=ALU.mult), r=[f"ps{byb}", f"eb{q}"], w=["tB"])
            P.dve(lambda e, ft=ft, s=s: e.tensor_tensor(out=mixT[:, ft, sb_(s)], in0=tA, in1=tB, op=ALU.add), r=["tA", "tB"], w=["mixT"])
    dump("mixT", mixT, [128, 8, T], BF16)
    if stop_after == "mix":
        return finish(nc, st, P, out, dbg_out, A)
    P.barrier(keep=("wout",))

    h1 = A.alloc([128, NT, D], F32)
    off_after_h1 = A.top
    fsl = [A.alloc([128, 8, 256], BF16) for _ in range(2)]
    fnb = A.alloc([128, D], F32)
    ot = [A.alloc([128, D], F32) for _ in range(2)]
    ssf = A.alloc([128, NT], F32)
    lnf = A.alloc([128, NT], F32)
    rsf = A.alloc([128, NT], F32)
    sqf = A.alloc([128, D], BF16)
    off_e2t = A.top
    A.top = off_e1t
    xt2 = [A.alloc([128, D], F32) for _ in range(2)]
    assert A.top <= off_wout

    def load_fsl(f):
        g = f % 2
        P.dma("pool", fsl[g][:, :, 0:128], w_fg[:, f * 128:(f + 1) * 128].rearrange("(c p) n -> p c n", p=128), r=[], w=[f"fslG{g}"])
        P.dma("pool", fsl[g][:, :, 128:256], w_fu[:, f * 128:(f + 1) * 128].rearrange("(c p) n -> p c n", p=128), r=[], w=[f"fslU{g}"])
    load_fsl(0)
    load_fsl(1)
    P.dma("pool", fnb, v_fn, r=[], w=["fnb"])
    u2T = uT
    sv_top = A.top
    A.top = off_e2t
    nt_v = make_nt(u2T, lnffn, "v", NT, banks=(6, 7))
    off_e2t = A.top
    A.top = sv_top
    for i in range(NT):
        b = i % 2
        P.dma("sp", xt2[b], x[tl(i), :], r=[], w=[f"xt2{b}"])
        for half in range(2):
            bk = 2 * b + half
            mmg(ps[bk], [(mixT[:, ft, tl(i)], wout[:, ft, half * 512:(half + 1) * 512]) for ft in range(8)], r=["mixT", "wout"], w=[f"ps{bk}"])
            P.dve(lambda e, i=i, b=b, half=half, bk=bk: e.tensor_tensor(out=h1[:, i, half * 512:(half + 1) * 512], in0=ps[bk],
                                                                       in1=xt2[b][:, half * 512:(half + 1) * 512], op=ALU.add),
                  r=[f"ps{bk}", f"xt2{b}"], w=[f"h1_{i}"])
            if half == 0:
                if i >= 1:
                    nt_v.part1(i - 1, h1[:, i - 1, :], [f"h1_{i - 1}"])
                if i >= 2:
                    nt_v.part2(i - 2)
    nt_v.part2(NT - 2)
    nt_v.part1(NT - 1, h1[:, NT - 1, :], [f"h1_{NT - 1}"])
    dump("h1", h1, [128, NT, D], F32)

    A.top = off_OGT
    wd = A.alloc([128, 6, 1024], BF16)
    A.top = m0
    aT = A.alloc([128, 6, T], BF16)
    tg = [A.alloc([128, 512], F32) for _ in range(2)]
    tt = [A.alloc([128, 512], F32) for _ in range(2)]
    assert A.top <= off_after_mix
    A.top = off_e2t

    def final_tile(i):
        b = i % 2
        P.act(lambda e, i=i: e.activation(out=sqf, in_=h1[:, i, :], func=AF.Square, accum_out=ssf[:, i:i + 1]), r=[f"h1_{i}"], w=["sqf", f"ssf{i}"])
        P.act(lambda e, i=i: e.activation(out=lnf[:, i:i + 1], in_=ssf[:, i:i + 1], func=AF.Ln, scale=1.0 / D, bias=epsb[:, 0:1]),
              r=[f"ssf{i}", "epsb"], w=[f"lnf{i}"])
        P.act(lambda e, i=i: e.activation(out=rsf[:, i:i + 1], in_=lnf[:, i:i + 1], func=AF.Exp, scale=-0.5), r=[f"lnf{i}"], w=[f"rsf{i}"])
        P.dve(lambda e, i=i, b=b: e.scalar_tensor_tensor(out=ot[b], in0=h1[:, i, :], scalar=rsf[:, i:i + 1], in1=fnb, op0=ALU.mult, op1=ALU.mult),
              r=[f"h1_{i}", f"rsf{i}", "fnb"], w=[f"ot{b}"])
        P.dma("sp", out[tl(i), :], ot[b], r=[f"ot{b}"], w=[f"out{i}"])

    groups = [(0, 5), (5, 10), (10, 16), (16, 22)]
    for (f0, f1) in groups:
        nf = f1 - f0
        for f in range(f0, f1):
            g = f % 2
            if f >= 2:
                load_fsl(f)
            if f == f0 + 1:
                P.dma("pool", wd[:, 0:nf, :], w_fd[f0 * 128:f1 * 128, :].rearrange("(f p) n -> p f n", p=128), r=[], w=["wd"])
            for s in range(NSB):
                q = s % 2
                bg_, bu_ = (0, 1) if q == 0 else (2, 3)
                alias0 = ["mixT"] if (f == 0 and s == 0) else []
                mmg(ps[bg_], [(fsl[g][:, c, 0:128], u2T[:, c, sb_(s)]) for c in range(8)], r=[f"fslG{g}", f"vT{s}"], w=[f"ps{bg_}"])
                mmg(ps[bu_], [(fsl[g][:, c, 128:256], u2T[:, c, sb_(s)]) for c in range(8)], r=[f"fslU{g}", f"vT{s}"], w=[f"ps{bu_}"])
                P.act(lambda e, q=q, bg_=bg_: e.activation(out=tg[q], in_=ps[bg_], func=AF.Exp, scale=-1.0), r=[f"ps{bg_}"], w=[f"tg{q}"] + alias0)
                P.act(lambda e, q=q: e.activation(out=tg[q], in_=tg[q], func=AF.Ln, bias=one_b[:, 0:1]), r=[f"tg{q}", "one_b"], w=[f"tg{q}"])
                P.act(lambda e, q=q: e.activation(out=tg[q], in_=tg[q], func=AF.Exp, scale=-1.0), r=[f"tg{q}"], w=[f"tg{q}"])
                P.dve(lambda e, q=q, bg_=bg_: e.tensor_tensor(out=tt[q], in0=ps[bg_], in1=tg[q], op=ALU.mult), r=[f"ps{bg_}", f"tg{q}"], w=[f"tt{q}"] + alias0)
                P.dve(lambda e, q=q, bu_=bu_, f=f, f0=f0, s=s: e.tensor_tensor(out=aT[:, f - f0, sb_(s)], in0=ps[bu_], in1=tt[q], op=ALU.mult),
                      r=[f"ps{bu_}", f"tt{q}"], w=[f"aT{s}"])
                if f == 0 and s == 2:
                    nt_v.part2(NT - 1)
        for i in range(NT):
            for half in range(2):
                bk = 4 + (2 * i + half) % 4
                mmg(ps[bk], [(aT[:, fl, tl(i)], wd[:, fl, half * 512:(half + 1) * 512]) for fl in range(nf)], r=[f"aT{i // 4}", "wd"], w=[f"ps{bk}"])
                P.dve(lambda e, i=i, half=half, bk=bk: e.tensor_tensor(out=h1[:, i, half * 512:(half + 1) * 512], in0=ps[bk],
                                                                      in1=h1[:, i, half * 512:(half + 1) * 512], op=ALU.add),
                      r=[f"ps{bk}", f"h1_{i}"], w=[f"h1_{i}"])
            if f1 == NF and i >= 1:
                final_tile(i - 1)
    final_tile(NT - 1)
    return finish(nc, st, P, out, dbg_out, A)


def finish(nc, st, P, out, dbg_out, A):
    P.barrier()
    P.emit(nc, st)
    st.close()
    return nc, dbg_out


def host_consts():
    half = 16
    inv64 = 1.0 / (10000.0 ** (np.arange(half, dtype=np.float64) / half))
    hi = inv64.astype(np.float32)
    lo = (inv64 - hi.astype(np.float64)).astype(np.float32)
    c = {}
    c["c_inv"] = np.ascontiguousarray(np.broadcast_to(np.concatenate([hi, lo])[None, :], (128, 32))).astype(np.float32)
    c["c_ident"] = np.eye(128, dtype=np.float32).astype(ml_dtypes.bfloat16)
    j = np.arange(128)[:, None]
    i = np.arange(128)[None, :]
    c["c_tri"] = (i >= j).astype(np.float32).astype(ml_dtypes.bfloat16)
    return c


def pcol(v, n):
    return np.ascontiguousarray(np.asarray(v, dtype=np.float32).reshape(n, 128).T)


def shared_map(inp):
    f = lambda a: np.ascontiguousarray(np.asarray(a, dtype=np.float32))
    m = dict(
        w_in=f(inp["w_in"][0]), w_uq=f(inp["mla_w_uq"][0]), w_ukv=f(inp["mla_w_ukv"][0]), w_oa=f(inp["mla_w_o"][0]),
        w_g2=f(inp["gla_w_gate2"][0]), w_ob=f(inp["gla_w_o"][0]), w_out=f(inp["w_out"][0]),
        w_fg=f(inp["ffn_w_gate"][0]), w_fu=f(inp["ffn_w_up"][0]), w_fd=f(inp["ffn_w_down"][0]),
        v_lnmix=pcol(inp["ln_mix"][0], 8), v_lnffn=pcol(inp["ln_ffn"][0], 8), v_ncq=pcol(inp["mla_norm_cq"][0], 3),
        v_nckv=pcol(inp["mla_norm_ckv"][0], 2), v_bg=pcol(inp["gla_b_gate"][0], 2),
        v_gn=np.ascontiguousarray(np.broadcast_to(np.asarray(inp["gla_norm"][0], dtype=np.float32)[None, :], (128, 128))),
        v_gnc=pcol(inp["gla_norm"][0], 1),
        v_fn=np.ascontiguousarray(np.broadcast_to(np.asarray(inp["final_norm"], dtype=np.float32)[None, :], (128, D))),
    )
    m.update(host_consts())
    return m


def core_map(inp, b, shared):
    m = dict(shared)
    m["x"] = np.ascontiguousarray(np.asarray(inp["x"][b], dtype=np.float32))
    m["pos"] = np.ascontiguousarray(np.asarray(inp["positions"][b], dtype=np.int32).reshape(NT, 128).T)
    return m


_CACHE = {}


def kernel(**inputs):
    if "nc" not in _CACHE:
        _CACHE["nc"] = build()[0]
    nc = _CACHE["nc"]
    shared = shared_map(inputs)
    B = np.asarray(inputs["x"]).shape[0]
    in_maps = [core_map(inputs, b, shared) for b in range(B)]
    res = run_bass_kernel_spmd(nc, in_maps, core_ids=list(range(B)))
    return np.stack([np.asarray(r["out"], dtype=np.float32) for r in res.results], axis=0)
```

```python
import contextlib
import numpy as np
import ml_dtypes
import concourse.bass as bass
import concourse.mybir as mybir
from concourse.bass_utils import run_bass_kernel_spmd

F32 = mybir.dt.float32
BF16 = mybir.dt.bfloat16
I32 = mybir.dt.int32
U8 = mybir.dt.uint8
AF = mybir.ActivationFunctionType
ALU = mybir.AluOpType

T = 2048
D = 1024
NT = 16
NSB = 4
DFF = 2816
NF = 22
DIN = 4272
O_CQ, O_CKV, O_KR, O_GQ, O_GK, O_GV, O_GLR, O_GOG, O_GA, O_GB = 0, 384, 640, 672, 928, 1184, 1696, 1712, 2224, 3248
EPS = 1e-6
PI = float(np.pi)
NSLOT = 8
B1_PER_STAGE = 2
ENGS = ("pe", "act", "dve", "pool", "sp")


_DISJOINT = ("mixT", "qz", "ktT", "vg", "sog", "ktok", "Vt", "QT", "KT", "OT", "OGT", "cqT", "ckvT", "krtok", "glrT", "Sall",
             "ebl", "uT", "vT", "ssg", "ssk", "aT")


def _disjoint(k):
    return k.rstrip("0123456789_") in _DISJOINT


class Prog:
    def __init__(self):
        self.ops = []
        self.lw = {}
        self.rd = {}
        self.last = {}
        self.dmas_since_barrier = []

    def add(self, eng, fn, r=(), w=(), dma=False):
        idx = len(self.ops)
        deps = set()
        for k in r:
            p = self.lw.get(k)
            if p is not None:
                deps.add(p)
        for k in w:
            p = self.lw.get(k)
            if p is not None and not (_disjoint(k) and not dma and not self.ops[p]["dma"] and self.ops[p]["eng"] == eng):
                deps.add(p)
            for q in self.rd.get(k, {}).values():
                if isinstance(q, list):
                    deps.update(q)
                else:
                    deps.add(q)
        for k in r:
            d = self.rd.setdefault(k, {})
            if dma:
                d.setdefault("dma", []).append(idx)
            else:
                d[eng] = idx
        for k in w:
            self.lw[k] = idx
            self.rd[k] = {}
        deps.discard(idx)
        self.ops.append(dict(eng=eng, fn=fn, deps=deps, dma=dma, sig=False))
        if dma:
            self.dmas_since_barrier.append(idx)
        else:
            self.last[eng] = idx
        return idx

    def pe(self, fn, r=(), w=()):
        return self.add("pe", fn, r, w)

    def act(self, fn, r=(), w=()):
        return self.add("act", fn, r, w)

    def dve(self, fn, r=(), w=()):
        return self.add("dve", fn, r, w)

    def pool(self, fn, r=(), w=()):
        return self.add("pool", fn, r, w)

    def dma(self, q, out, in_, r=(), w=()):
        return self.add(q, lambda e: e.dma_start(out=out, in_=in_), r, w, dma=True)

    def barrier(self, keep=()):
        kept = {self.lw[k] for k in keep if k in self.lw and self.ops[self.lw[k]]["dma"]}
        deps = set(self.last.values()) | (set(self.dmas_since_barrier) - kept)
        for eng in ENGS:
            self.ops.append(dict(eng=eng, fn=None, deps=set(deps), dma=False, sig=False))
        self.last = {}
        self.lw = {k: v for k, v in self.lw.items() if v in kept}
        self.rd = {}
        self.dmas_since_barrier = sorted(kept)

    def emit(self, nc, st):
        ops = self.ops
        for op in ops:
            for d in op["deps"]:
                p = ops[d]
                if p["dma"]:
                    continue
                if p["eng"] == "pe" and op["eng"] == "pe" and not op["dma"]:
                    continue
                p["sig"] = True
        cnt = {e: 0 for e in ENGS}
        dcnt = {"pool": 0, "sp": 0}
        for op in ops:
            if op["dma"]:
                k = dcnt[op["eng"]]
                dcnt[op["eng"]] += 1
                op["slot"] = k % NSLOT
                op["val"] = 16 * (k // NSLOT + 1)
            elif op["sig"]:
                cnt[op["eng"]] += 1
                op["cnt"] = cnt[op["eng"]]
        sem = {e: st.enter_context(nc.semaphore("s_" + e)) for e in ENGS}
        dsem = {q: [st.enter_context(nc.semaphore(f"d_{q}{i}")) for i in range(NSLOT)] for q in ("pool", "sp")}
        per = {e: [] for e in ENGS}
        for i, op in enumerate(ops):
            per[op["eng"]].append(i)

        def run(engname, e):
            seen = {}
            for i in per[engname]:
                op = ops[i]
                waits = {}
                for d in op["deps"]:
                    p = ops[d]
                    if p["dma"]:
                        key = ("d", p["eng"], p["slot"])
                        v = p["val"]
                    else:
                        if p["eng"] == "pe" and engname == "pe" and not op["dma"]:
                            continue
                        key = ("c", p["eng"])
                        v = p["cnt"]
                    if v > waits.get(key, 0):
                        waits[key] = v
                if op["dma"] and op["val"] > 16:
                    key = ("d", engname, op["slot"])
                    waits[key] = max(waits.get(key, 0), op["val"] - 16)
                for key, v in waits.items():
                    if seen.get(key, 0) >= v:
                        continue
                    seen[key] = v
                    s = sem[key[1]] if key[0] == "c" else dsem[key[1]][key[2]]
                    e.wait_ge(s, v)
                if op["fn"] is None:
                    continue
                ins = op["fn"](e)
                if op["dma"]:
                    ins.then_inc(dsem[engname][op["slot"]], 16)
                elif op["sig"]:
                    ins.then_inc(sem[engname], 1)

        block = st.enter_context(nc.Block())

        @block.tensor
        def _(e):
            run("pe", e)

        @block.scalar
        def _(e):
            run("act", e)

        @block.vector
        def _(e):
            run("dve", e)

        @block.gpsimd
        def _(e):
            run("pool", e)

        @block.sync
        def _(e):
            run("sp", e)


class Arena:
    def __init__(self, nc, nbytes):
        self.t = nc.alloc_sbuf_tensor("arena", [128, nbytes], U8)
        self.n = nbytes
        self.top = 0
        self.peak = 0

    def alloc(self, shape, dt):
        esz = 4 if dt in (F32, I32) else 2
        nb = int(np.prod(shape[1:])) * esz
        off = (self.top + 63) // 64 * 64
        assert off + nb <= self.n, f"SBUF arena overflow {off + nb} > {self.n}"
        self.top = off + nb
        self.peak = max(self.peak, self.top)
        a = self.t[0:shape[0], off:off + nb].bitcast(dt)
        if len(shape) == 3:
            a = a.rearrange("p (a b) -> p a b", a=shape[1])
        elif len(shape) == 4:
            a = a.rearrange("p (a b c) -> p a b c", a=shape[1], b=shape[2])
        return a


def build(debug=(), stop_after=None):
    nc = bass.Bass("TRN2", target_bir_lowering=False)

    def din(name, shape, dt=F32):
        return nc.dram_tensor(name, list(shape), dt, kind="ExternalInput").ap()

    x = din("x", [T, D])
    pos = din("pos", [128, NT], I32)
    w_in = din("w_in", [D, DIN])
    w_uq = din("w_uq", [384, 768])
    w_ukv = din("w_ukv", [256, 1024])
    w_oa = din("w_oa", [512, 1024])
    w_g2 = din("w_g2", [16, 256])
    w_ob = din("w_ob", [512, 1024])
    w_out = din("w_out", [D, D])
    w_fg = din("w_fg", [D, DFF])
    w_fu = din("w_fu", [D, DFF])
    w_fd = din("w_fd", [DFF, D])
    v_lnmix = din("v_lnmix", [128, 8])
    v_lnffn = din("v_lnffn", [128, 8])
    v_ncq = din("v_ncq", [128, 3])
    v_nckv = din("v_nckv", [128, 2])
    v_bg = din("v_bg", [128, 2])
    v_gn = din("v_gn", [128, 128])
    v_gnc = din("v_gnc", [128, 1])
    v_fn = din("v_fn", [128, D])
    c_inv = din("c_inv", [128, 32])
    c_ident = din("c_ident", [128, 128], BF16)
    c_tri = din("c_tri", [128, 128], BF16)
    out = nc.dram_tensor("out", [T, D], F32, kind="ExternalOutput").ap()
    dbg_out = {}

    st = contextlib.ExitStack()
    A = Arena(nc, 212800)
    P = Prog()
    ps = [nc.alloc_psum_tensor(f"ps{i}", [128, 512], F32)[:] for i in range(8)]
    psb = [p.bitcast(BF16) for p in ps]

    def sb_(s):
        return slice(s * 512, (s + 1) * 512)

    def tl(i):
        return slice(i * 128, (i + 1) * 128)

    def mmg(out_ap, pairs, r, w):
        def fn(e):
            n = len(pairs)
            ins = None
            for k, (l, rh) in enumerate(pairs):
                ins = e.matmul(out_ap, l, rh, start=(k == 0), stop=(k == n - 1))
            return ins
        P.pe(fn, r, w)

    def dump(name, ap, shape, dt=F32):
        if name not in debug:
            return
        d = nc.dram_tensor("dbg_" + name, list(shape), dt, kind="ExternalOutput").ap()
        dbg_out[name] = d
        P.barrier()
        P.dma("sp", d, ap, r=[], w=["dbg_" + name])
        P.barrier()

    lnmix = A.alloc([128, 8], F32)
    lnffn = A.alloc([128, 8], F32)
    ncq = A.alloc([128, 3], F32)
    nckv = A.alloc([128, 2], F32)
    bg = A.alloc([128, 2], F32)
    nbg = A.alloc([128, 2], F32)
    gnb = A.alloc([128, 128], F32)
    gnc = A.alloc([128, 1], F32)
    invb = A.alloc([128, 32], F32)
    ident = A.alloc([128, 128], BF16)
    tri = A.alloc([128, 128], BF16)
    ones_bf = A.alloc([128, 64], BF16)
    posi = A.alloc([128, NT], I32)
    ki = A.alloc([128, NT, 16], I32)
    for dst, src, k in ((lnmix, v_lnmix, "lnmix"), (lnffn, v_lnffn, "lnffn"), (ncq, v_ncq, "ncq"),
                        (nckv, v_nckv, "nckv"), (bg, v_bg, "bg"), (gnb, v_gn, "gnb"), (gnc, v_gnc, "gnc"), (invb, c_inv, "invb"),
                        (ident, c_ident, "ident"), (tri, c_tri, "tri")):
        P.dma("pool", dst, src, r=[], w=[k])
    P.dve(lambda e: e.tensor_scalar(out=nbg, in0=bg, scalar1=-1.0, scalar2=None, op0=ALU.mult), r=["bg"], w=["nbg"])
    P.dve(lambda e: e.memset(ones_bf, 1.0), r=[], w=["ones"])

    uT = A.alloc([128, 8, T], BF16)
    off_OGT = (A.top + 63) // 64 * 64
    OGT = A.alloc([128, 4, T], BF16)
    base_mark = A.top

    def make_nt(dstT, gain, tagp, ntiles, banks=(0, 1)):
        xn = [A.alloc([128, D], BF16) for _ in range(2)]
        sqj = A.alloc([128, D], BF16)
        ssq = A.alloc([128, ntiles], F32)
        lnv = A.alloc([128, ntiles], F32)
        rs = A.alloc([128, ntiles], F32)

        def part1(i, src, skeys):
            b = i % 2
            P.act(lambda e, src=src, i=i: e.activation(out=sqj, in_=src, func=AF.Square, accum_out=ssq[:, i:i + 1]),
                  r=skeys, w=[tagp + "sqj", f"{tagp}ss{i}"])
            P.act(lambda e, i=i: e.activation(out=lnv[:, i:i + 1], in_=ssq[:, i:i + 1], func=AF.Ln, scale=1.0 / D, bias=epsb[:, 0:1]),
                  r=[f"{tagp}ss{i}", "epsb"], w=[f"{tagp}ln{i}"])
            P.act(lambda e, i=i: e.activation(out=rs[:, i:i + 1], in_=lnv[:, i:i + 1], func=AF.Exp, scale=-0.5),
                  r=[f"{tagp}ln{i}"], w=[f"{tagp}rs{i}"])
            P.dve(lambda e, src=src, b=b, i=i: e.tensor_scalar(out=xn[b], in0=src, scalar1=rs[:, i:i + 1], scalar2=None, op0=ALU.mult),
                  r=skeys + [f"{tagp}rs{i}"], w=[f"{tagp}xn{b}"])

        def part2(i):
            b = i % 2
            bk = banks[b]
            pb = psb[bk]

            def tr(e, b=b, pb=pb):
                ins = None
                for c in range(8):
                    ins = e.transpose(pb[:, c * 128:(c + 1) * 128], xn[b][:, c * 128:(c + 1) * 128], ident)
                return ins
            P.pe(tr, r=[f"{tagp}xn{b}", "ident"], w=[f"ps{bk}"])
            P.dve(lambda e, pb=pb, i=i: e.tensor_tensor(out=dstT[:, :, tl(i)], in0=pb.rearrange("p (c t) -> p c t", c=8),
                                                        in1=gain[:, 0:8].unsqueeze(2).to_broadcast([128, 8, 128]), op=ALU.mult),
                  r=[f"ps{bk}", "lnmix", "lnffn"], w=[f"{tagp}T{i // 4}"])

        def step(i, src, skeys):
            part1(i, src, skeys)
            part2(i)
        step.part1 = part1
        step.part2 = part2
        return step

    epsb = A.alloc([128, 1], F32)
    P.dve(lambda e: e.memset(epsb, EPS), r=[], w=["epsb"])
    one_b = A.alloc([128, 1], F32)
    P.dve(lambda e: e.memset(one_b, 1.0), r=[], w=["one_b"])
    base_mark = A.top

    A1_BASE = 193000
    A.top = A1_BASE
    xt = [A.alloc([128, D], F32) for _ in range(3)]

    nt_u = make_nt(uT, lnmix, "u", NT)
    for i in range(NT):
        b = i % 3
        P.dma("sp", xt[b], x[tl(i), :], r=[], w=[f"xt{b}"])
        nt_u(i, xt[b], [f"xt{b}"])
    dump("uT", uT, [128, 8, T], BF16)
    A.top = base_mark
    if stop_after == "a1":
        return finish(nc, st, P, out, dbg_out, A)

    m0 = A.top
    qz = A.alloc([128, 4, T], BF16)
    ktT = A.alloc([128, 2, T], BF16)
    ktok = A.alloc([128, NT, 256], BF16)
    vg = A.alloc([128, NT, 512], BF16)
    sog = A.alloc([128, NT, 512], BF16)
    ebl = A.alloc([128, 2, NT], F32)
    Sall = A.alloc([128, NT - 1, 2, 256], BF16)
    Z = A.alloc([128, 2, 256], F32)
    m1 = A.top
    glrT = A.alloc([16, T], BF16)
    wlr = A.alloc([128, 8, 16], BF16)
    wg2 = A.alloc([16, 256], BF16)
    rmask = A.alloc([128, T], BF16)
    bufA = A.alloc([128, T], F32)
    bufB = A.alloc([128, T], F32)
    bufC = A.alloc([128, T], F32)
    tmpE = [A.alloc([128, 512], F32) for _ in range(2)]
    slab = [A.alloc([128, 8, 512], BF16) for _ in range(2)]

    P.dma("pool", wlr, w_in[:, O_GLR:O_GLR + 16].rearrange("(c p) n -> p c n", p=128), r=[], w=["wlr"])
    P.dma("pool", wg2, w_g2, r=[], w=["wg2"])
    P.dma("pool", slab[0], w_in[:, O_GQ:O_GQ + 512].rearrange("(c p) n -> p c n", p=128), r=[], w=["slab0"])
    P.dma("pool", slab[1], w_in[:, O_GV:O_GV + 512].rearrange("(c p) n -> p c n", p=128), r=[], w=["slab1"])
    sv_top = A.top
    A.top = A1_BASE
    slabM = A.alloc([128, 8, 672], BF16)
    wuq = A.alloc([128, 3, 768], BF16)
    wukv = A.alloc([128, 2, 1024], BF16)
    A.top = sv_top
    a1_keys = ["xt0", "xt1", "xt2", "uxn0", "uxn1", "usqj"] + [f"u{t}{i}" for t in ("ss", "ln", "rs") for i in range(NT)]
    P.dma("pool", slabM, w_in[:, 0:672].rearrange("(c p) n -> p c n", p=128), r=[], w=["slabM"] + a1_keys)
    P.dma("pool", wuq, w_uq.rearrange("(c p) n -> p c n", p=128), r=[], w=["wuq"] + a1_keys)
    P.dma("pool", wukv, w_ukv.rearrange("(c p) n -> p c n", p=128), r=[], w=["wukv"] + a1_keys)
    P.dve(lambda e: e.memset(qz, 0.0), r=[], w=["qz"])
    P.dve(lambda e: e.memset(rmask, 1.0), r=[], w=["rmask"])
    P.dve(lambda e: e.memset(rmask.rearrange("p (c t) -> p c t", t=128)[:, :, 0:1], 0.0), r=[], w=["rmask"])
    for s in range(NSB):
        b = s % 2
        mmg(ps[b][0:16, :], [(wlr[:, c, :], uT[:, c, sb_(s)]) for c in range(8)], r=["wlr", f"uT{s}"], w=[f"ps{b}"])
        P.act(lambda e, b=b, s=s: e.activation(out=glrT[:, sb_(s)], in_=ps[b][0:16, :], func=AF.Copy), r=[f"ps{b}"], w=["glrT"])
    def v_tile(i):
        b = 4 + i % 2
        mmg(ps[b], [(uT[:, c, tl(i)], slab[1][:, c, :]) for c in range(8)], r=["slab1", f"uT{i // 4}"], w=[f"ps{b}"])
        P.act(lambda e, b=b, i=i: e.activation(out=vg[:, i, :], in_=ps[b], func=AF.Copy), r=[f"ps{b}"], w=["vg"])

    for th in range(2):
        for s in range(NSB):
            b = s % 2
            mmg(ps[b], [(wg2[:, th * 128:(th + 1) * 128], glrT[:, sb_(s)])], r=["wg2", "glrT"], w=[f"ps{b}"])
            P.act(lambda e, b=b, th=th: e.activation(out=tmpE[b], in_=ps[b], func=AF.Exp, scale=-1.0, bias=nbg[:, th:th + 1]),
                  r=[f"ps{b}", "nbg"], w=[f"tmpE{b}"])
            P.act(lambda e, b=b, s=s: e.activation(out=bufA[:, sb_(s)], in_=tmpE[b], func=AF.Ln, bias=one_b[:, 0:1]),
                  r=[f"tmpE{b}", "one_b"], w=["bufA"])
        for i in range(th * 8, th * 8 + 8):
            v_tile(i)
        P.dve(lambda e: e.tensor_tensor_scan(out=bufB, data0=rmask, data1=bufA, initial=0.0, op0=ALU.mult, op1=ALU.add),
              r=["bufA", "rmask"], w=["bufB"])
        P.act(lambda e: e.activation(out=bufA, in_=bufB, func=AF.Exp, scale=-1.0 / 16.0), r=["bufB"], w=["bufA"])
        P.act(lambda e: e.activation(out=bufC, in_=bufB, func=AF.Exp, scale=1.0 / 16.0), r=["bufB"], w=["bufC"])
        P.dve(lambda e, th=th: e.tensor_copy(out=ebl[:, th, :], in_=bufA.rearrange("p (c t) -> p c t", t=128)[:, :, 127]),
              r=["bufA"], w=["ebl"])
        for s in range(NSB):
            b = s % 2
            mmg(ps[b], [(slab[0][:, c, th * 128:(th + 1) * 128], uT[:, c, sb_(s)]) for c in range(8)], r=["slab0", f"uT{s}"], w=[f"ps{b}"])
            for hh in range(2):
                pr = slice(hh * 64, hh * 64 + 64)
                P.dve(lambda e, b=b, s=s, th=th, hh=hh, pr=pr: e.scalar_tensor_tensor(out=qz[pr, 2 * th + hh, sb_(s)], in0=ps[b][pr, :], scalar=0.125,
                                                                                     in1=bufA[pr, sb_(s)], op0=ALU.mult, op1=ALU.mult),
                      r=[f"ps{b}", "bufA"], w=["qz"])
            b2 = 2 + s % 2
            mmg(ps[b2], [(slab[0][:, c, 256 + th * 128:256 + (th + 1) * 128], uT[:, c, sb_(s)]) for c in range(8)], r=["slab0", f"uT{s}"], w=[f"ps{b2}"])
            P.dve(lambda e, b2=b2, s=s, th=th: e.tensor_tensor(out=ktT[:, th, sb_(s)], in0=ps[b2], in1=bufC[:, sb_(s)], op=ALU.mult),
                  r=[f"ps{b2}", "bufC"], w=["ktT"])
    for g in range(4):
        b = 4 + g % 2
        pb = psb[b]

        def trk(e, g=g, pb=pb):
            ins = None
            for ii in range(4):
                for th in range(2):
                    k = ii * 2 + th
                    ins = e.transpose(pb[:, k * 128:(k + 1) * 128], ktT[:, th, tl(g * 4 + ii)], ident)
            return ins
        P.pe(trk, r=["ktT", "ident"], w=[f"ps{b}"])
        P.act(lambda e, g=g, pb=pb: e.activation(out=ktok[:, g * 4:(g + 1) * 4, :], in_=pb.rearrange("p (a b) -> p a b", a=4), func=AF.Copy),
              r=[f"ps{b}"], w=["ktok"])
    def d0_step(c):
        pk = 6 + c % 2

        def kv_mm(e, c=c, pk=pk):
            ins = None
            for th in range(2):
                ins = e.matmul(ps[pk][:, th * 256:(th + 1) * 256], ktok[:, c, th * 128:(th + 1) * 128],
                               vg[:, c, th * 256:(th + 1) * 256], start=True, stop=True)
            return ins
        P.pe(kv_mm, r=["ktok", "vg"], w=[f"ps{pk}"])
        for th in range(2):
            if c == 0:
                P.dve(lambda e, th=th, pk=pk: e.tensor_copy(out=Z[:, th, :], in_=ps[pk][:, th * 256:(th + 1) * 256]), r=[f"ps{pk}"], w=[f"Z{th}"])
            else:
                P.dve(lambda e, th=th, c=c, pk=pk: e.scalar_tensor_tensor(out=Z[:, th, :], in0=Z[:, th, :], scalar=ebl[:, th, c - 1:c],
                                                                          in1=ps[pk][:, th * 256:(th + 1) * 256], op0=ALU.mult, op1=ALU.add),
                      r=[f"ps{pk}", "ebl", f"Z{th}"], w=[f"Z{th}"])
            P.dve(lambda e, th=th, c=c: e.tensor_scalar(out=Sall[:, c, th, :], in0=Z[:, th, :], scalar1=ebl[:, th, c:c + 1], scalar2=None, op0=ALU.mult),
                  r=[f"Z{th}", "ebl"], w=["Sall"])

    P.dma("pool", slab[1], w_in[:, O_GOG:O_GOG + 512].rearrange("(c p) n -> p c n", p=128), r=[], w=["slab1"])
    assert A.top <= A1_BASE, (A.top, A1_BASE)
    for i in range(NT):
        b = 2 + i % 2
        mmg(ps[b], [(uT[:, c, tl(i)], slab[1][:, c, :]) for c in range(8)], r=["slab1", f"uT{i // 4}"], w=[f"ps{b}"])
        P.act(lambda e, b=b: e.activation(out=tmpE[b - 2], in_=ps[b], func=AF.Exp, scale=-1.0), r=[f"ps{b}"], w=[f"tmpE{b - 2}"])
        P.act(lambda e, b=b: e.activation(out=tmpE[b - 2], in_=tmpE[b - 2], func=AF.Ln, bias=one_b[:, 0:1]), r=[f"tmpE{b - 2}", "one_b"], w=[f"tmpE{b - 2}"])
        P.act(lambda e, b=b: e.activation(out=tmpE[b - 2], in_=tmpE[b - 2], func=AF.Exp, scale=-1.0), r=[f"tmpE{b - 2}"], w=[f"tmpE{b - 2}"])
        P.dve(lambda e, b=b, i=i: e.tensor_tensor(out=sog[:, i, :], in0=ps[b], in1=tmpE[b - 2], op=ALU.mult),
              r=[f"ps{b}", f"tmpE{b - 2}"], w=["sog"])
        if i < NT - 1:
            d0_step(i)
    dump("qz", qz, [128, 4, T], BF16)
    dump("ktT", ktT, [128, 2, T], BF16)
    dump("ktok", ktok, [128, NT, 256], BF16)
    dump("vg", vg, [128, NT, 512], BF16)
    dump("sog", sog, [128, NT, 512], BF16)
    dump("ebl", ebl, [128, 2, NT], F32)
    P.barrier()
    A.top = m1
    if stop_after == "a2":
        return finish(nc, st, P, out, dbg_out, A)

    attn_sb = [A.alloc([128, 4, 128], BF16) for _ in range(2)]
    ssg = A.alloc([128, 4 * NT], F32)
    lng = A.alloc([128, 4 * NT], F32)
    rsg = A.alloc([128, 4 * NT], F32)
    sqj2 = A.alloc([128, 4, 128], BF16)
    ogt = [A.alloc([128, 512], BF16) for _ in range(2)]
    d_end = A.top
    A.top = m0
    QT = A.alloc([96, 8, T], BF16)
    KT = A.alloc([96, 8, T], BF16)
    Vt4 = A.alloc([128, NT, 8, 128], BF16)
    mB = A.top
    cqT = A.alloc([128, 3, T], BF16)
    ckvT = A.alloc([128, 2, T], BF16)
    rq = A.alloc([128, NT], F32)
    rkv = A.alloc([128, NT], F32)
    krtok = A.alloc([128, NT, 32], F32)
    cs2 = A.alloc([128, NT, 32], F32)
    sn2 = A.alloc([128, NT, 32], F32)
    mB2 = A.top
    ssq = A.alloc([128, NT], F32)
    ssk = A.alloc([128, 2 * NT], F32)
    sskk = A.alloc([128, NT], F32)
    lq = A.alloc([128, NT], F32)
    lk = A.alloc([128, NT], F32)
    sqj3 = A.alloc([128, 3, 384], BF16)
    posf = A.alloc([128, NT], F32)
    ang = A.alloc([128, NT, 16], F32)
    angl = A.alloc([128, NT, 16], F32)
    ra = A.alloc([128, NT, 16], F32)
    rb = A.alloc([128, NT, 16], F32)
    rbs = A.alloc([128, NT, 16], F32)
    rbc = A.alloc([128, NT, 16], F32)
    kf = A.alloc([128, NT, 16], F32)

    C1 = 6.28125
    C2 = float(2.0 * np.pi - 6.28125)

    def b1_rope_setup():
        P.dma("sp", posi, pos, r=[], w=["posi"])
        P.dve(lambda e: e.tensor_copy(out=posf, in_=posi), r=["posi"], w=["posf"])
        P.dve(lambda e: e.tensor_tensor(out=ang, in0=posf.unsqueeze(2).to_broadcast([128, NT, 16]),
                                        in1=invb[:, 0:16].unsqueeze(1).to_broadcast([128, NT, 16]), op=ALU.mult), r=["posf", "invb"], w=["ang"])
        P.dve(lambda e: e.tensor_tensor(out=angl, in0=posf.unsqueeze(2).to_broadcast([128, NT, 16]),
                                        in1=invb[:, 16:32].unsqueeze(1).to_broadcast([128, NT, 16]), op=ALU.mult), r=["posf", "invb"], w=["angl"])

    def b1_rope_reduce(shift, rout, nm):
        P.dve(lambda e: e.tensor_scalar(out=ra, in0=ang, scalar1=shift, scalar2=None, op0=ALU.add), r=["ang"], w=["ra"])
        P.dve(lambda e: e.tensor_scalar(out=kf, in0=ra, scalar1=1.0 / (2.0 * PI), scalar2=None, op0=ALU.mult), r=["ra"], w=["kf"])
        P.dve(lambda e: e.tensor_copy(out=ki, in_=kf), r=["kf"], w=["ki"])
        P.dve(lambda e: e.tensor_copy(out=kf, in_=ki), r=["ki"], w=["kf"])
        P.dve(lambda e: e.scalar_tensor_tensor(out=rb, in0=kf, scalar=-C1, in1=ra, op0=ALU.mult, op1=ALU.add), r=["kf", "ra"], w=["rb"])
        P.dve(lambda e: e.scalar_tensor_tensor(out=ra, in0=kf, scalar=-C2, in1=rb, op0=ALU.mult, op1=ALU.add), r=["kf", "rb"], w=["ra"])
        P.dve(lambda e: e.tensor_tensor(out=ra, in0=ra, in1=angl, op=ALU.add), r=["ra", "angl"], w=["ra"])
        P.dve(lambda e: e.tensor_scalar(out=kf, in0=ra, scalar1=PI, scalar2=None, op0=ALU.is_gt), r=["ra"], w=["kf"])
        P.dve(lambda e: e.scalar_tensor_tensor(out=rb, in0=kf, scalar=-2.0 * PI, in1=ra, op0=ALU.mult, op1=ALU.add), r=["kf", "ra"], w=["rb"])
        P.dve(lambda e: e.tensor_scalar(out=kf, in0=rb, scalar1=-PI, scalar2=None, op0=ALU.is_lt), r=["rb"], w=["kf"])
        P.dve(lambda e: e.scalar_tensor_tensor(out=ra, in0=kf, scalar=2.0 * PI, in1=rb, op0=ALU.mult, op1=ALU.add), r=["kf", "rb"], w=["ra"])
        P.dve(lambda e: e.tensor_scalar(out=rout, in0=ra, scalar1=3.1415925, scalar2=-3.1415925, op0=ALU.min, op1=ALU.max), r=["ra"], w=[nm])

    def b1_rope_sin():
        P.act(lambda e: e.activation(out=sn2[:, :, 16:32], in_=rbs, func=AF.Sin), r=["rbs"], w=["sin"])
        P.act(lambda e: e.activation(out=cs2[:, :, 0:16], in_=rbc, func=AF.Sin), r=["rbc"], w=["cos"])
        P.act(lambda e: e.activation(out=cs2[:, :, 16:32], in_=rbc, func=AF.Sin), r=["rbc"], w=["cos"])
        P.dve(lambda e: e.tensor_scalar(out=sn2[:, :, 0:16], in0=sn2[:, :, 16:32], scalar1=-1.0, scalar2=None, op0=ALU.mult), r=["sin"], w=["sin"])

    def b1_stats(i):
        b0, b1 = 6, 7
        mmg(ps[b0], [(uT[:, c, tl(i)], slabM[:, c, 0:512]) for c in range(8)], r=["slabM", f"uT{i // 4}"], w=[f"ps{b0}"])
        mmg(ps[b1][:, 0:160], [(uT[:, c, tl(i)], slabM[:, c, 512:672]) for c in range(8)], r=["slabM", f"uT{i // 4}"], w=[f"ps{b1}"])
        P.act(lambda e, i=i, b0=b0: e.activation(out=sqj3[:, 0, :], in_=ps[b0][:, 0:384], func=AF.Square, accum_out=ssq[:, i:i + 1]),
              r=[f"ps{b0}"], w=["sqj3_0", f"ssq{i}"])
        P.act(lambda e, i=i, b0=b0: e.activation(out=sqj3[:, 1, 0:128], in_=ps[b0][:, 384:512], func=AF.Square, accum_out=ssk[:, 2 * i:2 * i + 1]),
              r=[f"ps{b0}"], w=["sqj3_1", f"ssk{i}"])
        P.act(lambda e, i=i, b1=b1: e.activation(out=sqj3[:, 2, 0:128], in_=ps[b1][:, 0:128], func=AF.Square, accum_out=ssk[:, 2 * i + 1:2 * i + 2]),
              r=[f"ps{b1}"], w=["sqj3_2", f"ssk{i}"])
        P.act(lambda e, i=i, b1=b1: e.activation(out=krtok[:, i, :], in_=ps[b1][:, 128:160], func=AF.Copy), r=[f"ps{b1}"], w=["krtok"])

    def b1_fin():
        allss = [f"ssq{i}" for i in range(NT)] + [f"ssk{i}" for i in range(NT)]
        P.dve(lambda e: e.tensor_tensor(out=sskk, in0=ssk.rearrange("p (i two) -> p i two", two=2)[:, :, 0],
                                        in1=ssk.rearrange("p (i two) -> p i two", two=2)[:, :, 1], op=ALU.add), r=allss, w=["sskk"])
        P.act(lambda e: e.activation(out=lq, in_=ssq, func=AF.Ln, scale=1.0 / 384.0, bias=epsb[:, 0:1]), r=allss + ["epsb"], w=["lq"])
        P.act(lambda e: e.activation(out=lk, in_=sskk, func=AF.Ln, scale=1.0 / 256.0, bias=epsb[:, 0:1]), r=["sskk", "epsb"], w=["lk"])
        P.act(lambda e: e.activation(out=rq, in_=lq, func=AF.Exp, scale=-0.5), r=["lq"], w=["rq0"])
        P.dve(lambda e: e.tensor_scalar(out=rq, in0=rq, scalar1=float(96.0 ** -0.5), scalar2=None, op0=ALU.mult), r=["rq0"], w=["rq"])
        P.act(lambda e: e.activation(out=rkv, in_=lk, func=AF.Exp, scale=-0.5), r=["lk"], w=["rkv"])

    def b1_feat(ct, s):
        b = 5
        mmg(ps[b], [(slabM[:, c, ct * 128:(ct + 1) * 128], uT[:, c, sb_(s)]) for c in range(8)], r=["slabM", f"uT{s}"], w=[f"ps{b}"])
        if ct < 3:
            P.dve(lambda e, b=b, ct=ct, s=s: e.tensor_scalar(out=cqT[:, ct, sb_(s)], in0=ps[b], scalar1=ncq[:, ct:ct + 1], scalar2=None, op0=ALU.mult),
                  r=[f"ps{b}", "ncq"], w=["cqT"])
        else:
            P.dve(lambda e, b=b, ct=ct, s=s: e.tensor_scalar(out=ckvT[:, ct - 3, sb_(s)], in0=ps[b], scalar1=nckv[:, ct - 3:ct - 2], scalar2=None, op0=ALU.mult),
                  r=[f"ps{b}", "nckv"], w=["ckvT"])

    b1_items = []
    feats = [(ct, s) for ct in range(5) for s in range(NSB)]
    extra = {1: b1_rope_setup, 3: lambda: b1_rope_reduce(0.0, rbs, "rbs"), 6: lambda: b1_rope_reduce(PI / 2.0, rbc, "rbc"), 9: b1_rope_sin}
    for k in range(NT):
        b1_items.append(lambda k=k: (b1_stats(k), extra[k]() if k in extra else None))
        b1_items.append(lambda k=k: b1_feat(*feats[k]))
    for k in range(NT, len(feats)):
        b1_items.append(lambda k=k: b1_feat(*feats[k]))
    b1_items.append(b1_fin)
    assert d_end <= mB, (d_end, mB)

    def stage_a(c):
        pa = c % 2

        def attn_mm(e, c=c, pa=pa):
            ins = None
            for h in range(4):
                ins = e.matmul(ps[pa][:, h * 128:(h + 1) * 128], ktT[:, h // 2, tl(c)], qz[:, h, tl(c)], start=True, stop=True)
            return ins
        P.pe(attn_mm, r=["qz", "ktT"], w=[f"ps{pa}"])
        P.dve(lambda e, pa=pa: e.tensor_tensor(out=attn_sb[pa], in0=ps[pa].rearrange("p (h t) -> p h t", h=4),
                                               in1=tri.unsqueeze(1).to_broadcast([128, 4, 128]), op=ALU.mult),
              r=[f"ps{pa}", "tri"], w=[f"attn{pa}"])

    def stage_b(c):
        po = 2 + c % 2
        ab = c % 2

        def o_mm(e, c=c, po=po, ab=ab):
            ins = None
            for h in range(4):
                if c > 0:
                    e.matmul(ps[po][:, h * 128:(h + 1) * 128], qz[:, h, tl(c)], Sall[:, c - 1, h // 2, (h % 2) * 128:(h % 2 + 1) * 128],
                             start=True, stop=False)
                ins = e.matmul(ps[po][:, h * 128:(h + 1) * 128], attn_sb[ab][:, h, :], vg[:, c, h * 128:(h + 1) * 128],
                               start=(c == 0), stop=True)
            return ins
        P.pe(o_mm, r=[f"attn{ab}", "vg", "qz", "Sall"], w=[f"ps{po}"])
        for h in range(4):
            P.act(lambda e, h=h, c=c, po=po: e.activation(out=sqj2[:, h, :], in_=ps[po][:, h * 128:(h + 1) * 128], func=AF.Square,
                                                          accum_out=ssg[:, c * 4 + h:c * 4 + h + 1]),
                  r=[f"ps{po}"], w=[f"sqj2_{h}", f"ssg{c}"])
        P.act(lambda e, c=c: e.activation(out=lng[:, c * 4:c * 4 + 4], in_=ssg[:, c * 4:c * 4 + 4], func=AF.Ln, scale=1.0 / 128.0, bias=epsb[:, 0:1]),
              r=[f"ssg{c}", "epsb"], w=[f"lng{c}"])
        P.act(lambda e, c=c: e.activation(out=rsg[:, c * 4:c * 4 + 4], in_=lng[:, c * 4:c * 4 + 4], func=AF.Exp, scale=-0.5),
              r=[f"lng{c}"], w=[f"rsg{c}"])
        for h in range(4):
            P.dve(lambda e, h=h, c=c, po=po, ab=ab: e.scalar_tensor_tensor(out=ogt[ab][:, h * 128:(h + 1) * 128], in0=ps[po][:, h * 128:(h + 1) * 128],
                                                                           scalar=rsg[:, c * 4 + h:c * 4 + h + 1], in1=sog[:, c, h * 128:(h + 1) * 128],
                                                                           op0=ALU.mult, op1=ALU.mult),
                  r=[f"ps{po}", f"rsg{c}", "sog"], w=[f"ogt{ab}"])

    def stage_c(c):
        ab = c % 2
        pt = 4
        pbt = psb[pt]

        def tro(e, ab=ab, pbt=pbt):
            ins = None
            for h in range(4):
                ins = e.transpose(pbt[:, h * 128:(h + 1) * 128], ogt[ab][:, h * 128:(h + 1) * 128], ident)
            return ins
        P.pe(tro, r=[f"ogt{ab}", "ident"], w=[f"ps{pt}"])
        P.act(lambda e, c=c, pbt=pbt: e.activation(out=OGT[:, :, tl(c)], in_=pbt[:, 0:512].rearrange("p (h t) -> p h t", h=4), func=AF.Copy,
                                                   scale=gnc[:, 0:1]),
              r=[f"ps{pt}", "gnc"], w=["OGT"])

    for c in range(NT + 2):
        if c < NT:
            stage_a(c)
        if 1 <= c <= NT:
            stage_b(c - 1)
        if c >= 2:
            stage_c(c - 2)
        for _ in range(B1_PER_STAGE):
            if b1_items:
                b1_items.pop(0)()
    while b1_items:
        b1_items.pop(0)()
    dump("OGT", OGT, [128, 4, T], BF16)

    if stop_after == "gla":
        return finish(nc, st, P, out, dbg_out, A)

    P.barrier()
    A.top = mB2
    qs9 = [A.alloc([128, 9, 32], F32) for _ in range(2)]
    ra9 = [A.alloc([128, 9, 32], F32) for _ in range(2)]
    rb9 = [A.alloc([128, 9, 32], F32) for _ in range(2)]
    qrot = [A.alloc([128, 8, 96], BF16) for _ in range(2)]
    krot = [A.alloc([128, 8, 96], BF16) for _ in range(2)]
    kro = A.alloc([128, 32], BF16)
    P.dve(lambda e: e.memset(Vt4[:, :, 0:8:2, 64:128], 1.0), r=[], w=["Vones"])
    P.dve(lambda e: e.memset(Vt4[:, :, 1:8:2, 0:64], 1.0), r=[], w=["Vones"])
    assert A.top <= A1_BASE, (A.top, A1_BASE)

    def b_proj(i):
        b = i % 2
        mmg(ps[0][:, 0:384], [(cqT[:, c, tl(i)], wuq[:, c, 0:384]) for c in range(3)], r=["cqT", "wuq"], w=["ps0"])
        mmg(ps[1][:, 0:384], [(cqT[:, c, tl(i)], wuq[:, c, 384:768]) for c in range(3)], r=["cqT", "wuq"], w=["ps1"])
        mmg(ps[2], [(ckvT[:, c, tl(i)], wukv[:, c, 0:512]) for c in range(2)], r=["ckvT", "wukv"], w=["ps2"])
        mmg(ps[3], [(ckvT[:, c, tl(i)], wukv[:, c, 512:1024]) for c in range(2)], r=["ckvT", "wukv"], w=["ps3"])
        for half in range(2):
            pq_ = ps[half][:, 0:384].rearrange("p (h d) -> p h d", h=4)
            hs = slice(half * 4, half * 4 + 4)
            P.act(lambda e, i=i, b=b, pq_=pq_, hs=hs: e.activation(out=qrot[b][:, hs, 0:64], in_=pq_[:, :, 0:64], func=AF.Copy, scale=rq[:, i:i + 1]),
                  r=[f"ps{half}", "rq"], w=[f"qrot{b}"])
            P.act(lambda e, i=i, b=b, pq_=pq_, hs=hs: e.activation(out=qs9[b][:, hs, :], in_=pq_[:, :, 64:96], func=AF.Copy, scale=rq[:, i:i + 1]),
                  r=[f"ps{half}", "rq"], w=[f"qs9{b}"])
        for half in range(2):
            pv = ps[2 + half].rearrange("p (h d) -> p h d", h=4)
            hs = slice(half * 4, half * 4 + 4)
            P.act(lambda e, i=i, b=b, pv=pv, hs=hs: e.activation(out=krot[b][:, hs, 0:64], in_=pv[:, :, 0:64], func=AF.Copy, scale=rkv[:, i:i + 1]),
                  r=[f"ps{2 + half}", "rkv"], w=[f"krot{b}"])
            P.act(lambda e, i=i, half=half, pv=pv: e.activation(out=Vt4[:, i, half * 4:half * 4 + 4:2, 0:64],
                                                                in_=pv[:, 0:4:2, 64:128], func=AF.Copy, scale=rkv[:, i:i + 1]),
                  r=[f"ps{2 + half}", "rkv"], w=["Vt"])
            P.act(lambda e, i=i, half=half, pv=pv: e.activation(out=Vt4[:, i, half * 4 + 1:half * 4 + 4:2, 64:128],
                                                                in_=pv[:, 1:4:2, 64:128], func=AF.Copy, scale=rkv[:, i:i + 1]),
                  r=[f"ps{2 + half}", "rkv"], w=["Vt"])
        P.dve(lambda e, i=i, b=b: e.tensor_copy(out=qs9[b][:, 8, :], in_=krtok[:, i, :]), r=["krtok"], w=[f"qs9{b}"])
        P.dve(lambda e, i=i, b=b: e.tensor_tensor(out=ra9[b], in0=qs9[b], in1=cs2[:, i, :].unsqueeze(1).to_broadcast([128, 9, 32]), op=ALU.mult),
              r=[f"qs9{b}", "cos"], w=[f"ra9{b}"])
        P.dve(lambda e, i=i, b=b: e.tensor_tensor(out=rb9[b][:, :, 0:16], in0=qs9[b][:, :, 16:32],
                                                  in1=sn2[:, i, 0:16].unsqueeze(1).to_broadcast([128, 9, 16]), op=ALU.mult),
              r=[f"qs9{b}", "sin"], w=[f"rb9{b}"])
        P.dve(lambda e, i=i, b=b: e.tensor_tensor(out=rb9[b][:, :, 16:32], in0=qs9[b][:, :, 0:16],
                                                  in1=sn2[:, i, 16:32].unsqueeze(1).to_broadcast([128, 9, 16]), op=ALU.mult),
              r=[f"qs9{b}", "sin"], w=[f"rb9{b}"])
        P.dve(lambda e, b=b: e.tensor_tensor(out=qrot[b][:, :, 64:96], in0=ra9[b][:, 0:8, :], in1=rb9[b][:, 0:8, :], op=ALU.add),
              r=[f"ra9{b}", f"rb9{b}"], w=[f"qrot{b}"])
        P.dve(lambda e, b=b: e.tensor_tensor(out=kro, in0=ra9[b][:, 8, :], in1=rb9[b][:, 8, :], op=ALU.add), r=[f"ra9{b}", f"rb9{b}"], w=["kro"])
        P.dve(lambda e, b=b: e.tensor_copy(out=krot[b][:, :, 64:96], in_=kro.unsqueeze(1).to_broadcast([128, 8, 32])), r=["kro"], w=[f"krot{b}"])

    def b_trans(i):
        b = i % 2
        pq = psb[4 + b]

        def trq(e, b=b, pq=pq):
            ins = None
            for h in range(8):
                ins = e.transpose(pq[0:96, h * 128:(h + 1) * 128], qrot[b][:, h, :], ident)
            return ins
        P.pe(trq, r=[f"qrot{b}", "ident"], w=[f"ps{4 + b}"])
        P.dve(lambda e, i=i, pq=pq: e.tensor_copy(out=QT[:, :, tl(i)], in_=pq[0:96, :].rearrange("p (h t) -> p h t", h=8)),
              r=[f"ps{4 + b}"], w=["QT"])
        pk = psb[6 + b]

        def trk2(e, b=b, pk=pk):
            ins = None
            for h in range(8):
                ins = e.transpose(pk[0:96, h * 128:(h + 1) * 128], krot[b][:, h, :], ident)
            return ins
        P.pe(trk2, r=[f"krot{b}", "ident"], w=[f"ps{6 + b}"])
        P.dve(lambda e, i=i, pk=pk: e.tensor_copy(out=KT[:, :, tl(i)], in_=pk[0:96, :].rearrange("p (h t) -> p h t", h=8)),
              r=[f"ps{6 + b}"], w=["KT"])

    for i in range(NT + 1):
        if i < NT:
            b_proj(i)
        if i >= 1:
            b_trans(i - 1)
    dump("QT", QT, [96, 8, T], BF16)
    dump("KT", KT, [96, 8, T], BF16)
    dump("Vt", Vt4, [128, NT, 8, 128], BF16)
    P.barrier()
    A.top = mB

    OT = A.alloc([128, 4, T], BF16)
    NSC = 6
    PT = [A.alloc([128, 512], BF16) for _ in range(NSC)]
    rinv = A.alloc([128, 512], F32)
    lnl = A.alloc([128, 512], F32)
    woa = A.alloc([128, 4, 1024], BF16)
    wob = A.alloc([128, 4, 1024], BF16)
    gsl = [A.alloc([128, 8, 256], BF16) for _ in range(2)]
    off_e1w = A.top
    P.dma("pool", woa, w_oa.rearrange("(c p) n -> p c n", p=128), r=[], w=["woa"])
    P.dma("pool", wob, w_ob.rearrange("(c p) n -> p c n", p=128), r=[], w=["wob"])

    def load_gsl(ft):
        g = ft % 2
        P.dma("pool", gsl[g][:, :, 0:128], w_in[:, O_GA + ft * 128:O_GA + (ft + 1) * 128].rearrange("(c p) n -> p c n", p=128), r=[], w=[f"gslA{g}"])
        P.dma("pool", gsl[g][:, :, 128:256], w_in[:, O_GB + ft * 128:O_GB + (ft + 1) * 128].rearrange("(c p) n -> p c n", p=128), r=[], w=[f"gslB{g}"])
    load_gsl(0)
    load_gsl(1)
    for p in range(4):
        for Q in range(NSB):
            nj = 4 * Q + 4
            items = [(hh, j) for hh in range(2) for j in range(nj)]
            obank = (6, 7)

            def geom(j, Q=Q):
                m = j - 4 * Q if j >= 4 * Q else 0
                c0 = m * 128
                return c0, 512 - c0

            def emit_qk(k, p=p, Q=Q, items=items):
                hh, j = items[k]
                h = 2 * p + hh
                c0, N = geom(j)
                sbk = k % NSC
                mmg(ps[sbk][:, 0:N], [(KT[:, h, tl(j)], QT[:, h, Q * 512 + c0:(Q + 1) * 512])], r=["KT", "QT"], w=[f"ps{sbk}"])
                P.act(lambda e, sbk=sbk, N=N: e.activation(out=PT[sbk][:, 0:N], in_=ps[sbk][:, 0:N], func=AF.Exp), r=[f"ps{sbk}"], w=[f"PT{sbk}"])
                if j >= 4 * Q:
                    P.dve(lambda e, sbk=sbk: e.tensor_tensor(out=PT[sbk][:, 0:128], in0=PT[sbk][:, 0:128], in1=tri, op=ALU.mult),
                          r=[f"PT{sbk}", "tri"], w=[f"PT{sbk}"])

            def emit_pv(k, p=p, Q=Q, items=items, nj=nj, obank=obank):
                hh, j = items[k]
                h = 2 * p + hh
                c0, N = geom(j)
                sbk = k % NSC
                bk = obank[hh]
                P.pe(lambda e, bk=bk, c0=c0, N=N, h=h, sbk=sbk, j=j: e.matmul(ps[bk][:, c0:512], Vt4[:, j, h, :], PT[sbk][:, 0:N],
                                                                            start=(j == 0), stop=(j == nj - 1)),
                     r=[f"PT{sbk}", "Vt", "Vones"], w=[f"ps{bk}"])
            n = len(items)
            LA = NSC - 1
            for k in range(n + LA):
                if k < n:
                    emit_qk(k)
                if k >= LA:
                    emit_pv(k - LA)
            for hh in range(2):
                bk = obank[hh]
                lr = slice(64, 128) if hh == 0 else slice(0, 64)
                orow = slice(0, 64) if hh == 0 else slice(64, 128)
                P.act(lambda e, bk=bk, lr=lr: e.activation(out=lnl[lr, :], in_=ps[bk][lr, :], func=AF.Ln), r=[f"ps{bk}"], w=[f"lnl{hh}"])
                P.act(lambda e, lr=lr: e.activation(out=lnl[lr, :], in_=lnl[lr, :], func=AF.Exp, scale=-1.0), r=[f"lnl{hh}"], w=[f"lnl{hh}"])
                P.dve(lambda e, lr=lr, orow=orow: e.tensor_copy(out=rinv[orow, :], in_=lnl[lr, :]), r=[f"lnl{hh}"], w=[f"rinv{hh}"])
                P.dve(lambda e, p=p, Q=Q, bk=bk, orow=orow: e.tensor_tensor(out=OT[orow, p, sb_(Q)], in0=ps[bk][orow, :], in1=rinv[orow, :], op=ALU.mult),
                      r=[f"ps{bk}", f"rinv{hh}"], w=["OT"])
    dump("OT", OT, [128, 4, T], BF16)
    if stop_after == "attn":
        return finish(nc, st, P, out, dbg_out, A)
    P.barrier(keep=("woa", "wob", "gslA0", "gslB0", "gslA1", "gslB1"))

    A.top = m0
    mixT = A.alloc([128, 8, T], BF16)
    off_after_mix = A.top
    assert A.top <= mB, (A.top, mB)
    off_e1t = A.top
    ea = [A.alloc([128, 512], F32) for _ in range(2)]
    eb2 = [A.alloc([128, 512], F32) for _ in range(2)]
    tA = A.alloc([128, 512], F32)
    tB = A.alloc([128, 512], F32)
    off_wout = (A.top + 63) // 64 * 64
    wout = A.alloc([128, 8, 1024], BF16)
    P.dma("pool", wout, w_out.rearrange("(c p) n -> p c n", p=128), r=[], w=["wout"])
    for ft in range(8):
        g = ft % 2
        if ft >= 2:
            load_gsl(ft)
        for s in range(NSB):
            q = (ft * NSB + s) % 2
            ba, bb, bya, byb = (0, 1, 2, 3) if q == 0 else (4, 5, 6, 7)
            mmg(ps[ba], [(gsl[g][:, c, 0:128], uT[:, c, sb_(s)]) for c in range(8)], r=[f"gslA{g}", f"uT{s}"], w=[f"ps{ba}"])
            mmg(ps[bb], [(gsl[g][:, c, 128:256], uT[:, c, sb_(s)]) for c in range(8)], r=[f"gslB{g}", f"uT{s}"], w=[f"ps{bb}"])
            mmg(ps[bya], [(woa[:, pp, tl(ft)], OT[:, pp, sb_(s)]) for pp in range(4)], r=["woa", "OT"], w=[f"ps{bya}"])
            mmg(ps[byb], [(wob[:, pp, tl(ft)], OGT[:, pp, sb_(s)]) for pp in range(4)], r=["wob", "OGT"], w=[f"ps{byb}"])
            for (bk, et, nm) in ((ba, ea[q], f"ea{q}"), (bb, eb2[q], f"eb{q}")):
                P.act(lambda e, bk=bk, et=et: e.activation(out=et, in_=ps[bk], func=AF.Exp, scale=-1.0), r=[f"ps{bk}"], w=[nm])
                P.act(lambda e, et=et: e.activation(out=et, in_=et, func=AF.Ln, bias=one_b[:, 0:1]), r=[nm, "one_b"], w=[nm])
                P.act(lambda e, et=et: e.activation(out=et, in_=et, func=AF.Exp, scale=-1.0), r=[nm], w=[nm])
            P.dve(lambda e, q=q, bya=bya: e.tensor_tensor(out=tA, in0=ps[bya], in1=ea[q], op=ALU.mult), r=[f"ps{bya}", f"ea{q}"], w=["tA"])
            P.dve(lambda e, q=q, byb=byb: e.tensor_tensor(out=tB, in0=ps[byb], in1=eb2[q], op=ALU.mult), r=[f"ps{byb}", f"eb{q}"], w=["tB"])
            P.dve(lambda e, ft=ft, s=s: e.tensor_tensor(out=mixT[:, ft, sb_(s)], in0=tA, in1=tB, op=ALU.add), r=["tA", "tB"], w=["mixT"])
    dump("mixT", mixT, [128, 8, T], BF16)
    if stop_after == "mix":
        return finish(nc, st, P, out, dbg_out, A)
    P.barrier(keep=("wout",))

    h1 = A.alloc([128, NT, D], F32)
    off_after_h1 = A.top
    fsl = [A.alloc([128, 8, 256], BF16) for _ in range(2)]
    fnb = A.alloc([128, D], F32)
    ot = [A.alloc([128, D], F32) for _ in range(2)]
    ssf = A.alloc([128, NT], F32)
    lnf = A.alloc([128, NT], F32)
    rsf = A.alloc([128, NT], F32)
    sqf = A.alloc([128, D], BF16)
    off_e2t = A.top
    A.top = off_e1t
    xt2 = [A.alloc([128, D], F32) for _ in range(2)]
    assert A.top <= off_wout

    def load_fsl(f):
        g = f % 2
        P.dma("pool", fsl[g][:, :, 0:128], w_fg[:, f * 128:(f + 1) * 128].rearrange("(c p) n -> p c n", p=128), r=[], w=[f"fslG{g}"])
        P.dma("pool", fsl[g][:, :, 128:256], w_fu[:, f * 128:(f + 1) * 128].rearrange("(c p) n -> p c n", p=128), r=[], w=[f"fslU{g}"])
    load_fsl(0)
    load_fsl(1)
    P.dma("pool", fnb, v_fn, r=[], w=["fnb"])
    u2T = uT
    sv_top = A.top
    A.top = off_e2t
    nt_v = make_nt(u2T, lnffn, "v", NT, banks=(6, 7))
    off_e2t = A.top
    A.top = sv_top
    for i in range(NT):
        b = i % 2
        P.dma("sp", xt2[b], x[tl(i), :], r=[], w=[f"xt2{b}"])
        for half in range(2):
            bk = 2 * b + half
            mmg(ps[bk], [(mixT[:, ft, tl(i)], wout[:, ft, half * 512:(half + 1) * 512]) for ft in range(8)], r=["mixT", "wout"], w=[f"ps{bk}"])
            P.dve(lambda e, i=i, b=b, half=half, bk=bk: e.tensor_tensor(out=h1[:, i, half * 512:(half + 1) * 512], in0=ps[bk],
                                                                       in1=xt2[b][:, half * 512:(half + 1) * 512], op=ALU.add),
                  r=[f"ps{bk}", f"xt2{b}"], w=[f"h1_{i}"])
            if half == 0:
                if i >= 1:
                    nt_v.part1(i - 1, h1[:, i - 1, :], [f"h1_{i - 1}"])
                if i >= 2:
                    nt_v.part2(i - 2)
    nt_v.part2(NT - 2)
    nt_v.part1(NT - 1, h1[:, NT - 1, :], [f"h1_{NT - 1}"])
    dump("h1", h1, [128, NT, D], F32)

    A.top = off_OGT
    wd = A.alloc([128, 6, 1024], BF16)
    A.top = m0
    aT = A.alloc([128, 6, T], BF16)
    tg = [A.alloc([128, 512], F32) for _ in range(2)]
    tt = [A.alloc([128, 512], F32) for _ in range(2)]
    assert A.top <= off_after_mix
    A.top = off_e2t

    def final_tile(i):
        b = i % 2
        P.act(lambda e, i=i: e.activation(out=sqf, in_=h1[:, i, :], func=AF.Square, accum_out=ssf[:, i:i + 1]), r=[f"h1_{i}"], w=["sqf", f"ssf{i}"])
        P.act(lambda e, i=i: e.activation(out=lnf[:, i:i + 1], in_=ssf[:, i:i + 1], func=AF.Ln, scale=1.0 / D, bias=epsb[:, 0:1]),
              r=[f"ssf{i}", "epsb"], w=[f"lnf{i}"])
        P.act(lambda e, i=i: e.activation(out=rsf[:, i:i + 1], in_=lnf[:, i:i + 1], func=AF.Exp, scale=-0.5), r=[f"lnf{i}"], w=[f"rsf{i}"])
        P.dve(lambda e, i=i, b=b: e.scalar_tensor_tensor(out=ot[b], in0=h1[:, i, :], scalar=rsf[:, i:i + 1], in1=fnb, op0=ALU.mult, op1=ALU.mult),
              r=[f"h1_{i}", f"rsf{i}", "fnb"], w=[f"ot{b}"])
        P.dma("sp", out[tl(i), :], ot[b], r=[f"ot{b}"], w=[f"out{i}"])

    groups = [(0, 5), (5, 10), (10, 16), (16, 22)]
    for (f0, f1) in groups:
        nf = f1 - f0
        for f in range(f0, f1):
            g = f % 2
            if f >= 2:
                load_fsl(f)
            if f == f0 + 1:
                P.dma("pool", wd[:, 0:nf, :], w_fd[f0 * 128:f1 * 128, :].rearrange("(f p) n -> p f n", p=128), r=[], w=["wd"])
            for s in range(NSB):
                q = s % 2
                bg_, bu_ = (0, 1) if q == 0 else (2, 3)
                alias0 = ["mixT"] if (f == 0 and s == 0) else []
                mmg(ps[bg_], [(fsl[g][:, c, 0:128], u2T[:, c, sb_(s)]) for c in range(8)], r=[f"fslG{g}", f"vT{s}"], w=[f"ps{bg_}"])
                mmg(ps[bu_], [(fsl[g][:, c, 128:256], u2T[:, c, sb_(s)]) for c in range(8)], r=[f"fslU{g}", f"vT{s}"], w=[f"ps{bu_}"])
                P.act(lambda e, q=q, bg_=bg_: e.activation(out=tg[q], in_=ps[bg_], func=AF.Exp, scale=-1.0), r=[f"ps{bg_}"], w=[f"tg{q}"] + alias0)
                P.act(lambda e, q=q: e.activation(out=tg[q], in_=tg[q], func=AF.Ln, bias=one_b[:, 0:1]), r=[f"tg{q}", "one_b"], w=[f"tg{q}"])
                P.act(lambda e, q=q: e.activation(out=tg[q], in_=tg[q], func=AF.Exp, scale=-1.0), r=[f"tg{q}"], w=[f"tg{q}"])
                P.dve(lambda e, q=q, bg_=bg_: e.tensor_tensor(out=tt[q], in0=ps[bg_], in1=tg[q], op=ALU.mult), r=[f"ps{bg_}", f"tg{q}"], w=[f"tt{q}"] + alias0)
                P.dve(lambda e, q=q, bu_=bu_, f=f, f0=f0, s=s: e.tensor_tensor(out=aT[:, f - f0, sb_(s)], in0=ps[bu_], in1=tt[q], op=ALU.mult),
                      r=[f"ps{bu_}", f"tt{q}"], w=[f"aT{s}"])
                if f == 0 and s == 2:
                    nt_v.part2(NT - 1)
        for i in range(NT):
            for half in range(2):
                bk = 4 + (2 * i + half) % 4
                mmg(ps[bk], [(aT[:, fl, tl(i)], wd[:, fl, half * 512:(half + 1) * 512]) for fl in range(nf)], r=[f"aT{i // 4}", "wd"], w=[f"ps{bk}"])
                P.dve(lambda e, i=i, half=half, bk=bk: e.tensor_tensor(out=h1[:, i, half * 512:(half + 1) * 512], in0=ps[bk],
                                                                      in1=h1[:, i, half * 512:(half + 1) * 512], op=ALU.add),
                      r=[f"ps{bk}", f"h1_{i}"], w=[f"h1_{i}"])
            if f1 == NF and i >= 1:
                final_tile(i - 1)
    final_tile(NT - 1)
    return finish(nc, st, P, out, dbg_out, A)


def finish(nc, st, P, out, dbg_out, A):
    P.barrier()
    P.emit(nc, st)
    st.close()
    return nc, dbg_out


def host_consts():
    half = 16
    inv64 = 1.0 / (10000.0 ** (np.arange(half, dtype=np.float64) / half))
    hi = inv64.astype(np.float32)
    lo = (inv64 - hi.astype(np.float64)).astype(np.float32)
    c = {}
    c["c_inv"] = np.ascontiguousarray(np.broadcast_to(np.concatenate([hi, lo])[None, :], (128, 32))).astype(np.float32)
    c["c_ident"] = np.eye(128, dtype=np.float32).astype(ml_dtypes.bfloat16)
    j = np.arange(128)[:, None]
    i = np.arange(128)[None, :]
    c["c_tri"] = (i >= j).astype(np.float32).astype(ml_dtypes.bfloat16)
    return c


def pcol(v, n):
    return np.ascontiguousarray(np.asarray(v, dtype=np.float32).reshape(n, 128).T)


def shared_map(inp):
    f = lambda a: np.ascontiguousarray(np.asarray(a, dtype=np.float32))
    m = dict(
        w_in=f(inp["w_in"][0]), w_uq=f(inp["mla_w_uq"][0]), w_ukv=f(inp["mla_w_ukv"][0]), w_oa=f(inp["mla_w_o"][0]),
        w_g2=f(inp["gla_w_gate2"][0]), w_ob=f(inp["gla_w_o"][0]), w_out=f(inp["w_out"][0]),
        w_fg=f(inp["ffn_w_gate"][0]), w_fu=f(inp["ffn_w_up"][0]), w_fd=f(inp["ffn_w_down"][0]),
        v_lnmix=pcol(inp["ln_mix"][0], 8), v_lnffn=pcol(inp["ln_ffn"][0], 8), v_ncq=pcol(inp["mla_norm_cq"][0], 3),
        v_nckv=pcol(inp["mla_norm_ckv"][0], 2), v_bg=pcol(inp["gla_b_gate"][0], 2),
        v_gn=np.ascontiguousarray(np.broadcast_to(np.asarray(inp["gla_norm"][0], dtype=np.float32)[None, :], (128, 128))),
        v_gnc=pcol(inp["gla_norm"][0], 1),
        v_fn=np.ascontiguousarray(np.broadcast_to(np.asarray(inp["final_norm"], dtype=np.float32)[None, :], (128, D))),
    )
    m.update(host_consts())
    return m


def core_map(inp, b, shared):
    m = dict(shared)
    m["x"] = np.ascontiguousarray(np.asarray(inp["x"][b], dtype=np.float32))
    m["pos"] = np.ascontiguousarray(np.asarray(inp["positions"][b], dtype=np.int32).reshape(NT, 128).T)
    return m


_CACHE = {}


def kernel(**inputs):
    if "nc" not in _CACHE:
        _CACHE["nc"] = build()[0]
    nc = _CACHE["nc"]
    shared = shared_map(inputs)
    B = np.asarray(inputs["x"]).shape[0]
    in_maps = [core_map(inputs, b, shared) for b in range(B)]
    res = run_bass_kernel_spmd(nc, in_maps, core_ids=list(range(B)))
    return np.stack([np.asarray(r["out"], dtype=np.float32) for r in res.results], axis=0)
```

```python
import contextlib
import numpy as np
import ml_dtypes
import concourse.bass as bass
import concourse.mybir as mybir
from concourse.bass_utils import run_bass_kernel_spmd

F32 = mybir.dt.float32
BF16 = mybir.dt.bfloat16
I32 = mybir.dt.int32
U8 = mybir.dt.uint8
AF = mybir.ActivationFunctionType
ALU = mybir.AluOpType

T = 2048
D = 1024
NT = 16
NSB = 4
DFF = 2816
NF = 22
DIN = 4272
O_CQ, O_CKV, O_KR, O_GQ, O_GK, O_GV, O_GLR, O_GOG, O_GA, O_GB = 0, 384, 640, 672, 928, 1184, 1696, 1712, 2224, 3248
EPS = 1e-6
PI = float(np.pi)
NSLOT = 8
B1_PER_STAGE = 2
ENGS = ("pe", "act", "dve", "pool", "sp")


_DISJOINT = ("mixT", "qz", "ktT", "vg", "sog", "ktok", "Vt", "QT", "KT", "OT", "OGT", "cqT", "ckvT", "krtok", "glrT", "Sall",
             "ebl", "uT", "vT", "ssg", "ssk", "aT")


def _disjoint(k):
    return k.rstrip("0123456789_") in _DISJOINT


class Prog:
    def __init__(self):
        self.ops = []
        self.lw = {}
        self.rd = {}
        self.last = {}
        self.dmas_since_barrier = []

    def add(self, eng, fn, r=(), w=(), dma=False):
        idx = len(self.ops)
        deps = set()
        for k in r:
            p = self.lw.get(k)
            if p is not None:
                deps.add(p)
        for k in w:
            p = self.lw.get(k)
            if p is not None and not (_disjoint(k) and not dma and not self.ops[p]["dma"] and self.ops[p]["eng"] == eng):
                deps.add(p)
            for q in self.rd.get(k, {}).values():
                if isinstance(q, list):
                    deps.update(q)
                else:
                    deps.add(q)
        for k in r:
            d = self.rd.setdefault(k, {})
            if dma:
                d.setdefault("dma", []).append(idx)
            else:
                d[eng] = idx
        for k in w:
            self.lw[k] = idx
            self.rd[k] = {}
        deps.discard(idx)
        self.ops.append(dict(eng=eng, fn=fn, deps=deps, dma=dma, sig=False))
        if dma:
            self.dmas_since_barrier.append(idx)
        else:
            self.last[eng] = idx
        return idx

    def pe(self, fn, r=(), w=()):
        return self.add("pe", fn, r, w)

    def act(self, fn, r=(), w=()):
        return self.add("act", fn, r, w)

    def dve(self, fn, r=(), w=()):
        return self.add("dve", fn, r, w)

    def pool(self, fn, r=(), w=()):
        return self.add("pool", fn, r, w)

    def dma(self, q, out, in_, r=(), w=()):
        return self.add(q, lambda e: e.dma_start(out=out, in_=in_), r, w, dma=True)

    def barrier(self, keep=()):
        kept = {self.lw[k] for k in keep if k in self.lw and self.ops[self.lw[k]]["dma"]}
        deps = set(self.last.values()) | (set(self.dmas_since_barrier) - kept)
        for eng in ENGS:
            self.ops.append(dict(eng=eng, fn=None, deps=set(deps), dma=False, sig=False))
        self.last = {}
        self.lw = {k: v for k, v in self.lw.items() if v in kept}
        self.rd = {}
        self.dmas_since_barrier = sorted(kept)

    def emit(self, nc, st):
        ops = self.ops
        for op in ops:
            for d in op["deps"]:
                p = ops[d]
                if p["dma"]:
                    continue
                if p["eng"] == "pe" and op["eng"] == "pe" and not op["dma"]:
                    continue
                p["sig"] = True
        cnt = {e: 0 for e in ENGS}
        dcnt = {"pool": 0, "sp": 0}
        for op in ops:
            if op["dma"]:
                k = dcnt[op["eng"]]
                dcnt[op["eng"]] += 1
                op["slot"] = k % NSLOT
                op["val"] = 16 * (k // NSLOT + 1)
            elif op["sig"]:
                cnt[op["eng"]] += 1
                op["cnt"] = cnt[op["eng"]]
        sem = {e: st.enter_context(nc.semaphore("s_" + e)) for e in ENGS}
        dsem = {q: [st.enter_context(nc.semaphore(f"d_{q}{i}")) for i in range(NSLOT)] for q in ("pool", "sp")}
        per = {e: [] for e in ENGS}
        for i, op in enumerate(ops):
            per[op["eng"]].append(i)

        def run(engname, e):
            seen = {}
            for i in per[engname]:
                op = ops[i]
                waits = {}
                for d in op["deps"]:
                    p = ops[d]
                    if p["dma"]:
                        key = ("d", p["eng"], p["slot"])
                        v = p["val"]
                    else:
                        if p["eng"] == "pe" and engname == "pe" and not op["dma"]:
                            continue
                        key = ("c", p["eng"])
                        v = p["cnt"]
                    if v > waits.get(key, 0):
                        waits[key] = v
                if op["dma"] and op["val"] > 16:
                    key = ("d", engname, op["slot"])
                    waits[key] = max(waits.get(key, 0), op["val"] - 16)
                for key, v in waits.items():
                    if seen.get(key, 0) >= v:
                        continue
                    seen[key] = v
                    s = sem[key[1]] if key[0] == "c" else dsem[key[1]][key[2]]
                    e.wait_ge(s, v)
                if op["fn"] is None:
                    continue
                ins = op["fn"](e)
                if op["dma"]:
                    ins.then_inc(dsem[engname][op["slot"]], 16)
                elif op["sig"]:
                    ins.then_inc(sem[engname], 1)

        block = st.enter_context(nc.Block())

        @block.tensor
        def _(e):
            run("pe", e)

        @block.scalar
        def _(e):
            run("act", e)

        @block.vector
        def _(e):
            run("dve", e)

        @block.gpsimd
        def _(e):
            run("pool", e)

        @block.sync
        def _(e):
            run("sp", e)


class Arena:
    def __init__(self, nc, nbytes):
        self.t = nc.alloc_sbuf_tensor("arena", [128, nbytes], U8)
        self.n = nbytes
        self.top = 0
        self.peak = 0

    def alloc(self, shape, dt):
        esz = 4 if dt in (F32, I32) else 2
        nb = int(np.prod(shape[1:])) * esz
        off = (self.top + 63) // 64 * 64
        assert off + nb <= self.n, f"SBUF arena overflow {off + nb} > {self.n}"
        self.top = off + nb
        self.peak = max(self.peak, self.top)
        a = self.t[0:shape[0], off:off + nb].bitcast(dt)
        if len(shape) == 3:
            a = a.rearrange("p (a b) -> p a b", a=shape[1])
        elif len(shape) == 4:
            a = a.rearrange("p (a b c) -> p a b c", a=shape[1], b=shape[2])
        return a


def build(debug=(), stop_after=None):
    nc = bass.Bass("TRN2", target_bir_lowering=False)

    def din(name, shape, dt=F32):
        return nc.dram_tensor(name, list(shape), dt, kind="ExternalInput").ap()

    x = din("x", [T, D])
    pos = din("pos", [128, NT], I32)
    w_in = din("w_in", [D, DIN])
    w_uq = din("w_uq", [384, 768])
    w_ukv = din("w_ukv", [256, 1024])
    w_oa = din("w_oa", [512, 1024])
    w_g2 = din("w_g2", [16, 256])
    w_ob = din("w_ob", [512, 1024])
    w_out = din("w_out", [D, D])
    w_fg = din("w_fg", [D, DFF])
    w_fu = din("w_fu", [D, DFF])
    w_fd = din("w_fd", [DFF, D])
    v_lnmix = din("v_lnmix", [128, 8])
    v_lnffn = din("v_lnffn", [128, 8])
    v_ncq = din("v_ncq", [128, 3])
    v_nckv = din("v_nckv", [128, 2])
    v_bg = din("v_bg", [128, 2])
    v_gn = din("v_gn", [128, 128])
    v_gnc = din("v_gnc", [128, 1])
    v_fn = din("v_fn", [128, D])
    c_inv = din("c_inv", [128, 32])
    c_ident = din("c_ident", [128, 128], BF16)
    c_tri = din("c_tri", [128, 128], BF16)
    out = nc.dram_tensor("out", [T, D], F32, kind="ExternalOutput").ap()
    dbg_out = {}

    st = contextlib.ExitStack()
    A = Arena(nc, 212800)
    P = Prog()
    ps = [nc.alloc_psum_tensor(f"ps{i}", [128, 512], F32)[:] for i in range(8)]
    psb = [p.bitcast(BF16) for p in ps]

    def sb_(s):
        return slice(s * 512, (s + 1) * 512)

    def tl(i):
        return slice(i * 128, (i + 1) * 128)

    def mmg(out_ap, pairs, r, w):
        def fn(e):
            n = len(pairs)
            ins = None
            for k, (l, rh) in enumerate(pairs):
                ins = e.matmul(out_ap, l, rh, start=(k == 0), stop=(k == n - 1))
            return ins
        P.pe(fn, r, w)

    def dump(name, ap, shape, dt=F32):
        if name not in debug:
            return
        d = nc.dram_tensor("dbg_" + name, list(shape), dt, kind="ExternalOutput").ap()
        dbg_out[name] = d
        P.barrier()
        P.dma("sp", d, ap, r=[], w=["dbg_" + name])
        P.barrier()

    lnmix = A.alloc([128, 8], F32)
    lnffn = A.alloc([128, 8], F32)
    ncq = A.alloc([128, 3], F32)
    nckv = A.alloc([128, 2], F32)
    bg = A.alloc([128, 2], F32)
    nbg = A.alloc([128, 2], F32)
    gnb = A.alloc([128, 128], F32)
    gnc = A.alloc([128, 1], F32)
    invb = A.alloc([128, 32], F32)
    ident = A.alloc([128, 128], BF16)
    tri = A.alloc([128, 128], BF16)
    ones_bf = A.alloc([128, 64], BF16)
    posi = A.alloc([128, NT], I32)
    ki = A.alloc([128, NT, 16], I32)
    for dst, src, k in ((ident, c_ident, "ident"), (lnmix, v_lnmix, "lnmix"), (bg, v_bg, "bg"), (lnffn, v_lnffn, "lnffn"), (ncq, v_ncq, "ncq"),
                        (nckv, v_nckv, "nckv"), (gnb, v_gn, "gnb"), (gnc, v_gnc, "gnc"), (invb, c_inv, "invb"), (tri, c_tri, "tri")):
        P.dma("pool", dst, src, r=[], w=[k])
    P.dve(lambda e: e.tensor_scalar(out=nbg, in0=bg, scalar1=-1.0, scalar2=None, op0=ALU.mult), r=["bg"], w=["nbg"])
    P.dve(lambda e: e.memset(ones_bf, 1.0), r=[], w=["ones"])

    uT = A.alloc([128, 8, T], BF16)
    off_OGT = (A.top + 63) // 64 * 64
    OGT = A.alloc([128, 4, T], BF16)
    base_mark = A.top

    def make_nt(dstT, gain, tagp, ntiles, banks=(0, 1)):
        xn = [A.alloc([128, D], BF16) for _ in range(2)]
        sqj = A.alloc([128, D], BF16)
        ssq = A.alloc([128, ntiles], F32)
        lnv = A.alloc([128, ntiles], F32)
        rs = A.alloc([128, ntiles], F32)

        def part1(i, src, skeys):
            b = i % 2
            P.act(lambda e, src=src, i=i: e.activation(out=sqj, in_=src, func=AF.Square, accum_out=ssq[:, i:i + 1]),
                  r=skeys, w=[tagp + "sqj", f"{tagp}ss{i}"])
            P.act(lambda e, i=i: e.activation(out=lnv[:, i:i + 1], in_=ssq[:, i:i + 1], func=AF.Ln, scale=1.0 / D, bias=epsb[:, 0:1]),
                  r=[f"{tagp}ss{i}", "epsb"], w=[f"{tagp}ln{i}"])
            P.act(lambda e, i=i: e.activation(out=rs[:, i:i + 1], in_=lnv[:, i:i + 1], func=AF.Exp, scale=-0.5),
                  r=[f"{tagp}ln{i}"], w=[f"{tagp}rs{i}"])
            P.dve(lambda e, src=src, b=b, i=i: e.tensor_scalar(out=xn[b], in0=src, scalar1=rs[:, i:i + 1], scalar2=None, op0=ALU.mult),
                  r=skeys + [f"{tagp}rs{i}"], w=[f"{tagp}xn{b}"])

        def part2(i):
            b = i % 2
            bk = banks[b]
            pb = psb[bk]

            def tr(e, b=b, pb=pb):
                ins = None
                for c in range(8):
                    ins = e.transpose(pb[:, c * 128:(c + 1) * 128], xn[b][:, c * 128:(c + 1) * 128], ident)
                return ins
            P.pe(tr, r=[f"{tagp}xn{b}", "ident"], w=[f"ps{bk}"])
            P.dve(lambda e, pb=pb, i=i: e.tensor_tensor(out=dstT[:, :, tl(i)], in0=pb.rearrange("p (c t) -> p c t", c=8),
                                                        in1=gain[:, 0:8].unsqueeze(2).to_broadcast([128, 8, 128]), op=ALU.mult),
                  r=[f"ps{bk}", "lnmix", "lnffn"], w=[f"{tagp}T{i // 4}"])

        def step(i, src, skeys):
            part1(i, src, skeys)
            part2(i)
        step.part1 = part1
        step.part2 = part2
        return step

    epsb = A.alloc([128, 1], F32)
    P.dve(lambda e: e.memset(epsb, EPS), r=[], w=["epsb"])
    one_b = A.alloc([128, 1], F32)
    P.dve(lambda e: e.memset(one_b, 1.0), r=[], w=["one_b"])
    base_mark = A.top

    A1_BASE = 193000
    A.top = A1_BASE
    xt = [A.alloc([128, D], F32) for _ in range(3)]

    nt_u = make_nt(uT, lnmix, "u", NT)
    for i in range(NT):
        b = i % 3
        P.dma("sp", xt[b], x[tl(i), :], r=[], w=[f"xt{b}"])
        nt_u(i, xt[b], [f"xt{b}"])
    dump("uT", uT, [128, 8, T], BF16)
    A.top = base_mark
    if stop_after == "a1":
        return finish(nc, st, P, out, dbg_out, A)

    m0 = A.top
    qz = A.alloc([128, 4, T], BF16)
    ktT = A.alloc([128, 2, T], BF16)
    ktok = A.alloc([128, NT, 256], BF16)
    vg = A.alloc([128, NT, 512], BF16)
    sog = A.alloc([128, NT, 512], BF16)
    ebl = A.alloc([128, 2, NT], F32)
    Sall = A.alloc([128, NT - 1, 2, 256], BF16)
    Z = A.alloc([128, 2, 256], F32)
    m1 = A.top
    glrT = A.alloc([16, T], BF16)
    wlr = A.alloc([128, 8, 16], BF16)
    wg2 = A.alloc([16, 256], BF16)
    rmask = A.alloc([128, T], BF16)
    bufA = A.alloc([128, T], F32)
    bufB = A.alloc([128, T], F32)
    bufC = A.alloc([128, T], F32)
    tmpE = [A.alloc([128, 512], F32) for _ in range(2)]
    slab = [A.alloc([128, 8, 512], BF16) for _ in range(2)]

    P.dma("pool", wlr, w_in[:, O_GLR:O_GLR + 16].rearrange("(c p) n -> p c n", p=128), r=[], w=["wlr"])
    P.dma("pool", wg2, w_g2, r=[], w=["wg2"])
    P.dma("pool", slab[0], w_in[:, O_GQ:O_GQ + 512].rearrange("(c p) n -> p c n", p=128), r=[], w=["slab0"])
    P.dma("pool", slab[1], w_in[:, O_GV:O_GV + 512].rearrange("(c p) n -> p c n", p=128), r=[], w=["slab1"])
    sv_top = A.top
    A.top = A1_BASE
    slabM = A.alloc([128, 8, 672], BF16)
    wuq = A.alloc([128, 3, 768], BF16)
    wukv = A.alloc([128, 2, 1024], BF16)
    A.top = sv_top
    a1_keys = ["xt0", "xt1", "xt2", "uxn0", "uxn1", "usqj"] + [f"u{t}{i}" for t in ("ss", "ln", "rs") for i in range(NT)]
    P.dma("pool", slabM, w_in[:, 0:672].rearrange("(c p) n -> p c n", p=128), r=[], w=["slabM"] + a1_keys)
    P.dma("pool", wuq, w_uq.rearrange("(c p) n -> p c n", p=128), r=[], w=["wuq"] + a1_keys)
    P.dma("pool", wukv, w_ukv.rearrange("(c p) n -> p c n", p=128), r=[], w=["wukv"] + a1_keys)
    P.dve(lambda e: e.memset(qz, 0.0), r=[], w=["qz"])
    P.dve(lambda e: e.memset(rmask, 1.0), r=[], w=["rmask"])
    P.dve(lambda e: e.memset(rmask.rearrange("p (c t) -> p c t", t=128)[:, :, 0:1], 0.0), r=[], w=["rmask"])
    for s in range(NSB):
        b = s % 2
        mmg(ps[b][0:16, :], [(wlr[:, c, :], uT[:, c, sb_(s)]) for c in range(8)], r=["wlr", f"uT{s}"], w=[f"ps{b}"])
        P.act(lambda e, b=b, s=s: e.activation(out=glrT[:, sb_(s)], in_=ps[b][0:16, :], func=AF.Copy), r=[f"ps{b}"], w=["glrT"])
    def v_tile(i):
        b = 4 + i % 2
        mmg(ps[b], [(uT[:, c, tl(i)], slab[1][:, c, :]) for c in range(8)], r=["slab1", f"uT{i // 4}"], w=[f"ps{b}"])
        P.act(lambda e, b=b, i=i: e.activation(out=vg[:, i, :], in_=ps[b], func=AF.Copy), r=[f"ps{b}"], w=["vg"])

    for th in range(2):
        for s in range(NSB):
            b = s % 2
            mmg(ps[b], [(wg2[:, th * 128:(th + 1) * 128], glrT[:, sb_(s)])], r=["wg2", "glrT"], w=[f"ps{b}"])
            P.act(lambda e, b=b, th=th: e.activation(out=tmpE[b], in_=ps[b], func=AF.Exp, scale=-1.0, bias=nbg[:, th:th + 1]),
                  r=[f"ps{b}", "nbg"], w=[f"tmpE{b}"])
            P.act(lambda e, b=b, s=s: e.activation(out=bufA[:, sb_(s)], in_=tmpE[b], func=AF.Ln, bias=one_b[:, 0:1]),
                  r=[f"tmpE{b}", "one_b"], w=["bufA"])
        for i in range(th * 8, th * 8 + 8):
            v_tile(i)
        P.dve(lambda e: e.tensor_tensor_scan(out=bufB, data0=rmask, data1=bufA, initial=0.0, op0=ALU.mult, op1=ALU.add),
              r=["bufA", "rmask"], w=["bufB"])
        P.act(lambda e: e.activation(out=bufA, in_=bufB, func=AF.Exp, scale=-1.0 / 16.0), r=["bufB"], w=["bufA"])
        P.act(lambda e: e.activation(out=bufC, in_=bufB, func=AF.Exp, scale=1.0 / 16.0), r=["bufB"], w=["bufC"])
        P.dve(lambda e, th=th: e.tensor_copy(out=ebl[:, th, :], in_=bufA.rearrange("p (c t) -> p c t", t=128)[:, :, 127]),
              r=["bufA"], w=["ebl"])
        for s in range(NSB):
            b = s % 2
            mmg(ps[b], [(slab[0][:, c, th * 128:(th + 1) * 128], uT[:, c, sb_(s)]) for c in range(8)], r=["slab0", f"uT{s}"], w=[f"ps{b}"])
            for hh in range(2):
                pr = slice(hh * 64, hh * 64 + 64)
                P.dve(lambda e, b=b, s=s, th=th, hh=hh, pr=pr: e.scalar_tensor_tensor(out=qz[pr, 2 * th + hh, sb_(s)], in0=ps[b][pr, :], scalar=0.125,
                                                                                     in1=bufA[pr, sb_(s)], op0=ALU.mult, op1=ALU.mult),
                      r=[f"ps{b}", "bufA"], w=["qz"])
            b2 = 2 + s % 2
            mmg(ps[b2], [(slab[0][:, c, 256 + th * 128:256 + (th + 1) * 128], uT[:, c, sb_(s)]) for c in range(8)], r=["slab0", f"uT{s}"], w=[f"ps{b2}"])
            P.dve(lambda e, b2=b2, s=s, th=th: e.tensor_tensor(out=ktT[:, th, sb_(s)], in0=ps[b2], in1=bufC[:, sb_(s)], op=ALU.mult),
                  r=[f"ps{b2}", "bufC"], w=["ktT"])
    for g in range(4):
        b = 4 + g % 2
        pb = psb[b]

        def trk(e, g=g, pb=pb):
            ins = None
            for ii in range(4):
                for th in range(2):
                    k = ii * 2 + th
                    ins = e.transpose(pb[:, k * 128:(k + 1) * 128], ktT[:, th, tl(g * 4 + ii)], ident)
            return ins
        P.pe(trk, r=["ktT", "ident"], w=[f"ps{b}"])
        P.act(lambda e, g=g, pb=pb: e.activation(out=ktok[:, g * 4:(g + 1) * 4, :], in_=pb.rearrange("p (a b) -> p a b", a=4), func=AF.Copy),
              r=[f"ps{b}"], w=["ktok"])
    def d0_step(c):
        pk = 6 + c % 2

        def kv_mm(e, c=c, pk=pk):
            ins = None
            for th in range(2):
                ins = e.matmul(ps[pk][:, th * 256:(th + 1) * 256], ktok[:, c, th * 128:(th + 1) * 128],
                               vg[:, c, th * 256:(th + 1) * 256], start=True, stop=True)
            return ins
        P.pe(kv_mm, r=["ktok", "vg"], w=[f"ps{pk}"])
        for th in range(2):
            if c == 0:
                P.dve(lambda e, th=th, pk=pk: e.tensor_copy(out=Z[:, th, :], in_=ps[pk][:, th * 256:(th + 1) * 256]), r=[f"ps{pk}"], w=[f"Z{th}"])
            else:
                P.dve(lambda e, th=th, c=c, pk=pk: e.scalar_tensor_tensor(out=Z[:, th, :], in0=Z[:, th, :], scalar=ebl[:, th, c - 1:c],
                                                                          in1=ps[pk][:, th * 256:(th + 1) * 256], op0=ALU.mult, op1=ALU.add),
                      r=[f"ps{pk}", "ebl", f"Z{th}"], w=[f"Z{th}"])
            P.dve(lambda e, th=th, c=c: e.tensor_scalar(out=Sall[:, c, th, :], in0=Z[:, th, :], scalar1=ebl[:, th, c:c + 1], scalar2=None, op0=ALU.mult),
                  r=[f"Z{th}", "ebl"], w=["Sall"])

    P.dma("pool", slab[1], w_in[:, O_GOG:O_GOG + 512].rearrange("(c p) n -> p c n", p=128), r=[], w=["slab1"])
    assert A.top <= A1_BASE, (A.top, A1_BASE)
    for i in range(NT):
        b = 2 + i % 2
        mmg(ps[b], [(uT[:, c, tl(i)], slab[1][:, c, :]) for c in range(8)], r=["slab1", f"uT{i // 4}"], w=[f"ps{b}"])
        P.act(lambda e, b=b: e.activation(out=tmpE[b - 2], in_=ps[b], func=AF.Exp, scale=-1.0), r=[f"ps{b}"], w=[f"tmpE{b - 2}"])
        P.act(lambda e, b=b: e.activation(out=tmpE[b - 2], in_=tmpE[b - 2], func=AF.Ln, bias=one_b[:, 0:1]), r=[f"tmpE{b - 2}", "one_b"], w=[f"tmpE{b - 2}"])
        P.act(lambda e, b=b: e.activation(out=tmpE[b - 2], in_=tmpE[b - 2], func=AF.Exp, scale=-1.0), r=[f"tmpE{b - 2}"], w=[f"tmpE{b - 2}"])
        P.dve(lambda e, b=b, i=i: e.tensor_tensor(out=sog[:, i, :], in0=ps[b], in1=tmpE[b - 2], op=ALU.mult),
              r=[f"ps{b}", f"tmpE{b - 2}"], w=["sog"])
        if i < NT - 1:
            d0_step(i)
    dump("qz", qz, [128, 4, T], BF16)
    dump("ktT", ktT, [128, 2, T], BF16)
    dump("ktok", ktok, [128, NT, 256], BF16)
    dump("vg", vg, [128, NT, 512], BF16)
    dump("sog", sog, [128, NT, 512], BF16)
    dump("ebl", ebl, [128, 2, NT], F32)
    P.barrier()
    A.top = m1
    if stop_after == "a2":
        return finish(nc, st, P, out, dbg_out, A)

    attn_sb = [A.alloc([128, 4, 128], BF16) for _ in range(2)]
    ssg = A.alloc([128, 4 * NT], F32)
    lng = A.alloc([128, 4 * NT], F32)
    rsg = A.alloc([128, 4 * NT], F32)
    sqj2 = A.alloc([128, 4, 128], BF16)
    ogt = [A.alloc([128, 512], BF16) for _ in range(2)]
    d_end = A.top
    A.top = m0
    QT = A.alloc([96, 8, T], BF16)
    KT = A.alloc([96, 8, T], BF16)
    Vt4 = A.alloc([128, NT, 8, 128], BF16)
    mB = A.top
    cqT = A.alloc([128, 3, T], BF16)
    ckvT = A.alloc([128, 2, T], BF16)
    rq = A.alloc([128, NT], F32)
    rkv = A.alloc([128, NT], F32)
    krtok = A.alloc([128, NT, 32], F32)
    cs2 = A.alloc([128, NT, 32], F32)
    sn2 = A.alloc([128, NT, 32], F32)
    mB2 = A.top
    ssq = A.alloc([128, NT], F32)
    ssk = A.alloc([128, 2 * NT], F32)
    sskk = A.alloc([128, NT], F32)
    lq = A.alloc([128, NT], F32)
    lk = A.alloc([128, NT], F32)
    sqj3 = A.alloc([128, 3, 384], BF16)
    posf = A.alloc([128, NT], F32)
    ang = A.alloc([128, NT, 16], F32)
    angl = A.alloc([128, NT, 16], F32)
    ra = A.alloc([128, NT, 16], F32)
    rb = A.alloc([128, NT, 16], F32)
    rbs = A.alloc([128, NT, 16], F32)
    rbc = A.alloc([128, NT, 16], F32)
    kf = A.alloc([128, NT, 16], F32)

    C1 = 6.28125
    C2 = float(2.0 * np.pi - 6.28125)

    def b1_rope_setup():
        P.dma("sp", posi, pos, r=[], w=["posi"])
        P.dve(lambda e: e.tensor_copy(out=posf, in_=posi), r=["posi"], w=["posf"])
        P.dve(lambda e: e.tensor_tensor(out=ang, in0=posf.unsqueeze(2).to_broadcast([128, NT, 16]),
                                        in1=invb[:, 0:16].unsqueeze(1).to_broadcast([128, NT, 16]), op=ALU.mult), r=["posf", "invb"], w=["ang"])
        P.dve(lambda e: e.tensor_tensor(out=angl, in0=posf.unsqueeze(2).to_broadcast([128, NT, 16]),
                                        in1=invb[:, 16:32].unsqueeze(1).to_broadcast([128, NT, 16]), op=ALU.mult), r=["posf", "invb"], w=["angl"])

    def b1_rope_reduce(shift, rout, nm):
        P.dve(lambda e: e.tensor_scalar(out=ra, in0=ang, scalar1=shift, scalar2=None, op0=ALU.add), r=["ang"], w=["ra"])
        P.dve(lambda e: e.tensor_scalar(out=kf, in0=ra, scalar1=1.0 / (2.0 * PI), scalar2=None, op0=ALU.mult), r=["ra"], w=["kf"])
        P.dve(lambda e: e.tensor_copy(out=ki, in_=kf), r=["kf"], w=["ki"])
        P.dve(lambda e: e.tensor_copy(out=kf, in_=ki), r=["ki"], w=["kf"])
        P.dve(lambda e: e.scalar_tensor_tensor(out=rb, in0=kf, scalar=-C1, in1=ra, op0=ALU.mult, op1=ALU.add), r=["kf", "ra"], w=["rb"])
        P.dve(lambda e: e.scalar_tensor_tensor(out=ra, in0=kf, scalar=-C2, in1=rb, op0=ALU.mult, op1=ALU.add), r=["kf", "rb"], w=["ra"])
        P.dve(lambda e: e.tensor_tensor(out=ra, in0=ra, in1=angl, op=ALU.add), r=["ra", "angl"], w=["ra"])
        P.dve(lambda e: e.tensor_scalar(out=kf, in0=ra, scalar1=PI, scalar2=None, op0=ALU.is_gt), r=["ra"], w=["kf"])
        P.dve(lambda e: e.scalar_tensor_tensor(out=rb, in0=kf, scalar=-2.0 * PI, in1=ra, op0=ALU.mult, op1=ALU.add), r=["kf", "ra"], w=["rb"])
        P.dve(lambda e: e.tensor_scalar(out=kf, in0=rb, scalar1=-PI, scalar2=None, op0=ALU.is_lt), r=["rb"], w=["kf"])
        P.dve(lambda e: e.scalar_tensor_tensor(out=ra, in0=kf, scalar=2.0 * PI, in1=rb, op0=ALU.mult, op1=ALU.add), r=["kf", "rb"], w=["ra"])
        P.dve(lambda e: e.tensor_scalar(out=rout, in0=ra, scalar1=3.1415925, scalar2=-3.1415925, op0=ALU.min, op1=ALU.max), r=["ra"], w=[nm])

    def b1_rope_sin():
        P.act(lambda e: e.activation(out=sn2[:, :, 16:32], in_=rbs, func=AF.Sin), r=["rbs"], w=["sin"])
        P.act(lambda e: e.activation(out=cs2[:, :, 0:16], in_=rbc, func=AF.Sin), r=["rbc"], w=["cos"])
        P.act(lambda e: e.activation(out=cs2[:, :, 16:32], in_=rbc, func=AF.Sin), r=["rbc"], w=["cos"])
        P.dve(lambda e: e.tensor_scalar(out=sn2[:, :, 0:16], in0=sn2[:, :, 16:32], scalar1=-1.0, scalar2=None, op0=ALU.mult), r=["sin"], w=["sin"])

    def b1_stats(i):
        b0, b1 = 6, 7
        mmg(ps[b0], [(uT[:, c, tl(i)], slabM[:, c, 0:512]) for c in range(8)], r=["slabM", f"uT{i // 4}"], w=[f"ps{b0}"])
        mmg(ps[b1][:, 0:160], [(uT[:, c, tl(i)], slabM[:, c, 512:672]) for c in range(8)], r=["slabM", f"uT{i // 4}"], w=[f"ps{b1}"])
        P.act(lambda e, i=i, b0=b0: e.activation(out=sqj3[:, 0, :], in_=ps[b0][:, 0:384], func=AF.Square, accum_out=ssq[:, i:i + 1]),
              r=[f"ps{b0}"], w=["sqj3_0", f"ssq{i}"])
        P.act(lambda e, i=i, b0=b0: e.activation(out=sqj3[:, 1, 0:128], in_=ps[b0][:, 384:512], func=AF.Square, accum_out=ssk[:, 2 * i:2 * i + 1]),
              r=[f"ps{b0}"], w=["sqj3_1", f"ssk{i}"])
        P.act(lambda e, i=i, b1=b1: e.activation(out=sqj3[:, 2, 0:128], in_=ps[b1][:, 0:128], func=AF.Square, accum_out=ssk[:, 2 * i + 1:2 * i + 2]),
              r=[f"ps{b1}"], w=["sqj3_2", f"ssk{i}"])
        P.act(lambda e, i=i, b1=b1: e.activation(out=krtok[:, i, :], in_=ps[b1][:, 128:160], func=AF.Copy), r=[f"ps{b1}"], w=["krtok"])

    def b1_fin():
        allss = [f"ssq{i}" for i in range(NT)] + [f"ssk{i}" for i in range(NT)]
        P.dve(lambda e: e.tensor_tensor(out=sskk, in0=ssk.rearrange("p (i two) -> p i two", two=2)[:, :, 0],
                                        in1=ssk.rearrange("p (i two) -> p i two", two=2)[:, :, 1], op=ALU.add), r=allss, w=["sskk"])
        P.act(lambda e: e.activation(out=lq, in_=ssq, func=AF.Ln, scale=1.0 / 384.0, bias=epsb[:, 0:1]), r=allss + ["epsb"], w=["lq"])
        P.act(lambda e: e.activation(out=lk, in_=sskk, func=AF.Ln, scale=1.0 / 256.0, bias=epsb[:, 0:1]), r=["sskk", "epsb"], w=["lk"])
        P.act(lambda e: e.activation(out=rq, in_=lq, func=AF.Exp, scale=-0.5), r=["lq"], w=["rq0"])
        P.dve(lambda e: e.tensor_scalar(out=rq, in0=rq, scalar1=float(96.0 ** -0.5), scalar2=None, op0=ALU.mult), r=["rq0"], w=["rq"])
        P.act(lambda e: e.activation(out=rkv, in_=lk, func=AF.Exp, scale=-0.5), r=["lk"], w=["rkv"])

    def b1_feat(ct, s):
        b = 5
        mmg(ps[b], [(slabM[:, c, ct * 128:(ct + 1) * 128], uT[:, c, sb_(s)]) for c in range(8)], r=["slabM", f"uT{s}"], w=[f"ps{b}"])
        if ct < 3:
            P.dve(lambda e, b=b, ct=ct, s=s: e.tensor_scalar(out=cqT[:, ct, sb_(s)], in0=ps[b], scalar1=ncq[:, ct:ct + 1], scalar2=None, op0=ALU.mult),
                  r=[f"ps{b}", "ncq"], w=["cqT"])
        else:
            P.dve(lambda e, b=b, ct=ct, s=s: e.tensor_scalar(out=ckvT[:, ct - 3, sb_(s)], in0=ps[b], scalar1=nckv[:, ct - 3:ct - 2], scalar2=None, op0=ALU.mult),
                  r=[f"ps{b}", "nckv"], w=["ckvT"])

    b1_items = []
    feats = [(ct, s) for ct in range(5) for s in range(NSB)]
    extra = {1: b1_rope_setup, 3: lambda: b1_rope_reduce(0.0, rbs, "rbs"), 6: lambda: b1_rope_reduce(PI / 2.0, rbc, "rbc"), 9: b1_rope_sin}
    for k in range(NT):
        b1_items.append(lambda k=k: (b1_stats(k), extra[k]() if k in extra else None))
        b1_items.append(lambda k=k: b1_feat(*feats[k]))
    for k in range(NT, len(feats)):
        b1_items.append(lambda k=k: b1_feat(*feats[k]))
    b1_items.append(b1_fin)
    assert d_end <= mB, (d_end, mB)

    def stage_a(c):
        pa = c % 2

        def attn_mm(e, c=c, pa=pa):
            ins = None
            for h in range(4):
                ins = e.matmul(ps[pa][:, h * 128:(h + 1) * 128], ktT[:, h // 2, tl(c)], qz[:, h, tl(c)], start=True, stop=True)
            return ins
        P.pe(attn_mm, r=["qz", "ktT"], w=[f"ps{pa}"])
        P.dve(lambda e, pa=pa: e.tensor_tensor(out=attn_sb[pa], in0=ps[pa].rearrange("p (h t) -> p h t", h=4),
                                               in1=tri.unsqueeze(1).to_broadcast([128, 4, 128]), op=ALU.mult),
              r=[f"ps{pa}", "tri"], w=[f"attn{pa}"])

    def stage_b(c):
        po = 2 + c % 2
        ab = c % 2

        def o_mm(e, c=c, po=po, ab=ab):
            ins = None
            for h in range(4):
                if c > 0:
                    e.matmul(ps[po][:, h * 128:(h + 1) * 128], qz[:, h, tl(c)], Sall[:, c - 1, h // 2, (h % 2) * 128:(h % 2 + 1) * 128],
                             start=True, stop=False)
                ins = e.matmul(ps[po][:, h * 128:(h + 1) * 128], attn_sb[ab][:, h, :], vg[:, c, h * 128:(h + 1) * 128],
                               start=(c == 0), stop=True)
            return ins
        P.pe(o_mm, r=[f"attn{ab}", "vg", "qz", "Sall"], w=[f"ps{po}"])
        for h in range(4):
            P.act(lambda e, h=h, c=c, po=po: e.activation(out=sqj2[:, h, :], in_=ps[po][:, h * 128:(h + 1) * 128], func=AF.Square,
                                                          accum_out=ssg[:, c * 4 + h:c * 4 + h + 1]),
                  r=[f"ps{po}"], w=[f"sqj2_{h}", f"ssg{c}"])
        P.act(lambda e, c=c: e.activation(out=lng[:, c * 4:c * 4 + 4], in_=ssg[:, c * 4:c * 4 + 4], func=AF.Ln, scale=1.0 / 128.0, bias=epsb[:, 0:1]),
              r=[f"ssg{c}", "epsb"], w=[f"lng{c}"])
        P.act(lambda e, c=c: e.activation(out=rsg[:, c * 4:c * 4 + 4], in_=lng[:, c * 4:c * 4 + 4], func=AF.Exp, scale=-0.5),
              r=[f"lng{c}"], w=[f"rsg{c}"])
        for h in range(4):
            P.dve(lambda e, h=h, c=c, po=po, ab=ab: e.scalar_tensor_tensor(out=ogt[ab][:, h * 128:(h + 1) * 128], in0=ps[po][:, h * 128:(h + 1) * 128],
                                                                           scalar=rsg[:, c * 4 + h:c * 4 + h + 1], in1=sog[:, c, h * 128:(h + 1) * 128],
                                                                           op0=ALU.mult, op1=ALU.mult),
                  r=[f"ps{po}", f"rsg{c}", "sog"], w=[f"ogt{ab}"])

    def stage_c(c):
        ab = c % 2
        pt = 4
        pbt = psb[pt]

        def tro(e, ab=ab, pbt=pbt):
            ins = None
            for h in range(4):
                ins = e.transpose(pbt[:, h * 128:(h + 1) * 128], ogt[ab][:, h * 128:(h + 1) * 128], ident)
            return ins
        P.pe(tro, r=[f"ogt{ab}", "ident"], w=[f"ps{pt}"])
        P.act(lambda e, c=c, pbt=pbt: e.activation(out=OGT[:, :, tl(c)], in_=pbt[:, 0:512].rearrange("p (h t) -> p h t", h=4), func=AF.Copy,
                                                   scale=gnc[:, 0:1]),
              r=[f"ps{pt}", "gnc"], w=["OGT"])

    for c in range(NT + 2):
        if c < NT:
            stage_a(c)
        if 1 <= c <= NT:
            stage_b(c - 1)
        if c >= 2:
            stage_c(c - 2)
        for _ in range(B1_PER_STAGE):
            if b1_items:
                b1_items.pop(0)()
    while b1_items:
        b1_items.pop(0)()
    dump("OGT", OGT, [128, 4, T], BF16)

    if stop_after == "gla":
        return finish(nc, st, P, out, dbg_out, A)

    P.barrier()
    A.top = mB2
    qs9 = [A.alloc([128, 9, 32], F32) for _ in range(2)]
    ra9 = [A.alloc([128, 9, 32], F32) for _ in range(2)]
    rb9 = [A.alloc([128, 9, 32], F32) for _ in range(2)]
    qrot = [A.alloc([128, 8, 96], BF16) for _ in range(2)]
    krot = [A.alloc([128, 8, 96], BF16) for _ in range(2)]
    kro = A.alloc([128, 32], BF16)
    P.dve(lambda e: e.memset(Vt4[:, :, 0:8:2, 64:128], 1.0), r=[], w=["Vones"])
    P.dve(lambda e: e.memset(Vt4[:, :, 1:8:2, 0:64], 1.0), r=[], w=["Vones"])
    assert A.top <= A1_BASE, (A.top, A1_BASE)

    def b_proj(i):
        b = i % 2
        mmg(ps[0][:, 0:384], [(cqT[:, c, tl(i)], wuq[:, c, 0:384]) for c in range(3)], r=["cqT", "wuq"], w=["ps0"])
        mmg(ps[1][:, 0:384], [(cqT[:, c, tl(i)], wuq[:, c, 384:768]) for c in range(3)], r=["cqT", "wuq"], w=["ps1"])
        mmg(ps[2], [(ckvT[:, c, tl(i)], wukv[:, c, 0:512]) for c in range(2)], r=["ckvT", "wukv"], w=["ps2"])
        mmg(ps[3], [(ckvT[:, c, tl(i)], wukv[:, c, 512:1024]) for c in range(2)], r=["ckvT", "wukv"], w=["ps3"])
        for half in range(2):
            pq_ = ps[half][:, 0:384].rearrange("p (h d) -> p h d", h=4)
            hs = slice(half * 4, half * 4 + 4)
            P.act(lambda e, i=i, b=b, pq_=pq_, hs=hs: e.activation(out=qrot[b][:, hs, 0:64], in_=pq_[:, :, 0:64], func=AF.Copy, scale=rq[:, i:i + 1]),
                  r=[f"ps{half}", "rq"], w=[f"qrot{b}"])
            P.act(lambda e, i=i, b=b, pq_=pq_, hs=hs: e.activation(out=qs9[b][:, hs, :], in_=pq_[:, :, 64:96], func=AF.Copy, scale=rq[:, i:i + 1]),
                  r=[f"ps{half}", "rq"], w=[f"qs9{b}"])
        for half in range(2):
            pv = ps[2 + half].rearrange("p (h d) -> p h d", h=4)
            hs = slice(half * 4, half * 4 + 4)
            P.act(lambda e, i=i, b=b, pv=pv, hs=hs: e.activation(out=krot[b][:, hs, 0:64], in_=pv[:, :, 0:64], func=AF.Copy, scale=rkv[:, i:i + 1]),
                  r=[f"ps{2 + half}", "rkv"], w=[f"krot{b}"])
            P.act(lambda e, i=i, half=half, pv=pv: e.activation(out=Vt4[:, i, half * 4:half * 4 + 4:2, 0:64],
                                                                in_=pv[:, 0:4:2, 64:128], func=AF.Copy, scale=rkv[:, i:i + 1]),
                  r=[f"ps{2 + half}", "rkv"], w=["Vt"])
            P.act(lambda e, i=i, half=half, pv=pv: e.activation(out=Vt4[:, i, half * 4 + 1:half * 4 + 4:2, 64:128],
                                                                in_=pv[:, 1:4:2, 64:128], func=AF.Copy, scale=rkv[:, i:i + 1]),
                  r=[f"ps{2 + half}", "rkv"], w=["Vt"])
        P.dve(lambda e, i=i, b=b: e.tensor_copy(out=qs9[b][:, 8, :], in_=krtok[:, i, :]), r=["krtok"], w=[f"qs9{b}"])
        P.dve(lambda e, i=i, b=b: e.tensor_tensor(out=ra9[b], in0=qs9[b], in1=cs2[:, i, :].unsqueeze(1).to_broadcast([128, 9, 32]), op=ALU.mult),
              r=[f"qs9{b}", "cos"], w=[f"ra9{b}"])
        P.dve(lambda e, i=i, b=b: e.tensor_tensor(out=rb9[b][:, :, 0:16], in0=qs9[b][:, :, 16:32],
                                                  in1=sn2[:, i, 0:16].unsqueeze(1).to_broadcast([128, 9, 16]), op=ALU.mult),
              r=[f"qs9{b}", "sin"], w=[f"rb9{b}"])
        P.dve(lambda e, i=i, b=b: e.tensor_tensor(out=rb9[b][:, :, 16:32], in0=qs9[b][:, :, 0:16],
                                                  in1=sn2[:, i, 16:32].unsqueeze(1).to_broadcast([128, 9, 16]), op=ALU.mult),
              r=[f"qs9{b}", "sin"], w=[f"rb9{b}"])
        P.dve(lambda e, b=b: e.tensor_tensor(out=qrot[b][:, :, 64:96], in0=ra9[b][:, 0:8, :], in1=rb9[b][:, 0:8, :], op=ALU.add),
              r=[f"ra9{b}", f"rb9{b}"], w=[f"qrot{b}"])
        P.dve(lambda e, b=b: e.tensor_tensor(out=kro, in0=ra9[b][:, 8, :], in1=rb9[b][:, 8, :], op=ALU.add), r=[f"ra9{b}", f"rb9{b}"], w=["kro"])
        P.dve(lambda e, b=b: e.tensor_copy(out=krot[b][:, :, 64:96], in_=kro.unsqueeze(1).to_broadcast([128, 8, 32])), r=["kro"], w=[f"krot{b}"])

    def b_trans(i):
        b = i % 2
        pq = psb[4 + b]

        def trq(e, b=b, pq=pq):
            ins = None
            for h in range(8):
                ins = e.transpose(pq[0:96, h * 128:(h + 1) * 128], qrot[b][:, h, :], ident)
            return ins
        P.pe(trq, r=[f"qrot{b}", "ident"], w=[f"ps{4 + b}"])
        P.dve(lambda e, i=i, pq=pq: e.tensor_copy(out=QT[:, :, tl(i)], in_=pq[0:96, :].rearrange("p (h t) -> p h t", h=8)),
              r=[f"ps{4 + b}"], w=["QT"])
        pk = psb[6 + b]

        def trk2(e, b=b, pk=pk):
            ins = None
            for h in range(8):
                ins = e.transpose(pk[0:96, h * 128:(h + 1) * 128], krot[b][:, h, :], ident)
            return ins
        P.pe(trk2, r=[f"krot{b}", "ident"], w=[f"ps{6 + b}"])
        P.dve(lambda e, i=i, pk=pk: e.tensor_copy(out=KT[:, :, tl(i)], in_=pk[0:96, :].rearrange("p (h t) -> p h t", h=8)),
              r=[f"ps{6 + b}"], w=["KT"])

    for i in range(NT + 1):
        if i < NT:
            b_proj(i)
        if i >= 1:
            b_trans(i - 1)
    dump("QT", QT, [96, 8, T], BF16)
    dump("KT", KT, [96, 8, T], BF16)
    dump("Vt", Vt4, [128, NT, 8, 128], BF16)
    P.barrier()
    A.top = mB

    OT = A.alloc([128, 4, T], BF16)
    NSC = 6
    PT = [A.alloc([128, 512], BF16) for _ in range(NSC)]
    rinv = A.alloc([128, 512], F32)
    lnl = A.alloc([128, 512], F32)
    woa = A.alloc([128, 4, 1024], BF16)
    wob = A.alloc([128, 4, 1024], BF16)
    gsl = [A.alloc([128, 8, 256], BF16) for _ in range(2)]
    off_e1w = A.top
    P.dma("pool", woa, w_oa.rearrange("(c p) n -> p c n", p=128), r=[], w=["woa"])
    P.dma("pool", wob, w_ob.rearrange("(c p) n -> p c n", p=128), r=[], w=["wob"])

    def load_gsl(ft):
        g = ft % 2
        P.dma("pool", gsl[g][:, :, 0:128], w_in[:, O_GA + ft * 128:O_GA + (ft + 1) * 128].rearrange("(c p) n -> p c n", p=128), r=[], w=[f"gslA{g}"])
        P.dma("pool", gsl[g][:, :, 128:256], w_in[:, O_GB + ft * 128:O_GB + (ft + 1) * 128].rearrange("(c p) n -> p c n", p=128), r=[], w=[f"gslB{g}"])
    load_gsl(0)
    load_gsl(1)
    for p in range(4):
        for Q in range(NSB):
            nj = 4 * Q + 4
            items = [(hh, j) for hh in range(2) for j in range(nj)]
            obank = (6, 7)

            def geom(j, Q=Q):
                m = j - 4 * Q if j >= 4 * Q else 0
                c0 = m * 128
                return c0, 512 - c0

            def emit_qk(k, p=p, Q=Q, items=items):
                hh, j = items[k]
                h = 2 * p + hh
                c0, N = geom(j)
                sbk = k % NSC
                mmg(ps[sbk][:, 0:N], [(KT[:, h, tl(j)], QT[:, h, Q * 512 + c0:(Q + 1) * 512])], r=["KT", "QT"], w=[f"ps{sbk}"])
                P.act(lambda e, sbk=sbk, N=N: e.activation(out=PT[sbk][:, 0:N], in_=ps[sbk][:, 0:N], func=AF.Exp), r=[f"ps{sbk}"], w=[f"PT{sbk}"])
                if j >= 4 * Q:
                    P.dve(lambda e, sbk=sbk: e.tensor_tensor(out=PT[sbk][:, 0:128], in0=PT[sbk][:, 0:128], in1=tri, op=ALU.mult),
                          r=[f"PT{sbk}", "tri"], w=[f"PT{sbk}"])

            def emit_pv(k, p=p, Q=Q, items=items, nj=nj, obank=obank):
                hh, j = items[k]
                h = 2 * p + hh
                c0, N = geom(j)
                sbk = k % NSC
                bk = obank[hh]
                P.pe(lambda e, bk=bk, c0=c0, N=N, h=h, sbk=sbk, j=j: e.matmul(ps[bk][:, c0:512], Vt4[:, j, h, :], PT[sbk][:, 0:N],
                                                                            start=(j == 0), stop=(j == nj - 1)),
                     r=[f"PT{sbk}", "Vt", "Vones"], w=[f"ps{bk}"])
            n = len(items)
            LA = NSC - 1
            for k in range(n + LA):
                if k < n:
                    emit_qk(k)
                if k >= LA:
                    emit_pv(k - LA)
            for hh in range(2):
                bk = obank[hh]
                lr = slice(64, 128) if hh == 0 else slice(0, 64)
                orow = slice(0, 64) if hh == 0 else slice(64, 128)
                P.act(lambda e, bk=bk, lr=lr: e.activation(out=lnl[lr, :], in_=ps[bk][lr, :], func=AF.Ln), r=[f"ps{bk}"], w=[f"lnl{hh}"])
                P.act(lambda e, lr=lr: e.activation(out=lnl[lr, :], in_=lnl[lr, :], func=AF.Exp, scale=-1.0), r=[f"lnl{hh}"], w=[f"lnl{hh}"])
                P.dve(lambda e, lr=lr, orow=orow: e.tensor_copy(out=rinv[orow, :], in_=lnl[lr, :]), r=[f"lnl{hh}"], w=[f"rinv{hh}"])
                P.dve(lambda e, p=p, Q=Q, bk=bk, orow=orow: e.tensor_tensor(out=OT[orow, p, sb_(Q)], in0=ps[bk][orow, :], in1=rinv[orow, :], op=ALU.mult),
                      r=[f"ps{bk}", f"rinv{hh}"], w=["OT"])
    dump("OT", OT, [128, 4, T], BF16)
    if stop_after == "attn":
        return finish(nc, st, P, out, dbg_out, A)
    P.barrier(keep=("woa", "wob", "gslA0", "gslB0", "gslA1", "gslB1"))

    A.top = m0
    mixT = A.alloc([128, 8, T], BF16)
    off_after_mix = A.top
    assert A.top <= mB, (A.top, mB)
    off_e1t = A.top
    ea = [A.alloc([128, 512], F32) for _ in range(2)]
    eb2 = [A.alloc([128, 512], F32) for _ in range(2)]
    tA = A.alloc([128, 512], F32)
    tB = A.alloc([128, 512], F32)
    off_wout = (A.top + 63) // 64 * 64
    wout = A.alloc([128, 8, 1024], BF16)
    P.dma("pool", wout, w_out.rearrange("(c p) n -> p c n", p=128), r=[], w=["wout"])
    for ft in range(8):
        g = ft % 2
        if ft >= 2:
            load_gsl(ft)
        for s in range(NSB):
            q = (ft * NSB + s) % 2
            ba, bb, bya, byb = (0, 1, 2, 3) if q == 0 else (4, 5, 6, 7)
            mmg(ps[ba], [(gsl[g][:, c, 0:128], uT[:, c, sb_(s)]) for c in range(8)], r=[f"gslA{g}", f"uT{s}"], w=[f"ps{ba}"])
            mmg(ps[bb], [(gsl[g][:, c, 128:256], uT[:, c, sb_(s)]) for c in range(8)], r=[f"gslB{g}", f"uT{s}"], w=[f"ps{bb}"])
            mmg(ps[bya], [(woa[:, pp, tl(ft)], OT[:, pp, sb_(s)]) for pp in range(4)], r=["woa", "OT"], w=[f"ps{bya}"])
            mmg(ps[byb], [(wob[:, pp, tl(ft)], OGT[:, pp, sb_(s)]) for pp in range(4)], r=["wob", "OGT"], w=[f"ps{byb}"])
            for (bk, et, nm) in ((ba, ea[q], f"ea{q}"), (bb, eb2[q], f"eb{q}")):
                P.act(lambda e, bk=bk, et=et: e.activation(out=et, in_=ps[bk], func=AF.Exp, scale=-1.0), r=[f"ps{bk}"], w=[nm])
                P.act(lambda e, et=et: e.activation(out=et, in_=et, func=AF.Ln, bias=one_b[:, 0:1]), r=[nm, "one_b"], w=[nm])
                P.act(lambda e, et=et: e.activation(out=et, in_=et, func=AF.Exp, scale=-1.0), r=[nm], w=[nm])
            P.dve(lambda e, q=q, bya=bya: e.tensor_tensor(out=tA, in0=ps[bya], in1=ea[q], op=ALU.mult), r=[f"ps{bya}", f"ea{q}"], w=["tA"])
            P.dve(lambda e, q=q, byb=byb: e.tensor_tensor(out=tB, in0=ps[byb], in1=eb2[q], op=ALU.mult), r=[f"ps{byb}", f"eb{q}"], w=["tB"])
            P.dve(lambda e, ft=ft, s=s: e.tensor_tensor(out=mixT[:, ft, sb_(s)], in0=tA, in1=tB, op=ALU.add), r=["tA", "tB"], w=["mixT"])
    dump("mixT", mixT, [128, 8, T], BF16)
    if stop_after == "mix":
        return finish(nc, st, P, out, dbg_out, A)
    P.barrier(keep=("wout",))

    h1 = A.alloc([128, NT, D], F32)
    off_after_h1 = A.top
    fsl = [A.alloc([128, 8, 256], BF16) for _ in range(2)]
    fnb = A.alloc([128, D], F32)
    ot = [A.alloc([128, D], F32) for _ in range(2)]
    ssf = A.alloc([128, NT], F32)
    lnf = A.alloc([128, NT], F32)
    rsf = A.alloc([128, NT], F32)
    sqf = A.alloc([128, D], BF16)
    off_e2t = A.top
    A.top = off_e1t
    xt2 = [A.alloc([128, D], F32) for _ in range(2)]
    assert A.top <= off_wout

    def load_fsl(f):
        g = f % 2
        P.dma("pool", fsl[g][:, :, 0:128], w_fg[:, f * 128:(f + 1) * 128].rearrange("(c p) n -> p c n", p=128), r=[], w=[f"fslG{g}"])
        P.dma("pool", fsl[g][:, :, 128:256], w_fu[:, f * 128:(f + 1) * 128].rearrange("(c p) n -> p c n", p=128), r=[], w=[f"fslU{g}"])
    load_fsl(0)
    load_fsl(1)
    P.dma("pool", fnb, v_fn, r=[], w=["fnb"])
    u2T = uT
    sv_top = A.top
    A.top = off_e2t
    nt_v = make_nt(u2T, lnffn, "v", NT, banks=(6, 7))
    off_e2t = A.top
    A.top = sv_top
    for i in range(NT):
        b = i % 2
        P.dma("sp", xt2[b], x[tl(i), :], r=[], w=[f"xt2{b}"])
        for half in range(2):
            bk = 2 * b + half
            mmg(ps[bk], [(mixT[:, ft, tl(i)], wout[:, ft, half * 512:(half + 1) * 512]) for ft in range(8)], r=["mixT", "wout"], w=[f"ps{bk}"])
            P.dve(lambda e, i=i, b=b, half=half, bk=bk: e.tensor_tensor(out=h1[:, i, half * 512:(half + 1) * 512], in0=ps[bk],
                                                                       in1=xt2[b][:, half * 512:(half + 1) * 512], op=ALU.add),
                  r=[f"ps{bk}", f"xt2{b}"], w=[f"h1_{i}"])
            if half == 0:
                if i >= 1:
                    nt_v.part1(i - 1, h1[:, i - 1, :], [f"h1_{i - 1}"])
                if i >= 2:
                    nt_v.part2(i - 2)
    nt_v.part2(NT - 2)
    nt_v.part1(NT - 1, h1[:, NT - 1, :], [f"h1_{NT - 1}"])
    dump("h1", h1, [128, NT, D], F32)

    A.top = off_OGT
    wd = A.alloc([128, 6, 1024], BF16)
    A.top = m0
    aT = A.alloc([128, 6, T], BF16)
    tg = [A.alloc([128, 512], F32) for _ in range(2)]
    tt = [A.alloc([128, 512], F32) for _ in range(2)]
    assert A.top <= off_after_mix
    A.top = off_e2t

    def final_tile(i):
        b = i % 2
        P.act(lambda e, i=i: e.activation(out=sqf, in_=h1[:, i, :], func=AF.Square, accum_out=ssf[:, i:i + 1]), r=[f"h1_{i}"], w=["sqf", f"ssf{i}"])
        P.act(lambda e, i=i: e.activation(out=lnf[:, i:i + 1], in_=ssf[:, i:i + 1], func=AF.Ln, scale=1.0 / D, bias=epsb[:, 0:1]),
              r=[f"ssf{i}", "epsb"], w=[f"lnf{i}"])
        P.act(lambda e, i=i: e.activation(out=rsf[:, i:i + 1], in_=lnf[:, i:i + 1], func=AF.Exp, scale=-0.5), r=[f"lnf{i}"], w=[f"rsf{i}"])
        P.dve(lambda e, i=i, b=b: e.scalar_tensor_tensor(out=ot[b], in0=h1[:, i, :], scalar=rsf[:, i:i + 1], in1=fnb, op0=ALU.mult, op1=ALU.mult),
              r=[f"h1_{i}", f"rsf{i}", "fnb"], w=[f"ot{b}"])
        P.dma("sp", out[tl(i), :], ot[b], r=[f"ot{b}"], w=[f"out{i}"])

    groups = [(0, 5), (5, 10), (10, 16), (16, 22)]
    for (f0, f1) in groups:
        nf = f1 - f0
        for f in range(f0, f1):
            g = f % 2
            if f >= 2:
                load_fsl(f)
            if f == f0 + 1:
                P.dma("pool", wd[:, 0:nf, :], w_fd[f0 * 128:f1 * 128, :].rearrange("(f p) n -> p f n", p=128), r=[], w=["wd"])
            for s in range(NSB):
                q = s % 2
                bg_, bu_ = (0, 1) if q == 0 else (2, 3)
                alias0 = ["mixT"] if (f == 0 and s == 0) else []
                mmg(ps[bg_], [(fsl[g][:, c, 0:128], u2T[:, c, sb_(s)]) for c in range(8)], r=[f"fslG{g}", f"vT{s}"], w=[f"ps{bg_}"])
                mmg(ps[bu_], [(fsl[g][:, c, 128:256], u2T[:, c, sb_(s)]) for c in range(8)], r=[f"fslU{g}", f"vT{s}"], w=[f"ps{bu_}"])
                P.act(lambda e, q=q, bg_=bg_: e.activation(out=tg[q], in_=ps[bg_], func=AF.Exp, scale=-1.0), r=[f"ps{bg_}"], w=[f"tg{q}"] + alias0)
                P.act(lambda e, q=q: e.activation(out=tg[q], in_=tg[q], func=AF.Ln, bias=one_b[:, 0:1]), r=[f"tg{q}", "one_b"], w=[f"tg{q}"])
                P.act(lambda e, q=q: e.activation(out=tg[q], in_=tg[q], func=AF.Exp, scale=-1.0), r=[f"tg{q}"], w=[f"tg{q}"])
                P.dve(lambda e, q=q, bg_=bg_: e.tensor_tensor(out=tt[q], in0=ps[bg_], in1=tg[q], op=ALU.mult), r=[f"ps{bg_}", f"tg{q}"], w=[f"tt{q}"] + alias0)
                P.dve(lambda e, q=q, bu_=bu_, f=f, f0=f0, s=s: e.tensor_tensor(out=aT[:, f - f0, sb_(s)], in0=ps[bu_], in1=tt[q], op=ALU.mult),
                      r=[f"ps{bu_}", f"tt{q}"], w=[f"aT{s}"])
                if f == 0 and s == 2:
                    nt_v.part2(NT - 1)
        for i in range(NT):
            for half in range(2):
                bk = 4 + (2 * i + half) % 4
                mmg(ps[bk], [(aT[:, fl, tl(i)], wd[:, fl, half * 512:(half + 1) * 512]) for fl in range(nf)], r=[f"aT{i // 4}", "wd"], w=[f"ps{bk}"])
                P.dve(lambda e, i=i, half=half, bk=bk: e.tensor_tensor(out=h1[:, i, half * 512:(half + 1) * 512], in0=ps[bk],
                                                                      in1=h1[:, i, half * 512:(half + 1) * 512], op=ALU.add),
                      r=[f"ps{bk}", f"h1_{i}"], w=[f"h1_{i}"])
            if f1 == NF and i >= 1:
                final_tile(i - 1)
    final_tile(NT - 1)
    return finish(nc, st, P, out, dbg_out, A)


def finish(nc, st, P, out, dbg_out, A):
    P.barrier()
    P.emit(nc, st)
    st.close()
    return nc, dbg_out


def host_consts():
    half = 16
    inv64 = 1.0 / (10000.0 ** (np.arange(half, dtype=np.float64) / half))
    hi = inv64.astype(np.float32)
    lo = (inv64 - hi.astype(np.float64)).astype(np.float32)
    c = {}
    c["c_inv"] = np.ascontiguousarray(np.broadcast_to(np.concatenate([hi, lo])[None, :], (128, 32))).astype(np.float32)
    c["c_ident"] = np.eye(128, dtype=np.float32).astype(ml_dtypes.bfloat16)
    j = np.arange(128)[:, None]
    i = np.arange(128)[None, :]
    c["c_tri"] = (i >= j).astype(np.float32).astype(ml_dtypes.bfloat16)
    return c


def pcol(v, n):
    return np.ascontiguousarray(np.asarray(v, dtype=np.float32).reshape(n, 128).T)


def shared_map(inp):
    f = lambda a: np.ascontiguousarray(np.asarray(a, dtype=np.float32))
    m = dict(
        w_in=f(inp["w_in"][0]), w_uq=f(inp["mla_w_uq"][0]), w_ukv=f(inp["mla_w_ukv"][0]), w_oa=f(inp["mla_w_o"][0]),
        w_g2=f(inp["gla_w_gate2"][0]), w_ob=f(inp["gla_w_o"][0]), w_out=f(inp["w_out"][0]),
        w_fg=f(inp["ffn_w_gate"][0]), w_fu=f(inp["ffn_w_up"][0]), w_fd=f(inp["ffn_w_down"][0]),
        v_lnmix=pcol(inp["ln_mix"][0], 8), v_lnffn=pcol(inp["ln_ffn"][0], 8), v_ncq=pcol(inp["mla_norm_cq"][0], 3),
        v_nckv=pcol(inp["mla_norm_ckv"][0], 2), v_bg=pcol(inp["gla_b_gate"][0], 2),
        v_gn=np.ascontiguousarray(np.broadcast_to(np.asarray(inp["gla_norm"][0], dtype=np.float32)[None, :], (128, 128))),
        v_gnc=pcol(inp["gla_norm"][0], 1),
        v_fn=np.ascontiguousarray(np.broadcast_to(np.asarray(inp["final_norm"], dtype=np.float32)[None, :], (128, D))),
    )
    m.update(host_consts())
    return m


def core_map(inp, b, shared):
    m = dict(shared)
    m["x"] = np.ascontiguousarray(np.asarray(inp["x"][b], dtype=np.float32))
    m["pos"] = np.ascontiguousarray(np.asarray(inp["positions"][b], dtype=np.int32).reshape(NT, 128).T)
    return m


_CACHE = {}


def kernel(**inputs):
    if "nc" not in _CACHE:
        _CACHE["nc"] = build()[0]
    nc = _CACHE["nc"]
    shared = shared_map(inputs)
    B = np.asarray(inputs["x"]).shape[0]
    in_maps = [core_map(inputs, b, shared) for b in range(B)]
    res = run_bass_kernel_spmd(nc, in_maps, core_ids=list(range(B)))
    return np.stack([np.asarray(r["out"], dtype=np.float32) for r in res.results], axis=0)
```
